# Optimizing a Trainium2 kernel written in Bass

```python
import jax, jax.numpy as jnp
from jax import lax
import numpy as np

D_MODEL = 1024
BATCH = 32
SEQ = 256
DEPTH = 4
DEC_BATCH = 2
DEC_SEQ = 1024
PAST_LEN = 256

GRID_W = 64
N_EVEN = (DEPTH + 1) // 2
N_ODD = DEPTH // 2
N_MOD = 9
D_FF = 2816
EPS = 1e-6
A_WIDTH = 512
A_GROUPS = 4
A_GROUP_DIM = A_WIDTH // A_GROUPS
A_CHUNK = 128
SSM_HEADS = 8
SSM_HEAD_DIM = 64
SSM_INNER = SSM_HEADS * SSM_HEAD_DIM
SSM_GROUPS = 2
SSM_STATE = 128
SSM_CONV = 3
SSM_CHUNK = 128
SSM_XBC = SSM_INNER + 2 * SSM_GROUPS * SSM_STATE
EVEN_IN = 2 * A_WIDTH + SSM_INNER + SSM_XBC + 2 * SSM_HEADS
EVEN_MIX = A_WIDTH + SSM_INNER
MLA_HEADS = 8
MLA_NOPE = 64
MLA_ROPE = 32
MLA_V = 64
MLA_Q_RANK = 384
MLA_KV_RANK = 256
ROPE_THETA = 10000.0
Q_BLOCK = 128
CONV_WIDTH = 512
CONV_K = 31
ODD_IN = MLA_Q_RANK + MLA_KV_RANK + MLA_ROPE + 2 * CONV_WIDTH
ODD_MIX = MLA_HEADS * MLA_V + CONV_WIDTH

kernel_name = 'hybrid_gmlp_ssd_mla_conformer_dit_step'


def rmsnorm(x, g):
    xf = x.astype(jnp.float32)
    y = xf * lax.rsqrt(jnp.mean(xf * xf, axis=-1, keepdims=True) + EPS)
    return (y * g.astype(jnp.float32)).astype(x.dtype)


def layernorm(x, g, b):
    xf = x.astype(jnp.float32)
    xc = xf - jnp.mean(xf, axis=-1, keepdims=True)
    var = jnp.mean(xc * xc, axis=-1, keepdims=True)
    return (xc * lax.rsqrt(var + EPS) * g.astype(jnp.float32) + b.astype(jnp.float32)).astype(x.dtype)


def modulate(x, shift, scale):
    return x * (1 + scale) + shift


def swiglu(x, w_gu, w_d):
    g, u = jnp.split(x @ w_gu, 2, axis=-1)
    return (jax.nn.silu(g) * u) @ w_d


def ffn_sublayer(x, g, shift, scale, gate, w_gu, w_d):
    return x + 0.5 * gate * swiglu(modulate(rmsnorm(x, g), shift, scale), w_gu, w_d)


def depthwise_conv(x, w, b):
    k = w.shape[0]
    y = lax.conv_general_dilated(x, w[:, None, :].astype(x.dtype), window_strides=(1,),
                                 padding=[(k // 2, k // 2)], dimension_numbers=('NWC', 'WIO', 'NWC'),
                                 feature_group_count=x.shape[-1])
    return y + b


def chunk_gmlp(uv, w_s, b_s, g_v):
    bsz, length, _ = uv.shape
    u, v = jnp.split(jax.nn.gelu(uv), 2, axis=-1)
    v = rmsnorm(v, g_v).reshape(bsz, length // A_CHUNK, A_CHUNK, A_GROUPS, A_GROUP_DIM)
    s = jnp.einsum('gij,bcjgd->bcigd', w_s, v) + b_s.T[None, None, :, :, None]
    return u * s.reshape(bsz, length, A_WIDTH)


def ssd_scan(x, dt, a, bm, cm, h0):
    bsz, length, n_h, p = x.shape
    nc = length // SSM_CHUNK
    rep = n_h // SSM_GROUPS
    bh = jnp.repeat(bm, rep, axis=2).reshape(bsz, nc, SSM_CHUNK, n_h, SSM_STATE)
    ch = jnp.repeat(cm, rep, axis=2).reshape(bsz, nc, SSM_CHUNK, n_h, SSM_STATE)
    xdt = (x * dt[..., None]).reshape(bsz, nc, SSM_CHUNK, n_h, p)
    cum = jnp.cumsum((dt.astype(jnp.float32) * a.astype(jnp.float32)).reshape(bsz, nc, SSM_CHUNK, n_h), axis=2)
    seg = cum[:, :, :, None, :] - cum[:, :, None, :, :]
    lower = jnp.tril(jnp.ones((SSM_CHUNK, SSM_CHUNK), dtype=bool))[None, None, :, :, None]
    lmat = jnp.exp(jnp.where(lower, seg, -jnp.inf)).astype(x.dtype)
    scores = jnp.einsum('bcihn,bcjhn->bcijh', ch, bh) * lmat
    y_diag = jnp.einsum('bcijh,bcjhp->bcihp', scores, xdt)
    decay_end = jnp.exp(cum[:, :, -1:, :] - cum).astype(x.dtype)
    chunk_states = jnp.einsum('bcjhn,bcjh,bcjhp->bchpn', bh, decay_end, xdt)
    chunk_decay = jnp.exp(cum[:, :, -1, :]).astype(x.dtype)

    def step(h, inp):
        dec, st = inp
        return h * dec[:, :, None, None] + st, h

    h_last, h_in = lax.scan(step, h0, (jnp.moveaxis(chunk_decay, 1, 0), jnp.moveaxis(chunk_states, 1, 0)))
    h_in = jnp.moveaxis(h_in, 0, 1)
    y_off = jnp.einsum('bcihn,bchpn->bcihp', ch, h_in) * jnp.exp(cum).astype(x.dtype)[..., None]
    return (y_diag + y_off).reshape(bsz, length, n_h, p), h_last


def ssd_gmlp_mixer(h, w_in, w_out, w_s, b_s, g_v, w_conv, b_conv, dt_bias, a_log, d_skip, g_out, h0_f, h0_b):
    bsz, length, _ = h.shape
    proj = h @ w_in
    o1 = 2 * A_WIDTH
    o2 = o1 + SSM_INNER
    o3 = o2 + SSM_XBC
    y_a = chunk_gmlp(proj[..., :o1], w_s, b_s, g_v)
    z = proj[..., o1:o2]
    xbc = jax.nn.silu(depthwise_conv(proj[..., o2:o3], w_conv, b_conv))
    dt_raw = proj[..., o3:]
    xs = xbc[..., :SSM_INNER].reshape(bsz, length, SSM_HEADS, SSM_HEAD_DIM)
    bm = xbc[..., SSM_INNER:SSM_INNER + SSM_GROUPS * SSM_STATE].reshape(bsz, length, SSM_GROUPS, SSM_STATE)
    cm = xbc[..., SSM_INNER + SSM_GROUPS * SSM_STATE:].reshape(bsz, length, SSM_GROUPS, SSM_STATE)
    dt_f = jax.nn.softplus(dt_raw[..., :SSM_HEADS] + dt_bias[0])
    dt_b = jax.nn.softplus(dt_raw[..., SSM_HEADS:] + dt_bias[1])
    y_f, h_f = ssd_scan(xs, dt_f, -jnp.exp(a_log[0]), bm, cm, h0_f)
    y_b, h_b = ssd_scan(jnp.flip(xs, 1), jnp.flip(dt_b, 1), -jnp.exp(a_log[1]),
                        jnp.flip(bm, 1), jnp.flip(cm, 1), h0_b)
    y = y_f + jnp.flip(y_b, 1) + xs * (d_skip[0] + d_skip[1])[:, None]
    y = rmsnorm(y.reshape(bsz, length, SSM_INNER) * jax.nn.silu(z), g_out)
    return jnp.concatenate([y_a, y], axis=-1) @ w_out, h_f, h_b


def rope_2d(length):
    n_rows = length // GRID_W
    row = jnp.repeat(jnp.arange(n_rows, dtype=jnp.float32), GRID_W)
    col = jnp.tile(jnp.arange(GRID_W, dtype=jnp.float32), n_rows)
    n_freq = MLA_ROPE // 4
    freqs = ROPE_THETA ** (-jnp.arange(n_freq, dtype=jnp.float32) / n_freq)
    ang = jnp.stack([row[:, None] * freqs, col[:, None] * freqs], axis=1)
    return jnp.cos(ang), jnp.sin(ang)


def apply_rope(x, cos, sin):
    xs = x.reshape(x.shape[:-1] + (2, 2, MLA_ROPE // 4))
    x1, x2 = xs[..., 0, :], xs[..., 1, :]
    cos = cos.astype(x.dtype)
    sin = sin.astype(x.dtype)
    out = jnp.stack([x1 * cos - x2 * sin, x1 * sin + x2 * cos], axis=-2)
    return out.reshape(x.shape)


def mla_attend(q_nope, q_rope, k_nope, k_rope, v):
    bsz, lq = q_nope.shape[:2]
    nb = lq // Q_BLOCK
    scale = (MLA_NOPE + MLA_ROPE) ** -0.5
    qn_b = jnp.moveaxis(q_nope.reshape(bsz, nb, Q_BLOCK, MLA_HEADS, MLA_NOPE), 1, 0)
    qr_b = jnp.moveaxis(q_rope.reshape(bsz, nb, Q_BLOCK, MLA_HEADS, MLA_ROPE), 1, 0)

    def block(args):
        qn, qr = args
        s = jnp.einsum('bqhd,bkhd->bhqk', qn, k_nope) + jnp.einsum('bqhd,bkd->bhqk', qr, k_rope)
        p = jax.nn.softmax(s.astype(jnp.float32) * scale, axis=-1).astype(v.dtype)
        return jnp.einsum('bhqk,bkhd->bqhd', p, v)

    o = lax.map(block, (qn_b, qr_b))
    return jnp.moveaxis(o, 0, 1).reshape(bsz, lq, MLA_HEADS * MLA_V)


def up_kv(ckv, w_ukv):
    kv = (ckv @ w_ukv).reshape(ckv.shape[:-1] + (MLA_HEADS, MLA_NOPE + MLA_V))
    return kv[..., :MLA_NOPE], kv[..., MLA_NOPE:]


def mla_conv_mixer(h, w_in, w_out, g_cq, w_uq, g_ckv, w_ukv, w_dw, b_dw, g_ln, b_ln, ctx):
    bsz, length, _ = h.shape
    proj = h @ w_in
    o1 = MLA_Q_RANK
    o2 = o1 + MLA_KV_RANK
    o3 = o2 + MLA_ROPE
    q = (rmsnorm(proj[..., :o1], g_cq) @ w_uq).reshape(bsz, length, MLA_HEADS, MLA_NOPE + MLA_ROPE)
    q_nope, q_rope = q[..., :MLA_NOPE], q[..., MLA_NOPE:]
    ckv = rmsnorm(proj[..., o1:o2], g_ckv)
    k_rope = proj[..., o2:o3]
    k_nope, v = up_kv(ckv, w_ukv)
    if ctx is None:
        attn = mla_attend(q_nope, q_rope, k_nope, k_rope, v)
    else:
        cache_ckv, cache_kr, cos, sin = ctx
        kc_nope, vc = up_kv(cache_ckv, w_ukv)
        attn = mla_attend(q_nope, apply_rope(q_rope, cos[:, None], sin[:, None]),
                          jnp.concatenate([kc_nope, k_nope], axis=1),
                          jnp.concatenate([cache_kr, apply_rope(k_rope, cos, sin)], axis=1),
                          jnp.concatenate([vc, v], axis=1))
    a_half, g_half = jnp.split(proj[..., o3:], 2, axis=-1)
    d = depthwise_conv(a_half * jax.nn.sigmoid(g_half), w_dw, b_dw)
    d = jax.nn.silu(layernorm(d, g_ln, b_ln))
    return jnp.concatenate([attn, d], axis=-1) @ w_out, ckv, k_rope


def setup_inputs(seed: int = 0) -> dict:
    key = jax.random.key(seed)
    ks = jax.random.split(key, 34)
    f32 = jnp.float32

    def nrm(k, shape, scale):
        return jax.random.normal(k, shape, f32) * scale

    dt0 = jnp.exp(jax.random.uniform(ks[19], (N_EVEN, 2, SSM_HEADS), f32,
                                     np.log(1e-3), np.log(1e-1)))
    return {
        'x_prompt': nrm(ks[0], (BATCH, SEQ, D_MODEL), 1.0),
        'x_sample': nrm(ks[1], (DEC_BATCH, DEC_SEQ, D_MODEL), 1.0),
        'state_ssd': nrm(ks[2], (DEC_BATCH, N_EVEN, 2, SSM_HEADS, SSM_HEAD_DIM, SSM_STATE), 0.5),
        'cache_mla_ckv': nrm(ks[3], (DEC_BATCH, N_ODD, PAST_LEN, MLA_KV_RANK), 1.0),
        'cache_mla_krope': nrm(ks[4], (DEC_BATCH, N_ODD, PAST_LEN, MLA_ROPE), 1.0),
        'c': nrm(ks[5], (DEC_BATCH, D_MODEL), 1.0),
        'c_ctx': nrm(ks[6], (D_MODEL,), 1.0),
        'w_mod': nrm(ks[7], (DEPTH, D_MODEL, N_MOD * D_MODEL), 0.5 * D_MODEL ** -0.5),
        'b_mod': nrm(ks[8], (DEPTH, N_MOD * D_MODEL), 0.02),
        'g_norm': 1.0 + nrm(ks[9], (DEPTH, 3, D_MODEL), 0.1),
        'w_ff_gu': nrm(ks[10], (DEPTH, 2, D_MODEL, 2 * D_FF), D_MODEL ** -0.5),
        'w_ff_down': nrm(ks[11], (DEPTH, 2, D_FF, D_MODEL), D_FF ** -0.5),
        'w_in_even': nrm(ks[12], (N_EVEN, D_MODEL, EVEN_IN), D_MODEL ** -0.5),
        'w_out_even': nrm(ks[13], (N_EVEN, EVEN_MIX, D_MODEL), EVEN_MIX ** -0.5),
        'w_spatial': nrm(ks[14], (N_EVEN, A_GROUPS, A_CHUNK, A_CHUNK), A_CHUNK ** -0.5),
        'b_spatial': 1.0 + nrm(ks[15], (N_EVEN, A_GROUPS, A_CHUNK), 0.1),
        'g_gmlp_v': 1.0 + nrm(ks[16], (N_EVEN, A_WIDTH), 0.1),
        'w_conv_ssm': nrm(ks[17], (N_EVEN, SSM_CONV, SSM_XBC), SSM_CONV ** -0.5),
        'b_conv_ssm': nrm(ks[18], (N_EVEN, SSM_XBC), 0.02),
        'dt_bias': dt0 + jnp.log(-jnp.expm1(-dt0)),
        'a_log': jnp.log(jax.random.uniform(ks[20], (N_EVEN, 2, SSM_HEADS), f32, 1.0, 16.0)),
        'd_skip': 1.0 + nrm(ks[21], (N_EVEN, 2, SSM_HEADS), 0.1),
        'g_ssm_out': 1.0 + nrm(ks[22], (N_EVEN, SSM_INNER), 0.1),
        'w_in_odd': nrm(ks[23], (N_ODD, D_MODEL, ODD_IN), D_MODEL ** -0.5),
        'w_out_odd': nrm(ks[24], (N_ODD, ODD_MIX, D_MODEL), ODD_MIX ** -0.5),
        'g_cq': 1.0 + nrm(ks[25], (N_ODD, MLA_Q_RANK), 0.1),
        'w_uq': nrm(ks[26], (N_ODD, MLA_Q_RANK, MLA_HEADS * (MLA_NOPE + MLA_ROPE)), MLA_Q_RANK ** -0.5),
        'g_ckv': 1.0 + nrm(ks[27], (N_ODD, MLA_KV_RANK), 0.1),
        'w_ukv': nrm(ks[28], (N_ODD, MLA_KV_RANK, MLA_HEADS * (MLA_NOPE + MLA_V)), MLA_KV_RANK ** -0.5),
        'w_dwconv': nrm(ks[29], (N_ODD, CONV_K, CONV_WIDTH), CONV_K ** -0.5),
        'b_dwconv': nrm(ks[30], (N_ODD, CONV_WIDTH), 0.02),
        'g_conv_ln': 1.0 + nrm(ks[31], (N_ODD, CONV_WIDTH), 0.1),
        'b_conv_ln': nrm(ks[32], (N_ODD, CONV_WIDTH), 0.02),
        'g_final': 1.0 + nrm(ks[33], (D_MODEL,), 0.1),
    }


def reference(x_prompt, x_sample, state_ssd, cache_mla_ckv, cache_mla_krope, c, c_ctx,
              w_mod, b_mod, g_norm, w_ff_gu, w_ff_down,
              w_in_even, w_out_even, w_spatial, b_spatial, g_gmlp_v, w_conv_ssm, b_conv_ssm,
              dt_bias, a_log, d_skip, g_ssm_out,
              w_in_odd, w_out_odd, g_cq, w_uq, g_ckv, w_ukv, w_dwconv, b_dwconv, g_conv_ln, b_conv_ln,
              g_final):
    bp = x_prompt.shape[0]
    xc, xs = x_prompt, x_sample
    rope_cos, rope_sin = rope_2d(x_sample.shape[1])
    silu_ctx = jax.nn.silu(c_ctx)
    silu_c = jax.nn.silu(c)
    new_ssd, new_ckv, new_kr = [], [], []
    for l in range(DEPTH):
        mc = jnp.split(silu_ctx @ w_mod[l] + b_mod[l], N_MOD, axis=-1)
        ms = jnp.split((silu_c @ w_mod[l] + b_mod[l])[:, None, :], N_MOD, axis=-1)
        xc = ffn_sublayer(xc, g_norm[l, 0], mc[0], mc[1], mc[2], w_ff_gu[l, 0], w_ff_down[l, 0])
        xs = ffn_sublayer(xs, g_norm[l, 0], ms[0], ms[1], ms[2], w_ff_gu[l, 0], w_ff_down[l, 0])
        hc = modulate(rmsnorm(xc, g_norm[l, 1]), mc[3], mc[4])
        hs = modulate(rmsnorm(xs, g_norm[l, 1]), ms[3], ms[4])
        i = l // 2
        if l % 2 == 0:
            ep = (w_in_even[i], w_out_even[i], w_spatial[i], b_spatial[i], g_gmlp_v[i],
                  w_conv_ssm[i], b_conv_ssm[i], dt_bias[i], a_log[i], d_skip[i], g_ssm_out[i])
            h0 = jnp.zeros((bp, SSM_HEADS, SSM_HEAD_DIM, SSM_STATE), hc.dtype)
            oc, hf, hb = ssd_gmlp_mixer(hc, *ep, h0, h0)
            os_, _, _ = ssd_gmlp_mixer(hs, *ep, state_ssd[:, i, 0].astype(hs.dtype),
                                       state_ssd[:, i, 1].astype(hs.dtype))
            new_ssd.append(jnp.stack([hf, hb], axis=1))
        else:
            op = (w_in_odd[i], w_out_odd[i], g_cq[i], w_uq[i], g_ckv[i], w_ukv[i],
                  w_dwconv[i], b_dwconv[i], g_conv_ln[i], b_conv_ln[i])
            oc, ckv, kr = mla_conv_mixer(hc, *op, None)
            os_, _, _ = mla_conv_mixer(hs, *op, (cache_mla_ckv[:, i].astype(hs.dtype),
                                                 cache_mla_krope[:, i].astype(hs.dtype),
                                                 rope_cos, rope_sin))
            new_ckv.append(ckv)
            new_kr.append(kr)
        xc = xc + mc[5] * oc
        xs = xs + ms[5] * os_
        xc = ffn_sublayer(xc, g_norm[l, 2], mc[6], mc[7], mc[8], w_ff_gu[l, 1], w_ff_down[l, 1])
        xs = ffn_sublayer(xs, g_norm[l, 2], ms[6], ms[7], ms[8], w_ff_gu[l, 1], w_ff_down[l, 1])
    y_prompt = rmsnorm(xc, g_final)
    y_sample = rmsnorm(xs, g_final)
    new_state_ssd = jnp.stack(new_ssd, axis=1)
    new_cache_mla_ckv = jnp.stack(new_ckv, axis=1)
    new_cache_mla_krope = jnp.stack(new_kr, axis=1)
    return (y_prompt, y_sample, new_state_ssd, new_cache_mla_ckv, new_cache_mla_krope)
```

```python
from contextlib import ExitStack
import numpy as np
import concourse.bass as bass
import concourse.mybir as mybir
from concourse.bass_utils import run_bass_kernel_spmd

F32 = mybir.dt.float32
F32R = mybir.dt.float32r
BF16 = mybir.dt.bfloat16
AF = mybir.ActivationFunctionType
ALU = mybir.AluOpType

D = 1024
DEPTH = 4
NT = 1280
TT = [(0, 512), (512, 512), (1024, 256)]
SLOT_OF_TT = [0, 0, 1]
SLOTS = [(0, 1024), (1024, 256)]
DFF = 2816
NFC = 22
EPS = 1e-6
NCORES = 8
NWSLOT = 3
WSLOT = 4096


def _layout(KC, ns):
    lay = []
    t = 0
    off = 0
    for n in ns:
        sz = KC * n
        if off + sz > WSLOT:
            t += 1
            off = 0
        lay.append((t, off, KC, n))
        off += sz
    return lay, t + 1


def _pack(w, blocks):
    KC = w.shape[0] // 128
    lay, nt = _layout(KC, [len(b) for b in blocks])
    out = np.zeros((nt, 128, WSLOT), np.float32)
    for (t, off, _, n), cols in zip(lay, blocks):
        blk = w[:, cols].reshape(KC, 128, n).transpose(1, 0, 2).reshape(128, KC * n)
        out[t, :, off:off + KC * n] = blk
    return out


def _ar(a, b):
    return list(range(a, b))


def _rope_partner(cols):
    c = np.asarray(cols).reshape(2, 2, 8)
    return list(c[:, ::-1, :].reshape(-1))


def _odd_in_blocks():
    kr = _ar(640, 672)
    blocks = [_ar(0, 128), _ar(128, 256), _ar(256, 384), kr + kr, _rope_partner(kr) * 2,
              _ar(384, 512), _ar(512, 640)]
    for c in range(4):
        blocks.append(_ar(672 + c * 128, 672 + (c + 1) * 128))
        blocks.append(_ar(1184 + c * 128, 1184 + (c + 1) * 128))
    return blocks


def _uq_blocks():
    blocks = []
    for m in range(4):
        blocks.append(_ar(2 * m * 96, 2 * m * 96 + 64) + _ar((2 * m + 1) * 96, (2 * m + 1) * 96 + 64))
    for m in range(4):
        blocks.append(_ar(2 * m * 96 + 64, 2 * m * 96 + 96) + _ar((2 * m + 1) * 96 + 64, (2 * m + 1) * 96 + 96))
    for m in range(4):
        blocks.append(_rope_partner(_ar(2 * m * 96 + 64, 2 * m * 96 + 96))
                      + _rope_partner(_ar((2 * m + 1) * 96 + 64, (2 * m + 1) * 96 + 96)))
    return blocks


def _ukv_blocks():
    blocks = []
    for m in range(4):
        blocks.append(_ar(2 * m * 128, 2 * m * 128 + 64) + _ar((2 * m + 1) * 128, (2 * m + 1) * 128 + 64))
    v = []
    for hh in range(8):
        v += _ar(hh * 128 + 64, hh * 128 + 128)
    blocks.append(v)
    return blocks


def _out_blocks():
    return [_ar(j * 128, (j + 1) * 128) for j in range(8)]


LAY_OIN, NT_OIN = _layout(8, [len(b) for b in _odd_in_blocks()])
LAY_UQ, NT_UQ = _layout(3, [len(b) for b in _uq_blocks()])
LAY_UKV, NT_UKV = _layout(2, [len(b) for b in _ukv_blocks()])
LAY_OUT, NT_OUT = _layout(8, [128] * 8)
NOV = 3 + 2 + 124 + 4 + 4 + 4


def _even_in_blocks():
    blocks = [_ar(c * 128, (c + 1) * 128) for c in range(4)]
    blocks += [_ar(1024 + c * 128, 1024 + (c + 1) * 128) for c in range(4)]
    blocks += [_ar(1536 + c * 128, 1536 + (c + 1) * 128) for c in range(8)]
    blocks.append(_ar(512, 1024))
    blocks.append(_ar(2560, 2576))
    return blocks


LAY_EIN, NT_EIN = _layout(8, [len(b) for b in _even_in_blocks()])
NEV = 24 + 8 + 4 + 4 + 4
NEB = 512 + 512 + 16 + 16
NEG = -30000.0

class Instr:
    __slots__ = ("fn", "waits", "signal", "dma")

    def __init__(self, fn, dma=None):
        self.fn = fn
        self.waits = []
        self.signal = False
        self.dma = dma


class Sched:
    ENGS = ("pe", "act", "dve", "pool", "sp")
    STRICT = 2
    CAP = 3000

    def __init__(self):
        self.q = {e: [] for e in self.ENGS}
        self.clock = {e: {} for e in self.ENGS}
        self.evclock = {}
        self.tok = {}
        self.dma_cnt = {}
        self.total_keys = set()

    def op(self, eng, fn, reads=(), writes=(), dma=None):
        q = self.q[eng]
        idx = len(q)
        ins = Instr(fn, dma)
        deps = set()
        raw = set()
        for t in reads:
            w = self.tok.get(t)
            if w is not None and w[0] is not None:
                deps.add(w[0])
                raw.add(w[0])
        for t in writes:
            w = self.tok.get(t)
            if w is not None:
                if w[0] is not None:
                    deps.add(w[0])
                deps.update(w[1])
        ck = self.clock[eng]
        for ev in sorted(deps, key=lambda e: (e[0], str(e[1]), -e[2])):
            kind, a, b = ev
            if kind == "e" and a == eng and Sched.STRICT != 1:
                if eng == "pe":
                    continue
                if Sched.STRICT == 0:
                    if ev not in raw:
                        continue
                    if idx - b > 3:
                        continue
            key = (kind, a)
            if ck.get(key, -1) >= b:
                continue
            ins.waits.append(ev)
            for k, v in self.evclock[ev].items():
                if ck.get(k, -1) < v:
                    ck[k] = v
            if kind == "e":
                self.q[a][b].signal = True
        if dma is not None:
            n = self.dma_cnt.get(dma, 0) + 1
            self.dma_cnt[dma] = n
            ev = ("d", dma, n)
        else:
            ev = ("e", eng, idx)
        ck2 = dict(ck)
        ck2[(ev[0], ev[1])] = ev[2]
        if dma is None:
            ck2[("e", eng)] = idx
        self.evclock[ev] = ck2
        q.append(ins)
        for t in reads:
            w = self.tok.setdefault(t, [None, []])
            w[1].append(ev)
        for t in writes:
            self.tok[t] = [ev, []]
        return ev

    def wait_all(self, eng, evs):
        ins = Instr(None)
        for ev in evs:
            ins.waits.append(ev)
            if ev[0] == "e":
                self.q[ev[1]][ev[2]].signal = True
        self.q[eng].append(ins)

    def emit(self, nc, stack):
        CAP = Sched.CAP
        sig = {}
        nsig = {}
        for e in self.ENGS:
            c = 0
            arr = []
            for ins in self.q[e]:
                if ins.signal and ins.dma is None:
                    c += 1
                arr.append(c)
            sig[e] = arr
            nsig[e] = c
        esem = {e: [stack.enter_context(nc.semaphore("s_%s%d" % (e, k))) for k in range(nsig[e] // CAP + 1)]
                for e in self.ENGS}
        dsem = {k: stack.enter_context(nc.semaphore("d_%s" % (k,))) for k in self.dma_cnt}
        for k, v in self.dma_cnt.items():
            assert 16 * v < 4000, (k, v)
        block = stack.enter_context(nc.Block())
        q = self.q
        total = self.total_keys
        dma_cnt = self.dma_cnt

        def run(ename, eng):
            for i, ins in enumerate(q[ename]):
                for (kind, a, b) in ins.waits:
                    if kind == "e":
                        c = sig[a][b]
                        eng.wait_ge(esem[a][(c - 1) // CAP], (c - 1) % CAP + 1)
                    else:
                        n = dma_cnt[a] if a in total else b
                        eng.wait_ge(dsem[a], 16 * n)
                if ins.fn is None:
                    continue
                r = ins.fn(eng)
                if ins.dma is not None:
                    r.then_inc(dsem[ins.dma], 16)
                elif ins.signal:
                    c = sig[ename][i]
                    r.then_inc(esem[ename][(c - 1) // CAP], 1)

        @block.tensor
        def _(e):
            run("pe", e)

        @block.scalar
        def _(e):
            run("act", e)

        @block.vector
        def _(e):
            run("dve", e)

        @block.gpsimd
        def _(e):
            run("pool", e)

        @block.sync
        def _(e):
            run("sp", e)


def build_program(cfg=None):
    cfg = cfg or {}
    nlayers = cfg.get("nlayers", DEPTH)
    do_mixer = cfg.get("mixer", True)
    nc = bass.Bass("TRN2", target_bir_lowering=False)
    S = Sched()
    stack = ExitStack()
    dram = {}

    def din(name, shape, dt=F32):
        dram[name] = nc.dram_tensor(name, list(shape), dt, kind="ExternalInput").ap()
        return dram[name]

    def dout(name, shape, dt=F32):
        dram[name] = nc.dram_tensor(name, list(shape), dt, kind="ExternalOutput").ap()
        return dram[name]

    def sb(name, shape, dt=F32):
        return stack.enter_context(nc.sbuf_tensor(name, list(shape), dt))

    xT_d = din("xT", [8, 128, NT])
    cvec_d = din("cvec", [128, 16])
    wmod_d = din("wmod", [DEPTH, 36, 128, 2048])
    bmod_d = din("bmod", [128, DEPTH * 72])
    gnorm_d = din("gnorm", [128, DEPTH * 3 * 8])
    gfin_d = din("gfin", [128, 8])
    wgu_d = din("wgu", [DEPTH, 2, 11, 128, 4096])
    wdn_d = din("wdn", [DEPTH, 2, 8, 128, 2816])
    yT_d = dout("yT", [8, 128, NT])
    flags_d = din("flags", [128, 4])
    woin_d = din("woin", [2, NT_OIN, 128, WSLOT])
    wuq_d = din("wuq", [2, NT_UQ, 128, WSLOT])
    wukv_d = din("wukv", [2, NT_UKV, 128, WSLOT])
    woout_d = din("woout", [2, NT_OUT, 128, WSLOT])
    oddv_d = din("oddv", [128, 2 * NOV])
    rope_d = din("rope", [2, 64, 1024])
    cckv_d = din("cckv", [2, 128, 2, 256])
    ckr_d = din("ckr", [2, 64, 256])
    ckvT_d = dout("ckvT", [2, 2, 128, NT])
    wein_d = din("wein", [2, NT_EIN, 128, WSLOT])
    weout_d = din("weout", [2, NT_OUT, 128, WSLOT])
    evv_d = din("evv", [128, 2 * NEV])
    evb_d = din("evb", [2, 128, NEB])
    wsT_d = din("wsT", [2, 128, 512])
    cst_d = din("cst", [128, 6 * 128])
    h0T_d = din("h0T", [2, 128, 2, 512])
    ssdT_d = dout("ssdT", [2, 5, 2, 128, 512])
    krT_d = dout("krT", [2, 32, NT])

    x = sb("x", [128, 8, NT], F32)
    h = sb("h", [128, 8, NT], BF16)
    big = sb("big", [128, 14336], F32)
    act = big[:, 0:14080].bitcast(BF16).rearrange("p (j t) -> p j t", j=NFC)
    wbuf = [sb("wbuf%d" % i, [128, WSLOT], BF16) for i in range(NWSLOT)]
    rstd = sb("rstd", [128, NT], F32)
    sq = [sb("sq%d" % i, [128, NT], BF16) for i in range(2)]
    tmp = [sb("tmp%d" % i, [128, 1440], F32) for i in range(2)]
    sg = [sb("sg%d" % i, [128, NT], BF16) for i in range(2)]
    ones_bf = sb("ones_bf", [128, 128], BF16)
    cvec = sb("cvec_sb", [128, 16], F32)
    csil = sb("csil", [128, 16], BF16)
    bmod = sb("bmod_sb", [128, DEPTH * 72], F32)
    gnorm = sb("gnorm_sb", [128, DEPTH * 24], F32)
    gfin = sb("gfin_sb", [128, 8], F32)
    modv = [sb("modv%d" % i, [128, 144], F32) for i in range(2)]
    amul = [sb("amul%d" % i, [128, 48], F32) for i in range(2)]
    gate = [sb("gate%d" % i, [128, 48], F32) for i in range(2)]
    eps_t = sb("eps_t", [128, 1], F32)
    flags = sb("flags_sb", [128, 4], F32)
    oddv = sb("oddv_sb", [128, 2 * NOV], F32)
    rstd2 = sb("rstd2", [128, NT], F32)
    ext = sb("ext", [128, 6400], F32)
    evv = sb("evv_sb", [128, 2 * NEV], F32)
    cst = sb("cst_sb", [128, 6 * 128], F32)
    identb = sb("identb", [128, 128], BF16)
    one_t = sb("one_t", [128, 1], F32)
    negub = sb("negub", [128, 256], BF16)
    aneg = sb("aneg", [128, 16], F32)
    s160 = [sb("s160_%d" % i, [128, 160], F32) for i in range(3)]
    sml = sb("sml", [128, 4], F32)
    fence_t = sb("fence_t", [128, 2], F32)
    psb = [stack.enter_context(nc.psum_tensor("ps%d" % i, [128, 512], F32)) for i in range(8)]

    ps_rr = [0]

    ps_held = set()

    def ps_alloc(hold=False):
        while True:
            b = ps_rr[0] % 8
            ps_rr[0] += 1
            if b not in ps_held:
                break
        if hold:
            ps_held.add(b)
        return b

    def ps_release(*bs):
        for b in bs:
            ps_held.discard(b)

    w_rr = [0]

    def load_w(src_ap, n):
        slot = w_rr[0] % NWSLOT
        w_rr[0] += 1
        S.op("pool", lambda e, slot=slot, src_ap=src_ap, n=n: e.dma_start(out=wbuf[slot][:, 0:n], in_=src_ap),
             writes=[("w", slot)], dma="w%d" % slot)
        return slot

    S.op("sp", lambda e: e.dma_start(out=cvec[:], in_=cvec_d), writes=["cvec"], dma="in")
    S.op("sp", lambda e: e.dma_start(out=bmod[:], in_=bmod_d), writes=["bmod"], dma="in")
    S.op("sp", lambda e: e.dma_start(out=gnorm[:], in_=gnorm_d), writes=["gnorm"], dma="in")
    S.op("sp", lambda e: e.dma_start(out=gfin[:], in_=gfin_d), writes=["gfin"], dma="in")
    for kc in range(8):
        S.op("sp", lambda e, kc=kc: e.dma_start(out=x[:, kc, :], in_=xT_d[kc]), writes=[("x", kc)], dma="in")
    S.op("sp", lambda e: e.dma_start(out=flags[:], in_=flags_d), writes=["flags"], dma="in")
    S.op("sp", lambda e: e.dma_start(out=oddv[:], in_=oddv_d), writes=["oddv"], dma="in")
    S.op("sp", lambda e: e.dma_start(out=evv[:], in_=evv_d), writes=["evv"], dma="in")
    S.op("sp", lambda e: e.dma_start(out=cst[:], in_=cst_d), writes=["cst"], dma="in")
    S.total_keys.add("in")
    S.op("dve", lambda e: e.memset(ones_bf[:], 1.0), writes=["ones"])
    S.op("dve", lambda e: e.memset(eps_t[:], EPS), writes=["eps"])
    S.op("dve", lambda e: e.memset(one_t[:], 1.0), writes=["one"])
    S.op("act", lambda e: e.activation(out=identb[:], in_=cst[:, 4 * 128:5 * 128], func=AF.Identity),
         reads=["cst"], writes=["identb"])
    S.op("act", lambda e: e.activation(out=negub[:], in_=cst[:, 2 * 128:4 * 128], func=AF.Identity),
         reads=["cst"], writes=["negub"])
    S.op("act", lambda e: e.activation(out=csil[:], in_=cvec[:], func=AF.Silu), reads=["cvec"], writes=["csil"])

    wm_rr = [0]
    bg = [None]

    bg_subs = [0]
    bg_limit = [0]

    def bg_start(gen, limit):
        bg[0] = gen
        bg_subs[0] = 0
        bg_limit[0] = limit

    def bg_step(n=1):
        for _ in range(n):
            if bg[0] is None or bg_subs[0] >= bg_limit[0]:
                return
            try:
                if next(bg[0]) == "sub":
                    bg_subs[0] += 1
                    if bg_subs[0] >= 3:
                        bg[0] = None
            except StopIteration:
                bg[0] = None

    def bg_flush():
        bg_limit[0] = 3
        while bg[0] is not None:
            bg_step(1)

    bg_done = set()

    def bg_ensure(l, sub):
        bg_limit[0] = max(bg_limit[0], sub + 1)
        while (l, sub) not in bg_done and bg[0] is not None:
            bg_step(1)

    def mod_layer_gen(l):
        mv = modv[l % 2]
        am = amul[l % 2]
        gt = gate[l % 2]
        mv3 = mv[:, :].rearrange("p (j s) -> p j s", s=2)
        NMS = 6
        for sub in range(3):
            pb = ps_alloc(hold=True)
            for t in range(12 * sub, 12 * (sub + 1)):
                slot = wm_rr[0] % NMS
                wm_rr[0] += 1
                wsl = ext[:, slot * 1024:(slot + 1) * 1024].bitcast(BF16)
                S.op("pool", lambda e, wsl=wsl, t=t: e.dma_start(out=wsl, in_=wmod_d[l, t]),
                     reads=["bigown"], writes=[("wm", slot)], dma="wm%d_%d" % (slot, l))
                for fcl in range(2):
                    j = 2 * t + fcl
                    for kc in range(8):
                        o = (fcl * 8 + kc) * 128
                        S.op("pe", lambda e, wsl=wsl, j=j, kc=kc, o=o, pb=pb: e.matmul(
                            psb[pb][:, 2 * j:2 * j + 2], lhsT=wsl[:, o:o + 128],
                            rhs=csil[:, 2 * kc:2 * kc + 2], start=(kc == 0), stop=(kc == 7)),
                            reads=[("wm", slot), "csil", "bigown"], writes=[("ps", pb)])
                yield
            j0, j1 = 24 * sub, 24 * (sub + 1)
            bm = bmod[:, l * 72 + j0:l * 72 + j1]
            for s in range(2):
                S.op("dve", lambda e, s=s, pb=pb, mv=mv, bm=bm, j0=j0, j1=j1: e.tensor_tensor(
                    out=mv[:, 2 * j0:2 * j1].rearrange("p (j s) -> p s j", s=2)[:, s, :],
                    in0=psb[pb][:, 2 * j0:2 * j1].rearrange("p (j s) -> p s j", s=2)[:, s, :],
                    in1=bm, op=ALU.add),
                    reads=[("ps", pb), "bmod"], writes=[("modv", l % 2, sub)])
            g_ap = gnorm[:, (l * 3 + sub) * 8:(l * 3 + sub + 1) * 8]
            for s in range(2):
                a_out = am[:, sub * 16:(sub + 1) * 16].rearrange("p (k s) -> p k s", s=2)[:, :, s]
                g_out = gt[:, sub * 16:(sub + 1) * 16].rearrange("p (k s) -> p k s", s=2)[:, :, s]
                sc_in = mv3[:, (3 * sub + 1) * 8:(3 * sub + 2) * 8, s]
                gt_in = mv3[:, (3 * sub + 2) * 8:(3 * sub + 3) * 8, s]
                S.op("dve", lambda e, a_out=a_out, sc_in=sc_in, g_ap=g_ap: e.scalar_tensor_tensor(
                    out=a_out, in0=sc_in, scalar=1.0, in1=g_ap, op0=ALU.add, op1=ALU.mult),
                    reads=[("modv", l % 2, sub), "gnorm"], writes=[("amul", l % 2, sub)])
                gs = 1.0 if sub == 1 else 0.5
                S.op("dve", lambda e, g_out=g_out, gt_in=gt_in, gs=gs: e.tensor_scalar(
                    out=g_out, in0=gt_in, scalar1=gs, scalar2=None, op0=ALU.mult),
                    reads=[("modv", l % 2, sub)], writes=[("gate", l % 2, sub)])
            bg_done.add((l, sub))
            ps_release(pb)
            yield "sub"

    def A_ap(l, sub, kc, s):
        o = sub * 16 + kc * 2 + s
        return amul[l % 2][:, o:o + 1]

    def G_ap(l, sub, kc, s):
        o = sub * 16 + kc * 2 + s
        return gate[l % 2][:, o:o + 1]

    def B_ap(l, sub, kc, s):
        o = ((3 * sub) * 8 + kc) * 2 + s
        return modv[l % 2][:, o:o + 1]

    pend = {"banks": None, "n": 0}

    def rms_begin():
        pend["banks"] = [ps_alloc(hold=True) for _ in TT]
        pend["n"] = 0

    def rms_chunk(kc):
        banks = pend["banks"]
        n = pend["n"]
        pend["n"] = n + 1
        sqb = sq[kc % 2]
        S.op("act", lambda e: e.activation(out=sqb[:], in_=x[:, kc, :], func=AF.Square),
             reads=[("x", kc)], writes=[("sq", kc % 2)])
        for ti, (t0, tn) in enumerate(TT):
            S.op("pe", lambda e, b=banks[ti], t0=t0, tn=tn: e.matmul(
                psb[b][:, 0:tn], lhsT=ones_bf[:], rhs=sqb[:, t0:t0 + tn], start=(n == 0), stop=(n == 7)),
                reads=[("sq", kc % 2), "ones"], writes=[("ps", banks[ti])])

    def rms_stats():
        if pend["banks"] is None:
            rms_begin()
            for kc in range(8):
                rms_chunk(kc)
        assert pend["n"] == 8
        banks = pend["banks"]
        for ti, (t0, tn) in enumerate(TT):
            S.op("act", lambda e, b=banks[ti], t0=t0, tn=tn: e.activation(
                out=rstd[:, t0:t0 + tn], in_=psb[b][:, 0:tn], func=AF.Ln, bias=eps_t[:], scale=1.0 / D),
                reads=[("ps", banks[ti]), "eps"], writes=[("rstd", ti)])
            S.op("act", lambda e, t0=t0, tn=tn: e.activation(
                out=rstd[:, t0:t0 + tn], in_=rstd[:, t0:t0 + tn], func=AF.Exp, scale=-0.5),
                reads=[("rstd", ti)], writes=[("rstd", ti)])
        ps_release(*banks)
        pend["banks"] = None

    def norm_mod(l, sub):
        rms_stats()
        for kc in range(8):
            tb = tmp[kc % 2]
            for s, (c0, cn) in enumerate(SLOTS):
                S.op("dve", lambda e, kc=kc, s=s, c0=c0, cn=cn, tb=tb: e.scalar_tensor_tensor(
                    out=tb[:, c0:c0 + cn], in0=x[:, kc, c0:c0 + cn], scalar=A_ap(l, sub, kc, s),
                    in1=rstd[:, c0:c0 + cn], op0=ALU.mult, op1=ALU.mult),
                    reads=[("x", kc), ("rstd", 0), ("rstd", 1), ("rstd", 2), ("amul", l % 2, sub)],
                    writes=[("tmp", kc % 2, s)])
                S.op("act", lambda e, kc=kc, s=s, c0=c0, cn=cn, tb=tb: e.activation(
                    out=h[:, kc, c0:c0 + cn], in_=tb[:, c0:c0 + cn], func=AF.Identity,
                    bias=B_ap(l, sub, kc, s), scale=1.0),
                    reads=[("tmp", kc % 2, s), ("modv", l % 2, sub)], writes=[("h", kc)])

    def ffn(l, si):
        sub = 0 if si == 0 else 2
        if pend["banks"] is None:
            rms_begin()
            for kc_ in range(8):
                rms_chunk(kc_)
        bg_ensure(l, sub)
        norm_mod(l, sub)
        hreads = [("h", kc) for kc in range(8)]
        for t in range(11):
            bg_step(3)
            slot = load_w(wgu_d[l, si, t], 4096)
            for fcl in range(2):
                j = 2 * t + fcl
                gb = [ps_alloc() for _ in TT]
                for kc in range(8):
                    off = ((0 * 2 + fcl) * 8 + kc) * 128
                    for ti, (t0, tn) in enumerate(TT):
                        S.op("pe", lambda e, b=gb[ti], slot=slot, off=off, kc=kc, t0=t0, tn=tn: e.matmul(
                            psb[b][:, 0:tn], lhsT=wbuf[slot][:, off:off + 128], rhs=h[:, kc, t0:t0 + tn],
                            start=(kc == 0), stop=(kc == 7)),
                            reads=[("w", slot), ("h", kc)], writes=[("ps", gb[ti])])
                sgb = sg[j % 2]
                for ti, (t0, tn) in enumerate(TT):
                    S.op("act", lambda e, b=gb[ti], t0=t0, tn=tn, sgb=sgb: e.activation(
                        out=sgb[:, t0:t0 + tn], in_=psb[b][:, 0:tn], func=AF.Silu),
                        reads=[("ps", gb[ti])], writes=[("sg", j % 2, ti)])
                ub = [ps_alloc() for _ in TT]
                for kc in range(8):
                    off = ((1 * 2 + fcl) * 8 + kc) * 128
                    for ti, (t0, tn) in enumerate(TT):
                        S.op("pe", lambda e, b=ub[ti], slot=slot, off=off, kc=kc, t0=t0, tn=tn: e.matmul(
                            psb[b][:, 0:tn], lhsT=wbuf[slot][:, off:off + 128], rhs=h[:, kc, t0:t0 + tn],
                            start=(kc == 0), stop=(kc == 7)),
                            reads=[("w", slot), ("h", kc)], writes=[("ps", ub[ti])])
                for ti, (t0, tn) in enumerate(TT):
                    S.op("dve", lambda e, b=ub[ti], t0=t0, tn=tn, sgb=sgb, j=j: e.tensor_tensor(
                        out=act[:, j, t0:t0 + tn], in0=psb[b][:, 0:tn], in1=sgb[:, t0:t0 + tn], op=ALU.mult),
                        reads=[("ps", ub[ti]), ("sg", j % 2, ti), "bigown"], writes=[("act", j)])
        areads = [("act", j) for j in range(NFC)] + ["bigown"]
        rms_begin()
        for j in range(8):
            slot = load_w(wdn_d[l, si, j], 2816)
            ob = [ps_alloc() for _ in TT]
            for kc in range(NFC):
                for ti, (t0, tn) in enumerate(TT):
                    S.op("pe", lambda e, b=ob[ti], slot=slot, kc=kc, t0=t0, tn=tn: e.matmul(
                        psb[b][:, 0:tn], lhsT=wbuf[slot][:, kc * 128:(kc + 1) * 128], rhs=act[:, kc, t0:t0 + tn],
                        start=(kc == 0), stop=(kc == NFC - 1)),
                        reads=[("w", slot)] + areads, writes=[("ps", ob[ti])])
            for ti, (t0, tn) in enumerate(TT):
                s = SLOT_OF_TT[ti]
                S.op("dve", lambda e, b=ob[ti], t0=t0, tn=tn, j=j, s=s: e.scalar_tensor_tensor(
                    out=x[:, j, t0:t0 + tn], in0=psb[b][:, 0:tn], scalar=G_ap(l, sub, j, s),
                    in1=x[:, j, t0:t0 + tn], op0=ALU.mult, op1=ALU.add),
                    reads=[("ps", ob[ti]), ("gate", l % 2, sub), ("x", j)], writes=[("x", j)])
            if j >= 1:
                rms_chunk(j - 1)
        rms_chunk(7)


    in_mixer = [False]

    def _r(reads):
        return list(reads) + (["bigown"] if in_mixer[0] else [])

    def MM(bank, out_ap, lhsT, rhs, start, stop, reads):
        reads = _r(reads)
        S.op("pe", lambda e: e.matmul(out_ap, lhsT=lhsT, rhs=rhs, start=start, stop=stop),
             reads=reads, writes=[("ps", bank)])

    def ACTF(out, in_, func, reads, writes, bias=None, scale=None):
        kw = {}
        if bias is not None:
            kw["bias"] = bias
        if scale is not None:
            kw["scale"] = scale
        reads = _r(reads)
        return S.op("act", lambda e: e.activation(out=out, in_=in_, func=func, **kw), reads=reads, writes=writes)

    def TTO(eng, out, in0, in1, op, reads, writes):
        reads = _r(reads)
        return S.op(eng, lambda e: e.tensor_tensor(out=out, in0=in0, in1=in1, op=op), reads=reads, writes=writes)

    def STT(out, in0, scalar, in1, op0, op1, reads, writes):
        reads = _r(reads)
        return S.op("dve", lambda e: e.scalar_tensor_tensor(out=out, in0=in0, scalar=scalar, in1=in1, op0=op0, op1=op1),
                    reads=reads, writes=writes)

    def TS(eng, out, in0, s1, s2, op0, op1, reads, writes):
        reads = _r(reads)
        if op1 is None:
            return S.op(eng, lambda e: e.tensor_scalar(out=out, in0=in0, scalar1=s1, scalar2=None, op0=op0),
                        reads=reads, writes=writes)
        return S.op(eng, lambda e: e.tensor_scalar(out=out, in0=in0, scalar1=s1, scalar2=s2, op0=op0, op1=op1),
                    reads=reads, writes=writes)

    def RECIP(out, in_, reads, writes):
        reads = _r(reads)
        return S.op("dve", lambda e: e.reciprocal(out=out, in_=in_), reads=reads, writes=writes)

    def MEMSET(eng, ap, val, reads, writes):
        reads = _r(reads)
        return S.op(eng, lambda e: e.memset(ap, val), reads=reads, writes=writes)

    def DMA(eng, out, in_, reads, writes, key):
        reads = _r(reads)
        return S.op(eng, lambda e: e.dma_start(out=out, in_=in_), reads=reads, writes=writes, dma=key)

    class WStream:
        def __init__(self, dram_ap, lay):
            self.d = dram_ap
            self.lay = lay
            self.loaded = {}
            self.used = {}
            for (t, off, KC, n) in lay:
                self.used[t] = max(self.used.get(t, 0), off + KC * n)

        def blk(self, b, kc):
            t, off, KC, n = self.lay[b]
            if t not in self.loaded:
                u = self.used[t]
                self.loaded[t] = load_w(self.d[t][:, 0:u], u)
            slot = self.loaded[t]
            return slot, wbuf[slot][:, off + kc * n:off + (kc + 1) * n]

    def proj(ws, b, KC, rhs_fn, rreads, M, tiles):
        banks = [ps_alloc(hold=True) for _ in tiles]
        for kc in range(KC):
            slot, lw = ws.blk(b, kc)
            for ti, (t0, tn) in enumerate(tiles):
                rr = rreads(kc) if callable(rreads) else rreads
                MM(banks[ti], psb[banks[ti]][0:M, 0:tn], lw, rhs_fn(kc, t0, tn), kc == 0, kc == KC - 1,
                   [("w", slot)] + rr)
        bg_step(1)
        return banks

    out_evs = []
    fence_n = [0]

    def fence():
        k = fence_n[0] % 2
        fence_n[0] += 1
        in_mixer[0] = False
        MEMSET("dve", fence_t[:, k:k + 1], 0.0, [], ["bigown"])

    def gated_out(ws, l, hreads):
        rms_begin()
        for j in range(8):
            banks = proj(ws, j, 8, lambda kc, t0, tn: h[:, kc, t0:t0 + tn], hreads, 128, TT)
            for ti, (t0, tn) in enumerate(TT):
                s_ = SLOT_OF_TT[ti]
                STT(x[:, j, t0:t0 + tn], psb[banks[ti]][:, 0:tn], G_ap(l, 1, j, s_), x[:, j, t0:t0 + tn],
                    ALU.mult, ALU.add, [("ps", banks[ti]), ("gate", l % 2, 1), ("x", j)], [("x", j)])
            ps_release(*banks)
            if j >= 1:
                rms_chunk(j - 1)
        rms_chunk(7)

    def odd_mixer(l):
        i = l // 2

        def ov(o, n=1):
            return oddv[:, i * NOV + o:i * NOV + o + n]

        bg_flush()
        norm_mod(l, 1)
        fence()
        in_mixer[0] = True
        BO = ["bigown"]
        hreads = lambda kc: [("h", kc)]
        hall = [("h", kc) for kc in range(8)]
        off = [0]

        def carve(nw):
            r = big[:, off[0]:off[0] + nw]
            off[0] += nw
            return r

        ckvn = carve(1536).bitcast(BF16).rearrange("p (c t) -> p c t", c=2)
        QN = carve(2560).bitcast(BF16).rearrange("p (m t) -> p m t", m=4)
        QR = carve(2560).bitcast(BF16).rearrange("p (m t) -> p m t", m=4)
        KN = carve(3072).bitcast(BF16).rearrange("p (m t) -> p m t", m=4)
        Vt = carve(3072).bitcast(BF16).rearrange("p (k f) -> p k f", k=12)
        ropeC = carve(512).bitcast(BF16)
        ropeS = carve(512).bitcast(BF16)
        eoff = [0]

        def ecarve(nw):
            r = ext[:, eoff[0]:eoff[0] + nw]
            eoff[0] += nw
            return r

        cqg = ecarve(1920).bitcast(BF16).rearrange("p (c t) -> p c t", c=3)
        KR2 = ecarve(768).bitcast(BF16)
        cpadB = ecarve(2860).bitcast(BF16).rearrange("p (c t) -> p c t", c=4)
        rcb = rstd2[:, 0:512]
        dgr = ecarve(256).bitcast(BF16).rearrange("p (r i) -> p r i", r=4)
        PT = [ecarve(256).bitcast(BF16) for _ in range(2)]
        tmp0t = [("tmp", 0, 0), ("tmp", 0, 1)]
        tmp1t = [("tmp", 1, 0), ("tmp", 1, 1)]
        tmpt = [tmp0t, tmp1t]
        rst = [("rstd", 0), ("rstd", 1), ("rstd", 2)]

        DMA("pool", ckvn[:, :, 1280:1536], cckv_d[i], BO, ["ckvn_c"], "mA")
        DMA("pool", KR2[0:64, 1280:1536], ckr_d[i], [], ["KR2_c"], "mB")
        DMA("pool", ropeC[0:64, :], rope_d[0], BO, ["ropeC"], "mC")
        DMA("pool", ropeS[0:64, :], rope_d[1], BO, ["ropeS"], "mD")

        WI = WStream(woin_d[i], LAY_OIN)

        def hrhs(kc, t0, tn):
            return h[:, kc, t0:t0 + tn]

        def rms_finish(statb, dst, scale, toks):
            for ti, (t0, tn) in enumerate(TT):
                ACTF(dst[:, t0:t0 + tn], psb[statb[ti]][:, 0:tn], AF.Ln, [("ps", statb[ti]), "eps"], [toks[ti]],
                     bias=eps_t[:], scale=scale)
                ACTF(dst[:, t0:t0 + tn], dst[:, t0:t0 + tn], AF.Exp, [toks[ti]], [toks[ti]], scale=-0.5)
            ps_release(*statb)

        statb = [ps_alloc(hold=True) for _ in TT]
        for c in range(3):
            banks = proj(WI, c, 8, hrhs, hreads, 128, TT)
            for ti, (t0, tn) in enumerate(TT):
                ACTF(cqg[:, c, t0:t0 + tn], psb[banks[ti]][:, 0:tn], AF.Identity, [("ps", banks[ti]), "oddv"],
                     [("cqg", c)], scale=ov(c))
                ACTF(sq[c % 2][:, t0:t0 + tn], psb[banks[ti]][:, 0:tn], AF.Square, [("ps", banks[ti])],
                     [("sq", c % 2)])
            ps_release(*banks)
            for ti, (t0, tn) in enumerate(TT):
                MM(statb[ti], psb[statb[ti]][:, 0:tn], ones_bf[:], sq[c % 2][:, t0:t0 + tn], c == 0, c == 2,
                   [("sq", c % 2), "ones"])
        rms_finish(statb, rstd, 1.0 / 384, rst)

        banks = proj(WI, 3, 8, hrhs, hreads, 64, TT)
        for ti, (t0, tn) in enumerate(TT):
            ACTF(tmp[0][0:64, t0:t0 + tn], psb[banks[ti]][0:64, 0:tn], AF.Identity, [("ps", banks[ti])], tmp0t)
        ps_release(*banks)
        out_evs.append(DMA("sp", krT_d[i], tmp[0][0:32, 0:NT], tmp0t, [], "o_tmp0"))
        banks = proj(WI, 4, 8, hrhs, hreads, 64, TT[0:2])
        for ti, (t0, tn) in enumerate(TT[0:2]):
            TTO("dve", tmp[1][0:64, t0:t0 + tn], tmp[0][0:64, t0:t0 + tn], ropeC[0:64, t0:t0 + tn], ALU.mult,
                tmp0t + ["ropeC"] + BO, tmp1t)
            TTO("dve", rstd2[0:64, t0:t0 + tn], psb[banks[ti]][0:64, 0:tn], ropeS[0:64, t0:t0 + tn], ALU.mult,
                [("ps", banks[ti]), "ropeS"] + BO, ["rstd2"])
            TTO("dve", KR2[0:64, t0:t0 + tn], tmp[1][0:64, t0:t0 + tn], rstd2[0:64, t0:t0 + tn], ALU.add,
                tmp1t + ["rstd2"], ["KR2"])
        ps_release(*banks)
        ACTF(KR2[0:64, 1024:1280], tmp[0][0:64, 1024:1280], AF.Identity, tmp0t, ["KR2"])

        statb = [ps_alloc(hold=True) for _ in TT]
        for c in range(2):
            banks = proj(WI, 5 + c, 8, hrhs, hreads, 128, TT)
            for ti, (t0, tn) in enumerate(TT):
                ACTF(tmp[c][:, t0:t0 + tn], psb[banks[ti]][:, 0:tn], AF.Identity, [("ps", banks[ti]), "oddv"],
                     tmpt[c], scale=ov(3 + c))
                ACTF(sq[c % 2][:, t0:t0 + tn], psb[banks[ti]][:, 0:tn], AF.Square, [("ps", banks[ti])],
                     [("sq", c % 2)])
            ps_release(*banks)
            for ti, (t0, tn) in enumerate(TT):
                MM(statb[ti], psb[statb[ti]][:, 0:tn], ones_bf[:], sq[c % 2][:, t0:t0 + tn], c == 0, c == 1,
                   [("sq", c % 2), "ones"])
        rms_finish(statb, rstd2, 1.0 / 256, ["rstd2", "rstd2", "rstd2"])
        for c in range(2):
            TTO("dve", tmp[c][:, 0:NT], tmp[c][:, 0:NT], rstd2[:, :], ALU.mult, tmpt[c] + ["rstd2"], tmpt[c])
            out_evs.append(DMA("sp", ckvT_d[i, c], tmp[c][:, 0:NT], tmpt[c], [], "o_tmp%d" % c))
            ACTF(ckvn[:, c, 0:NT], tmp[c][:, 0:NT], AF.Identity, tmpt[c] + BO, ["ckvn"])

        def cp5(c):
            return cpadB[:, c, :].rearrange("p (s w) -> p s w", s=5)

        for c in range(4):
            MEMSET("dve", cp5(c)[:, :, 0:15], 0.0, [], [("cpad", c)])
            MEMSET("dve", cp5(c)[:, :, 271:286], 0.0, [], [("cpad", c)])
            ba = proj(WI, 7 + 2 * c, 8, hrhs, hreads, 128, TT)
            bg = proj(WI, 8 + 2 * c, 8, hrhs, hreads, 128, TT)
            segs = [(0, 2), (2, 4), (4, 5)]
            for ti, (t0, tn) in enumerate(TT):
                ACTF(sg[c % 2][:, t0:t0 + tn], psb[bg[ti]][:, 0:tn], AF.Sigmoid, [("ps", bg[ti])], [("sg", c % 2, ti)])
                s0, s1 = segs[ti]
                TTO("dve", cp5(c)[:, s0:s1, 15:271], psb[ba[ti]][:, 0:tn].rearrange("p (s w) -> p s w", w=256),
                    sg[c % 2][:, t0:t0 + tn].rearrange("p (s w) -> p s w", w=256), ALU.mult,
                    [("ps", ba[ti]), ("sg", c % 2, ti)], [("cpad", c)])
            ps_release(*ba)
            ps_release(*bg)
            TS("dve", cp5(c)[:, 1:4, 0:15], cp5(c)[:, 0:3, 256:271], flags[:, 0:1], None, ALU.mult, None,
               [("cpad", c), "flags"], [("cpad", c)])
            TS("dve", cp5(c)[:, 0:3, 271:286], cp5(c)[:, 1:4, 15:30], flags[:, 0:1], None, ALU.mult, None,
               [("cpad", c), "flags"], [("cpad", c)])

        WQ = WStream(wuq_d[i], LAY_UQ)

        def qrhs(kc, t0, tn):
            return cqg[:, kc, t0:t0 + tn]

        cqr = [("cqg", c) for c in range(3)]
        for m in range(4):
            banks = proj(WQ, m, 3, qrhs, cqr, 128, TT)
            for ti, (t0, tn) in enumerate(TT):
                TTO("dve", QN[:, m, t0:t0 + tn], psb[banks[ti]][:, 0:tn], rstd[:, t0:t0 + tn], ALU.mult,
                    [("ps", banks[ti]), rst[ti]] + BO, ["QN"])
            ps_release(*banks)
        for m in range(4):
            bq = proj(WQ, 4 + m, 3, qrhs, cqr, 64, TT)
            bp = proj(WQ, 8 + m, 3, qrhs, cqr, 64, TT[0:2])
            for ti, (t0, tn) in enumerate(TT[0:2]):
                TTO("dve", tmp[0][0:64, t0:t0 + tn], psb[bq[ti]][0:64, 0:tn], ropeC[0:64, t0:t0 + tn], ALU.mult,
                    [("ps", bq[ti]), "ropeC"] + BO, tmp0t)
                TTO("dve", tmp[1][0:64, t0:t0 + tn], psb[bp[ti]][0:64, 0:tn], ropeS[0:64, t0:t0 + tn], ALU.mult,
                    [("ps", bp[ti]), "ropeS"] + BO, tmp1t)
                TTO("dve", tmp[0][0:64, t0:t0 + tn], tmp[0][0:64, t0:t0 + tn], tmp[1][0:64, t0:t0 + tn], ALU.add,
                    tmp0t + tmp1t, tmp0t)
                TTO("dve", QR[0:64, m, t0:t0 + tn], tmp[0][0:64, t0:t0 + tn], rstd[0:64, t0:t0 + tn], ALU.mult,
                    tmp0t + [rst[ti]] + BO, ["QR"])
            t0, tn = TT[2]
            TTO("dve", QR[0:64, m, t0:t0 + tn], psb[bq[2]][0:64, 0:tn], rstd[0:64, t0:t0 + tn], ALU.mult,
                [("ps", bq[2]), rst[2]] + BO, ["QR"])
            ps_release(*bq)
            ps_release(*bp)

        dbuf = [sq[0], sq[1], sg[0], sg[1]]
        dtok = [[("sq", 0)], [("sq", 1)], [("sg", 0, t_) for t_ in range(3)], [("sg", 1, t_) for t_ in range(3)]]
        conv_ops = []

        def mk_conv(c):
            accf = tmp[c % 2][:, 0:NT]
            acc = accf.rearrange("p (s w) -> p s w", s=5)
            c5 = cp5(c)
            tt_ = tmpt[c % 2]
            conv_ops.append(lambda: TS("dve", acc, c5[:, :, 0:256], ov(5 + c * 31), ov(129 + c), ALU.mult, ALU.add,
                                       [("cpad", c), "oddv"], tt_))
            for k in range(1, 31):
                conv_ops.append(lambda k=k: STT(acc, c5[:, :, k:k + 256], ov(5 + c * 31 + k), acc, ALU.mult, ALU.add,
                                                [("cpad", c), "oddv"] + tt_, tt_))
            conv_ops.append(lambda: S.op("pool", lambda e: e.tensor_copy(out=dbuf[c][:, :], in_=accf),
                                         reads=_r(tt_), writes=dtok[c]))

        for c in range(4):
            mk_conv(c)
        conv_pos = [0]

        def conv_some(n):
            for _ in range(n):
                if conv_pos[0] < len(conv_ops):
                    conv_ops[conv_pos[0]]()
                    conv_pos[0] += 1

        WK = WStream(wukv_d[i], LAY_UKV)
        KT = [(0, 512), (512, 512), (1024, 512)]
        ckr_ = ["ckvn", "ckvn_c"] + BO

        def krhs(kc, t0, tn):
            return ckvn[:, kc, t0:t0 + tn]

        for m in range(4):
            banks = proj(WK, m, 2, krhs, ckr_, 128, KT)
            for ti, (t0, tn) in enumerate(KT):
                ACTF(KN[:, m, t0:t0 + tn], psb[banks[ti]][:, 0:tn], AF.Identity, [("ps", banks[ti])] + BO, ["KN"])
            ps_release(*banks)
            conv_some(3)
        for kb in range(12):
            bk = ps_alloc(hold=True)
            for kc in range(2):
                slot, rw = WK.blk(4, kc)
                MM(bk, psb[bk][:, 0:512], ckvn[:, kc, kb * 128:(kb + 1) * 128], rw, kc == 0, kc == 1,
                   [("w", slot)] + ckr_)
            if kb % 2 == 0:
                ACTF(Vt[:, kb, :], psb[bk][:, 0:512], AF.Identity, [("ps", bk)] + BO, ["Vt"])
            else:
                S.op("dve", lambda e, kb=kb, bk=bk: e.tensor_copy(out=Vt[:, kb, :], in_=psb[bk][:, 0:512]),
                     reads=[("ps", bk)] + BO, writes=["Vt"])
            ps_release(bk)
            conv_some(3)

        sm_scale = 96.0 ** -0.5
        zb, nb_ = flags[:, 2:3], flags[:, 1:2]
        PT4 = PT + [carve(256).bitcast(BF16) for _ in range(2)]
        steps = []
        for m in range(4):
            for (qc, qn, kbs) in [(0, 512, list(range(8)) + [10, 11]), (512, 512, list(range(8)) + [10, 11]),
                                  (1024, 256, [8, 9])]:
                for ki, kb in enumerate(kbs):
                    halves = []
                    for hc in range(0, qn, 256):
                        qseq = (qc + hc) // 256
                        halves.append(zb if (kb // 2) == qseq else nb_)
                    steps.append((m, qc, qn, kb, ki == 0, ki == len(kbs) - 1, halves))
        sbk_of = {}
        grp = {}

        def stageS(k):
            m, qc, qn, kb, first, last, halves = steps[k]
            k0 = kb * 128
            sb2 = [ps_alloc(hold=True) for _ in range(2)]
            sbk_of[k] = sb2
            for hh in range(2):
                r0 = hh * 64
                MM(sb2[hh], psb[sb2[hh]][:, 0:qn], KN[r0:r0 + 64, m, k0:k0 + 128], QN[r0:r0 + 64, m, qc:qc + qn],
                   True, False, ["KN", "QN"])
            for hh in range(2):
                q0 = hh * 32
                MM(sb2[hh], psb[sb2[hh]][:, 0:qn], KR2[q0:q0 + 32, k0:k0 + 128], QR[q0:q0 + 32, m, qc:qc + qn],
                   False, True, ["KR2", "KR2_c", "QR"])

        def stageE(k):
            m, qc, qn, kb, first, last, halves = steps[k]
            sb2 = sbk_of.pop(k)
            for hh in range(2):
                pi = (k % 2) * 2 + hh
                pt = PT4[pi]
                ptt = ("PT", pi)
                if len(halves) == 1 or halves[0] is halves[1]:
                    ACTF(pt[:, 0:qn], psb[sb2[hh]][:, 0:qn], AF.Exp, [("ps", sb2[hh]), "flags"], [ptt], bias=halves[0],
                         scale=sm_scale)
                else:
                    for hi, hb in enumerate(halves):
                        ACTF(pt[:, hi * 256:(hi + 1) * 256], psb[sb2[hh]][:, hi * 256:(hi + 1) * 256], AF.Exp,
                             [("ps", sb2[hh]), "flags"], [ptt], bias=hb, scale=sm_scale)
            ps_release(*sb2)

        def stagePV(k):
            m, qc, qn, kb, first, last, halves = steps[k]
            if first:
                grp["ob"] = [ps_alloc(hold=True) for _ in range(2)]
                grp["smb"] = [ps_alloc(hold=True) for _ in range(2)]
            for hh in range(2):
                r0 = hh * 64
                ob, smb = grp["ob"][hh], grp["smb"][hh]
                pi = (k % 2) * 2 + hh
                pt = PT4[pi]
                ptt = ("PT", pi)
                MM(ob, psb[ob][:, 0:qn], Vt[:, kb, m * 128:(m + 1) * 128], pt[:, 0:qn], first, last, ["Vt", ptt])
                MM(smb, psb[smb][:, 0:qn], ones_bf[:], pt[:, 0:qn], first, last, [ptt, "ones"])
                if last:
                    ACTF(rcb[r0:r0 + 64, 0:qn], psb[smb][r0:r0 + 64, 0:qn], AF.Ln, [("ps", smb)], ["rstd2"])
                    ACTF(rcb[r0:r0 + 64, 0:qn], rcb[r0:r0 + 64, 0:qn], AF.Exp, ["rstd2"], ["rstd2"], scale=-1.0)
                    TTO("dve", h[r0:r0 + 64, m, qc:qc + qn], psb[ob][r0:r0 + 64, 0:qn], rcb[r0:r0 + 64, 0:qn], ALU.mult,
                        [("ps", ob), "rstd2"], [("h", m)])
            if last:
                ps_release(*grp["ob"])
                ps_release(*grp["smb"])

        stageS(0)
        for k in range(len(steps)):
            if k + 1 < len(steps):
                stageS(k + 1)
            stageE(k)
            stagePV(k)
            conv_some(1)
        conv_some(len(conv_ops))

        s1 = [ps_alloc(hold=True) for _ in TT]
        s2 = [ps_alloc(hold=True) for _ in TT]
        for c in range(4):
            dc = dbuf[c]
            sqc = cpadB[:, c, 0:NT]
            ACTF(sqc, dc[:, :], AF.Square, dtok[c], [("cpad", c)])
            for ti, (t0, tn) in enumerate(TT):
                MM(s1[ti], psb[s1[ti]][:, 0:tn], ones_bf[:], dc[:, t0:t0 + tn], c == 0, c == 3, dtok[c] + ["ones"])
                MM(s2[ti], psb[s2[ti]][:, 0:tn], ones_bf[:], sqc[:, t0:t0 + tn], c == 0, c == 3,
                   [("cpad", c), "ones"])
        for ti, (t0, tn) in enumerate(TT):
            ACTF(rstd[:, t0:t0 + tn], psb[s1[ti]][:, 0:tn], AF.Identity, [("ps", s1[ti])], [rst[ti]], scale=1.0 / 512)
            ACTF(tmp[0][:, t0:t0 + tn], psb[s1[ti]][:, 0:tn], AF.Square, [("ps", s1[ti])], tmp0t, scale=1.0 / 512)
            STT(tmp[0][:, t0:t0 + tn], psb[s2[ti]][:, 0:tn], 1.0 / 512, tmp[0][:, t0:t0 + tn], ALU.mult, ALU.subtract,
                [("ps", s2[ti])] + tmp0t, tmp0t)
            ACTF(rstd2[:, t0:t0 + tn], tmp[0][:, t0:t0 + tn], AF.Ln, tmp0t + ["eps"], ["rstd2"], bias=eps_t[:], scale=1.0)
            ACTF(rstd2[:, t0:t0 + tn], rstd2[:, t0:t0 + tn], AF.Exp, ["rstd2"], ["rstd2"], scale=-0.5)
        ps_release(*s1)
        ps_release(*s2)
        for c in range(4):
            dc = dbuf[c][:, :]
            TTO("dve", tmp[1][:, 0:NT], dc, rstd[:, :], ALU.subtract, dtok[c] + rst, tmp1t)
            TTO("dve", tmp[1][:, 0:NT], tmp[1][:, 0:NT], rstd2[:, :], ALU.mult, tmp1t + ["rstd2"], tmp1t)
            ACTF(h[:, 4 + c, :], tmp[1][:, 0:NT], AF.Silu, tmp1t + ["oddv"], [("h", 4 + c)],
                 bias=ov(137 + c), scale=ov(133 + c))

        WO = WStream(woout_d[i], LAY_OUT)
        gated_out(WO, l, hreads)
        fence()


    Umat = cst[:, 0:128]
    Lmat = cst[:, 128:256]
    NEGU = cst[:, 256:384]
    NEGL = cst[:, 384:512]
    identf = cst[:, 512:640]
    onesf = cst[:, 640:768]

    def even_mixer(l):
        i = l // 2

        def ev(o, n=1):
            return evv[:, i * NEV + o:i * NEV + o + n]

        bg_flush()
        norm_mod(l, 1)
        fence()
        in_mixer[0] = True
        hreads = lambda kc: [("h", kc)]
        hall = [("h", kc) for kc in range(8)]
        off = [0]

        def carve(nw):
            r = big[:, off[0]:off[0] + nw]
            off[0] += nw
            return r

        Ub = carve(2560).bitcast(BF16).rearrange("p (m t) -> p m t", m=4)
        Zb = carve(2560).bitcast(BF16).rearrange("p (m t) -> p m t", m=4)
        XS = carve(2560).bitcast(BF16).rearrange("p (m t) -> p m t", m=4)
        BC = carve(2560).bitcast(BF16).rearrange("p (m t) -> p m t", m=4)
        Hb = carve(2560).bitcast(BF16).rearrange("p (c f) -> p c f", c=10)
        SC = carve(960).rearrange("p (k c f) -> p k c f", k=6, c=10)
        Mb_b = carve(512).bitcast(BF16).rearrange("p (h i) -> p h i", h=8)
        eoff = [0]

        def ecarve(nw):
            r = ext[:, eoff[0]:eoff[0] + nw]
            eoff[0] += nw
            return r

        evb = ecarve(NEB)
        WsT = ecarve(256).bitcast(BF16).rearrange("p (g i) -> p g i", g=4)
        REf = ecarve(1024).rearrange("p (h i) -> p h i", h=8)
        REb = ecarve(1024).rearrange("p (h i) -> p h i", h=8)
        Mb = ecarve(512).bitcast(BF16).rearrange("p (h i) -> p h i", h=8)
        Xt = ecarve(256).bitcast(BF16)
        Btok = ecarve(128).bitcast(BF16)
        Xw = ecarve(256).bitcast(BF16)
        vtm = ecarve(256).bitcast(BF16)
        Wtok = ecarve(256).bitcast(BF16)
        Hfc = ecarve(512)
        Hbc = ecarve(512)
        Hfb = ecarve(256).bitcast(BF16)
        gv_bc = evb[:, 0:512]
        bs_bc = evb[:, 512:1024]
        dtb_bc = evb[:, 1024:1040]
        alog_bc = evb[:, 1040:1056]
        flagA = flags[:, 0:1]
        tmp0t = [("tmp", 0, 0), ("tmp", 0, 1)]
        tmp1t = [("tmp", 1, 0), ("tmp", 1, 1)]
        tmpt = [tmp0t, tmp1t]

        DMA("sp", evb, evb_d[i], [], ["evb"], "mE")
        DMA("pool", WsT, wsT_d[i].rearrange("p (g i) -> p g i", g=4), [], ["WsT"], "mF")
        ACTF(aneg[:], alog_bc, AF.Exp, ["evb"], ["aneg"])
        TS("dve", aneg[:], aneg[:], -1.0, None, ALU.mult, None, ["aneg"], ["aneg"])

        TTO("dve", ev(32, 4), ev(32, 4), ev(36, 4), ALU.add, ["evv"], ["dsk"])
        WE = WStream(wein_d[i], LAY_EIN)

        def hrhs(kc, t0, tn):
            return h[:, kc, t0:t0 + tn]

        for c in range(4):
            banks = proj(WE, c, 8, hrhs, hreads, 128, TT)
            for ti, (t0, tn) in enumerate(TT):
                ACTF(Ub[:, c, t0:t0 + tn], psb[banks[ti]][:, 0:tn], AF.Gelu_apprx_tanh, [("ps", banks[ti])], [("Ub", c)])
            ps_release(*banks)
        for c in range(4):
            banks = proj(WE, 4 + c, 8, hrhs, hreads, 128, TT)
            for ti, (t0, tn) in enumerate(TT):
                ACTF(Zb[:, c, t0:t0 + tn], psb[banks[ti]][:, 0:tn], AF.Silu, [("ps", banks[ti])], [("Zb", c)])
            ps_release(*banks)
        segs = [(0, 2), (2, 4), (4, 5)]
        for c in range(8):
            xp = tmp[c % 2][:, 0:1290].rearrange("p (s w) -> p s w", s=5)
            tt_ = tmpt[c % 2]
            MEMSET("dve", xp[:, :, 0:1], 0.0, [], tt_)
            MEMSET("dve", xp[:, :, 257:258], 0.0, [], tt_)
            banks = proj(WE, 8 + c, 8, hrhs, hreads, 128, TT)
            for ti, (t0, tn) in enumerate(TT):
                s0, s1 = segs[ti]
                ACTF(xp[:, s0:s1, 1:257], psb[banks[ti]][:, 0:tn].rearrange("p (s w) -> p s w", w=256), AF.Identity,
                     [("ps", banks[ti])], tt_)
            ps_release(*banks)
            TS("dve", xp[:, 1:4, 0:1], xp[:, 0:3, 256:257], flagA, None, ALU.mult, None, tt_ + ["flags"], tt_)
            TS("dve", xp[:, 0:3, 257:258], xp[:, 1:4, 1:2], flagA, None, ALU.mult, None, tt_ + ["flags"], tt_)
            accb = [rstd2, rstd][c % 2]
            acct = [["rstd2"], [("rstd", 0), ("rstd", 1), ("rstd", 2)]][c % 2]
            acc = accb[:, :].rearrange("p (s w) -> p s w", s=5)
            TS("dve", acc, xp[:, :, 0:256], ev(c * 3 + 0), ev(24 + c), ALU.mult, ALU.add, tt_ + ["evv"], acct)
            STT(acc, xp[:, :, 1:257], ev(c * 3 + 1), acc, ALU.mult, ALU.add, tt_ + ["evv"] + acct, acct)
            STT(acc, xp[:, :, 2:258], ev(c * 3 + 2), acc, ALU.mult, ALU.add, tt_ + ["evv"] + acct, acct)
            if c < 4:
                dst, dtok = XS[:, c, :], ("XS", c)
            else:
                dst, dtok = BC[:, c - 4, :], ("BC", c - 4)
            ACTF(dst, accb[:, :], AF.Silu, acct, [dtok])

        def bc8(ap8):
            return ap8.unsqueeze(2).broadcast_to([128, 8, 64])

        def v3(ap):
            return ap.rearrange("p (h q) -> p h q", h=8)

        def tok_major(c):
            cs = slice(c * 128, (c + 1) * 128)
            bk = ps_alloc(hold=True)
            pb = psb[bk][:, :].bitcast(BF16)
            for m in range(4):
                S.op("pe", lambda e, m=m: e.transpose(pb[:, m * 128:(m + 1) * 128], XS[:, m, cs], identb[:]),
                     reads=_r([("XS", m), "identb"]), writes=[("ps", bk)])
            for g in range(2):
                S.op("pe", lambda e, g=g: e.transpose(pb[:, 512 + g * 128:512 + (g + 1) * 128], BC[:, g, cs], identb[:]),
                     reads=_r([("BC", g), "identb"]), writes=[("ps", bk)])
            ACTF(Xt[:, :], pb[:, 0:512], AF.Identity, [("ps", bk)], ["Xt"])
            S.op("dve", lambda e: e.tensor_copy(out=Btok[:, :], in_=pb[:, 512:768]), reads=_r([("ps", bk)]), writes=["Btok"])
            ps_release(bk)

        def seq_of(c):
            return c // 2

        sct = "SC"

        def f160(ap3):
            return ap3.rearrange("p c f -> p (c f)")

        def v160(ap2):
            return ap2.rearrange("p (c f) -> p c f", f=16)

        P0 = cfg.get('ev_p0', 2)
        if P0 >= 1:
            bkd = ps_alloc(hold=True)
            for c in range(10):
                cs = slice(c * 128, (c + 1) * 128)
                for kc in range(8):
                    slot, rw = WE.blk(17, kc)
                    MM(bkd, psb[bkd][:, c * 16:(c + 1) * 16], h[:, kc, cs], rw, kc == 0, kc == 7, [("w", slot), ("h", kc)])
            TTO("dve", v160(s160[0][:, :]), v160(psb[bkd][:, 0:160]), dtb_bc.unsqueeze(1).broadcast_to([128, 10, 16]), ALU.add,
                [("ps", bkd), "evb"], [("s160", 0)])
            ps_release(bkd)
            ACTF(s160[0][:, :], s160[0][:, :], AF.Exp, [("s160", 0)], [("s160", 0)])
            ACTF(f160(SC[:, 0, :, :]), s160[0][:, :], AF.Ln, [("s160", 0), "one"], [sct], bias=one_t[:], scale=1.0)
            ACTF(s160[1][:, :], f160(SC[:, 0, :, :]), AF.Ln, [sct], [("s160", 1)])
            TTO("dve", SC[:, 1, :, :], SC[:, 0, :, :], aneg[:, :].unsqueeze(1).broadcast_to([128, 10, 16]), ALU.mult,
                [sct, "aneg"], [sct])
        if P0 >= 2:
            bkc = ps_alloc(hold=True)
            for c in range(10):
                MM(bkc, psb[bkc][:, c * 32:c * 32 + 8], Umat, SC[:, 1, c, 0:8], True, True, [sct, "cst"])
                MM(bkc, psb[bkc][:, c * 32 + 8:c * 32 + 16], Lmat, SC[:, 1, c, 8:16], True, True, [sct, "cst"])
                MM(bkc, psb[bkc][:, c * 32 + 16:c * 32 + 32], onesf, SC[:, 1, c, :], True, True, [sct, "cst"])
            if P0 == 3:
                ps_release(bkc)
            else:
                ACTF(tmp[0][:, 0:320], psb[bkc][:, 0:320], AF.Identity, [("ps", bkc)], tmp0t)
                ps_release(bkc)
                pc = tmp[0][:, 0:320].rearrange("p (c f) -> p c f", f=32)
                cumv = pc[:, :, 0:16]
                totv = pc[:, :, 16:32]
                TTO("dve", SC[:, 2, :, :], v160(s160[1][:, :]), cumv, ALU.subtract, [("s160", 1)] + tmp0t, [sct])
                ACTF(SC[:, 4, :, :], cumv, AF.Exp, tmp0t, [sct])
                ACTF(SC[:, 5, :, :], totv, AF.Exp, tmp0t, [sct])
                TTO("dve", v160(s160[0][:, :]), totv, cumv, ALU.subtract, tmp0t, [("s160", 0)])
                ACTF(s160[0][:, :], s160[0][:, :], AF.Exp, [("s160", 0)], [("s160", 0)])
                TTO("dve", SC[:, 3, :, :], v160(s160[0][:, :]), SC[:, 0, :, :], ALU.mult, [("s160", 0), sct], [sct])

        def chunk_state(c, d):
            TTO("dve", v3(Xw[:, :]), v3(Xt[:, :]), bc8(SC[:, 3, c, d * 8:(d + 1) * 8]), ALU.mult, ["Xt", sct], ["Xw"])
            bk = ps_alloc(hold=True)
            for g in range(2):
                MM(bk, psb[bk][:, g * 256:(g + 1) * 256], Btok[:, g * 128:(g + 1) * 128], Xw[:, g * 256:(g + 1) * 256],
                   True, True, ["Btok", "Xw"])
            return bk

        for c in ([9, 8, 7, 6, 5, 4, 3, 2, 1, 0] if cfg.get('ev_p1', True) else []):
            cs = slice(c * 128, (c + 1) * 128)
            par = c % 2
            tpt = tmpt[par]
            if c == 9:
                MEMSET("dve", Hbc[:, :], 0.0, [], ["Hbc"])
            if c == 7:
                DMA("sp", Hbc[:, :], h0T_d[i][:, 1, :], [], ["Hbc"], "mG")
            bk = ps_alloc(hold=True)
            for kc in range(8):
                slot, rw = WE.blk(16, kc)
                MM(bk, psb[bk][:, 0:512], h[:, kc, cs], rw, kc == 0, kc == 7, [("w", slot), ("h", kc)])
            ACTF(tmp[par][:, 0:512], psb[bk][:, 0:512], AF.Gelu_apprx_tanh, [("ps", bk)], tpt)
            ps_release(bk)
            S.op("act", lambda e, par=par: e.activation(out=sq[0][:, 0:512], in_=tmp[par][:, 0:512], func=AF.Square,
                                                        accum_out=sml[:, 2 * par:2 * par + 1]),
                 reads=_r(tpt), writes=[("sq", 0), ("sml", par)])
            ACTF(sml[:, 2 * par + 1:2 * par + 2], sml[:, 2 * par:2 * par + 1], AF.Ln, [("sml", par), "eps"], [("sml1", par)],
                 bias=eps_t[:], scale=1.0 / 512)
            ACTF(sml[:, 2 * par + 1:2 * par + 2], sml[:, 2 * par + 1:2 * par + 2], AF.Exp, [("sml1", par)], [("sml1", par)],
                 scale=-0.5)
            STT(vtm[:, :], tmp[par][:, 0:512], sml[:, 2 * par + 1:2 * par + 2], gv_bc, ALU.mult, ALU.mult,
                tpt + [("sml1", par), "evb"], ["vtm"])
            tok_major(c)
            bk = chunk_state(c, 1)
            ACTF(Hb[:, c, :], Hbc[:, :], AF.Identity, ["Hbc"], [("Hb", c)])
            TTO("dve", v3(Hbc[:, :]), v3(Hbc[:, :]), bc8(SC[:, 5, c, 8:16]), ALU.mult, ["Hbc", sct], ["Hbc"])
            TTO("dve", Hbc[:, :], Hbc[:, :], psb[bk][:, 0:512], ALU.add, ["Hbc", ("ps", bk)], ["Hbc"])
            ps_release(bk)
            bk = ps_alloc(hold=True)
            for g in range(4):
                MM(bk, psb[bk][:, g * 128:(g + 1) * 128], vtm[:, g * 128:(g + 1) * 128], WsT[:, g, :], True, True,
                   ["vtm", "WsT"])
            TTO("dve", tmp[par][:, 512:1024], psb[bk][:, 0:512], bs_bc, ALU.add, [("ps", bk), "evb"], tpt)
            ps_release(bk)
            TTO("dve", Ub[:, :, cs], tmp[par][:, 512:1024].rearrange("p (g i) -> p g i", g=4), Ub[:, :, cs], ALU.mult,
                tpt + [("Ub", m) for m in range(4)], [("Ub", m) for m in range(4)])
            if c % 2 == 0:
                out_evs.append(DMA("sp", ssdT_d[i, seq_of(c), 1], Hbc[:, :], ["Hbc"], [], "o_Hbc"))
                if c in (2, 4, 6):
                    TS("dve", Hbc[:, :], Hbc[:, :], flagA, None, ALU.mult, None, ["Hbc", "flags"], ["Hbc"])

        Mbs = [Mb, Mb_b]
        grp_s = {}

        def stageP(c):
            cs = slice(c * 128, (c + 1) * 128)
            Mc = Mbs[c % 2]
            bf_ = [ps_alloc(hold=True) for _ in range(2)]
            bb_ = [ps_alloc(hold=True) for _ in range(2)]
            for hh in range(8):
                o_ = (hh % 4) * 128
                MM(bf_[hh // 4], psb[bf_[hh // 4]][:, o_:o_ + 128], SC[:, 1, c, hh:hh + 1].broadcast_to([128, 128]), Umat,
                   True, False, [sct, "cst"])
                MM(bf_[hh // 4], psb[bf_[hh // 4]][:, o_:o_ + 128], identb[:], negub[:, 0:128], False, True, ["identb", "negub"])
                MM(bb_[hh // 4], psb[bb_[hh // 4]][:, o_:o_ + 128], SC[:, 1, c, 8 + hh:9 + hh].broadcast_to([128, 128]), Lmat,
                   True, False, [sct, "cst"])
                MM(bb_[hh // 4], psb[bb_[hh // 4]][:, o_:o_ + 128], identb[:], negub[:, 128:256], False, True, ["identb", "negub"])
            for hh in range(8):
                ACTF(REf[:, hh, :], psb[bf_[hh // 4]][:, (hh % 4) * 128:(hh % 4 + 1) * 128], AF.Exp,
                     [("ps", bf_[hh // 4]), sct], ["REf"], bias=SC[:, 2, c, hh:hh + 1], scale=1.0)
                ACTF(REb[:, hh, :], psb[bb_[hh // 4]][:, (hh % 4) * 128:(hh % 4 + 1) * 128], AF.Exp,
                     [("ps", bb_[hh // 4]), sct], ["REb"], bias=SC[:, 2, c, 8 + hh:9 + hh], scale=1.0)
            ps_release(*bf_)
            ps_release(*bb_)
            bg_ = ps_alloc(hold=True)
            for g in range(2):
                MM(bg_, psb[bg_][:, g * 128:(g + 1) * 128], BC[:, g, cs], BC[:, 2 + g, cs], True, True,
                   [("BC", g), ("BC", 2 + g)])
            TTO("dve", REf, REf, REb, ALU.add, ["REf", "REb"], ["REf"])
            for g in range(2):
                TTO("dve", Mc[:, 4 * g:4 * g + 4, :], REf[:, 4 * g:4 * g + 4, :],
                    psb[bg_][:, g * 128:(g + 1) * 128].unsqueeze(1).broadcast_to([128, 4, 128]), ALU.mult,
                    ["REf", ("ps", bg_)], [("Mb", c % 2)])
            ps_release(bg_)

        def stageQ(c):
            cs = slice(c * 128, (c + 1) * 128)
            Mc = Mbs[c % 2]
            if c == 8:
                MEMSET("dve", Hfc[:, :], 0.0, [], ["Hfc"])
            tok_major(c)
            ACTF(Hfb[:, :], Hfc[:, :], AF.Identity, ["Hfc"], ["Hfb"])
            bwf = ps_alloc(hold=True)
            bwb = ps_alloc(hold=True)
            for g in range(2):
                MM(bwf, psb[bwf][:, g * 256:(g + 1) * 256], BC[:, 2 + g, cs], Hfb[:, g * 256:(g + 1) * 256], True, True,
                   [("BC", 2 + g), "Hfb"])
                MM(bwb, psb[bwb][:, g * 256:(g + 1) * 256], BC[:, 2 + g, cs], Hb[:, c, g * 256:(g + 1) * 256], True, True,
                   [("BC", 2 + g), ("Hb", c)])
            TTO("dve", v3(tmp[0][:, 0:512]), v3(psb[bwf][:, 0:512]), bc8(SC[:, 4, c, 0:8]), ALU.mult,
                [("ps", bwf), sct], tmp0t)
            TTO("dve", v3(tmp[1][:, 0:512]), v3(psb[bwb][:, 0:512]), bc8(SC[:, 4, c, 8:16]), ALU.mult,
                [("ps", bwb), sct], tmp1t)
            ps_release(bwf, bwb)
            TTO("dve", Wtok[:, :], tmp[0][:, 0:512], tmp[1][:, 0:512], ALU.add, tmp0t + tmp1t, ["Wtok"])
            by = [ps_alloc(hold=True) for _ in range(2)]
            for m in range(4):
                for hh in range(2):
                    hd = 2 * m + hh
                    col = ((m % 2) * 2 + hh) * 128
                    bk = by[m // 2]
                    MM(bk, psb[bk][:, col:col + 128], Xt[:, m * 128:(m + 1) * 128], Mc[:, hd, :], True, False,
                       ["Xt", ("Mb", c % 2)])
                    MM(bk, psb[bk][:, col:col + 128], Wtok[:, m * 128:(m + 1) * 128], identb[:], False, True,
                       ["Wtok", "identb"])
            grp_s["bk"] = chunk_state(c, 0)
            return by

        def stageQ2(c, by):
            cs = slice(c * 128, (c + 1) * 128)
            yz = tmp[0][:, 0:512].rearrange("p (m i) -> p m i", m=4)
            for m in range(4):
                for hh in range(2):
                    r0 = hh * 64
                    col = ((m % 2) * 2 + hh) * 128
                    bk = by[m // 2]
                    STT(yz[r0:r0 + 64, m, :], XS[r0:r0 + 64, m, cs], ev(32 + m)[r0:r0 + 64, :],
                        psb[bk][r0:r0 + 64, col:col + 128], ALU.mult, ALU.add, [("XS", m), ("ps", bk), "dsk"], tmp0t)
            ps_release(*by)
            TTO("dve", yz, yz, Zb[:, :, cs], ALU.mult, tmp0t + [("Zb", m) for m in range(4)], tmp0t)
            sqv = sq[1][:, 0:512].rearrange("p (m i) -> p m i", m=4)
            ACTF(sq[1][:, 0:512], tmp[0][:, 0:512], AF.Square, tmp0t, [("sq", 1)])
            bn = ps_alloc(hold=True)
            for m in range(4):
                MM(bn, psb[bn][:, 0:128], ones_bf[:], sqv[:, m, :], m == 0, m == 3, [("sq", 1), "ones"])
            ACTF(tmp[1][:, 0:128], psb[bn][:, 0:128], AF.Ln, [("ps", bn), "eps"], tmp1t, bias=eps_t[:], scale=1.0 / 512)
            ps_release(bn)
            ACTF(tmp[1][:, 0:128], tmp[1][:, 0:128], AF.Exp, tmp1t, tmp1t, scale=-0.5)
            for m in range(4):
                STT(Zb[:, m, cs], yz[:, m, :], ev(40 + m), tmp[1][:, 0:128], ALU.mult, ALU.mult,
                    tmp0t + tmp1t + ["evv"], [("Zb", m)])
            bk = grp_s["bk"]
            TTO("dve", v3(Hfc[:, :]), v3(Hfc[:, :]), bc8(SC[:, 5, c, 0:8]), ALU.mult, ["Hfc", sct], ["Hfc"])
            TTO("dve", Hfc[:, :], Hfc[:, :], psb[bk][:, 0:512], ALU.add, ["Hfc", ("ps", bk)], ["Hfc"])
            ps_release(bk)
            if c % 2 == 1:
                out_evs.append(DMA("sp", ssdT_d[i, seq_of(c), 0], Hfc[:, :], ["Hfc"], [], "o_Hfc"))
                if c in (1, 3, 5):
                    TS("dve", Hfc[:, :], Hfc[:, :], flagA, None, ALU.mult, None, ["Hfc", "flags"], ["Hfc"])

        DMA("sp", Hfc[:, :], h0T_d[i][:, 0, :], [], ["Hfc"], "mH")
        PIPE = cfg.get('ev_pipe', False)
        if cfg.get('ev_p3', True) and PIPE:
            stageP(0)
        for c in (range(10) if cfg.get('ev_p3', True) else []):
            if PIPE:
                by_ = stageQ(c)
                if c + 1 < 10:
                    stageP(c + 1)
                stageQ2(c, by_)
            else:
                stageP(c)
                by_ = stageQ(c)
                stageQ2(c, by_)

        WO = WStream(weout_d[i], LAY_OUT)
        mreads = [("Ub", m) for m in range(4)] + [("Zb", m) for m in range(4)]
        rms_begin()
        for j in range(8):
            banks = proj(WO, j, 8, lambda kc, t0, tn: (Ub[:, kc, t0:t0 + tn] if kc < 4 else Zb[:, kc - 4, t0:t0 + tn]),
                         mreads, 128, TT)
            for ti, (t0, tn) in enumerate(TT):
                s_ = SLOT_OF_TT[ti]
                STT(x[:, j, t0:t0 + tn], psb[banks[ti]][:, 0:tn], G_ap(l, 1, j, s_), x[:, j, t0:t0 + tn],
                    ALU.mult, ALU.add, [("ps", banks[ti]), ("gate", l % 2, 1), ("x", j)], [("x", j)])
            ps_release(*banks)
            if j >= 1:
                rms_chunk(j - 1)
        rms_chunk(7)
        fence()

    bg_start(mod_layer_gen(0), 1)
    for l in range(nlayers):
        if l == 0:
            bg_ensure(0, 0)
            bg_limit[0] = 3
        else:
            bg_limit[0] = 3
        ffn(l, 0)
        if l % 2 == 1 and cfg.get("odd", True):
            odd_mixer(l)
        if l % 2 == 0 and cfg.get("even", True):
            even_mixer(l)
        bg_flush()
        if l + 1 < nlayers:
            bg_start(mod_layer_gen(l + 1), 2)
        ffn(l, 1)

    rms_stats()
    for kc in range(8):
        tb = tmp[kc % 2]
        S.op("dve", lambda e, kc=kc, tb=tb: e.scalar_tensor_tensor(
            out=tb[:, 0:NT], in0=x[:, kc, :], scalar=gfin[:, kc:kc + 1], in1=rstd[:], op0=ALU.mult, op1=ALU.mult),
            reads=[("x", kc), ("rstd", 0), ("rstd", 1), ("rstd", 2), "gfin"], writes=[("tmp", kc % 2, 0), ("tmp", kc % 2, 1)])
        ev = S.op("sp", lambda e, kc=kc, tb=tb: e.dma_start(out=yT_d[kc], in_=tb[:, 0:NT]),
                  reads=[("tmp", kc % 2, 0), ("tmp", kc % 2, 1)], dma="o_tmp%d" % (kc % 2))
        out_evs.append(ev)
    S.wait_all("sp", out_evs)

    S.emit(nc, stack)
    stack.close()
    return nc


def _core_tokens(core, x_prompt, x_sample):
    if core < 2:
        a = x_sample[core]
        b = x_prompt[core]
    else:
        p0 = 2 + (core - 2) * 5
        a = x_prompt[p0:p0 + 4].reshape(1024, D)
        b = x_prompt[p0 + 4]
    return np.concatenate([a, b], axis=0)


def _prep_shared(inp):
    f = np.float32
    sh = {}
    w_mod = np.asarray(inp["w_mod"], f)
    sh["wmod"] = np.ascontiguousarray(
        w_mod.reshape(DEPTH, 8, 128, 36, 2, 128).transpose(0, 3, 2, 4, 1, 5)).reshape(DEPTH, 36, 128, 2048)
    b_mod = np.asarray(inp["b_mod"], f)
    sh["bmod"] = np.ascontiguousarray(b_mod.reshape(DEPTH, 72, 128).transpose(2, 0, 1)).reshape(128, DEPTH * 72)
    g_norm = np.asarray(inp["g_norm"], f)
    sh["gnorm"] = np.ascontiguousarray(g_norm.reshape(DEPTH, 3, 8, 128).transpose(3, 0, 1, 2)).reshape(128, DEPTH * 24)
    sh["gfin"] = np.ascontiguousarray(np.asarray(inp["g_final"], f).reshape(8, 128).T)
    w_gu = np.asarray(inp["w_ff_gu"], f)
    sh["wgu"] = np.ascontiguousarray(
        w_gu.reshape(DEPTH, 2, 8, 128, 2, 11, 2, 128).transpose(0, 1, 5, 3, 4, 6, 2, 7)).reshape(DEPTH, 2, 11, 128, 4096)
    w_dn = np.asarray(inp["w_ff_down"], f)
    sh["wdn"] = np.ascontiguousarray(
        w_dn.reshape(DEPTH, 2, NFC, 128, 8, 128).transpose(0, 1, 4, 3, 2, 5)).reshape(DEPTH, 2, 8, 128, 2816)
    w_in_odd = np.asarray(inp["w_in_odd"], f)
    w_uq = np.asarray(inp["w_uq"], f)
    w_ukv = np.asarray(inp["w_ukv"], f)
    w_out_odd = np.asarray(inp["w_out_odd"], f)
    sh["woin"] = np.stack([_pack(w_in_odd[i], _odd_in_blocks()) for i in range(2)])
    sh["wuq"] = np.stack([_pack(w_uq[i], _uq_blocks()) for i in range(2)])
    sh["wukv"] = np.stack([_pack(w_ukv[i], _ukv_blocks()) for i in range(2)])
    sh["woout"] = np.stack([_pack(w_out_odd[i], _out_blocks()) for i in range(2)])
    ov = np.zeros((128, 2, NOV), f)
    for i in range(2):
        ov[:, i, 0:3] = np.asarray(inp["g_cq"], f)[i].reshape(3, 128).T
        ov[:, i, 3:5] = np.asarray(inp["g_ckv"], f)[i].reshape(2, 128).T
        wdw = np.asarray(inp["w_dwconv"], f)[i]
        ov[:, i, 5:129] = wdw.reshape(31, 4, 128).transpose(2, 1, 0).reshape(128, 124)
        ov[:, i, 129:133] = np.asarray(inp["b_dwconv"], f)[i].reshape(4, 128).T
        ov[:, i, 133:137] = np.asarray(inp["g_conv_ln"], f)[i].reshape(4, 128).T
        ov[:, i, 137:141] = np.asarray(inp["b_conv_ln"], f)[i].reshape(4, 128).T
    sh["oddv"] = np.ascontiguousarray(ov.reshape(128, 2 * NOV))
    w_in_even = np.asarray(inp["w_in_even"], f)
    w_out_even = np.asarray(inp["w_out_even"], f)
    sh["wein"] = np.stack([_pack(w_in_even[i], _even_in_blocks()) for i in range(2)])
    sh["weout"] = np.stack([_pack(w_out_even[i], _out_blocks()) for i in range(2)])
    evv = np.zeros((128, 2, NEV), f)
    evb = np.zeros((2, 128, NEB), f)
    for i in range(2):
        wc = np.asarray(inp["w_conv_ssm"], f)[i]
        evv[:, i, 0:24] = wc.reshape(3, 8, 128).transpose(2, 1, 0).reshape(128, 24)
        evv[:, i, 24:32] = np.asarray(inp["b_conv_ssm"], f)[i].reshape(8, 128).T
        dsk = np.asarray(inp["d_skip"], f)[i]
        for m in range(4):
            evv[0:64, i, 32 + m] = dsk[0, 2 * m]
            evv[64:128, i, 32 + m] = dsk[0, 2 * m + 1]
            evv[0:64, i, 36 + m] = dsk[1, 2 * m]
            evv[64:128, i, 36 + m] = dsk[1, 2 * m + 1]
        evv[:, i, 40:44] = np.asarray(inp["g_ssm_out"], f)[i].reshape(4, 128).T
        evb[i, :, 0:512] = np.asarray(inp["g_gmlp_v"], f)[i][None, :]
        evb[i, :, 512:1024] = np.asarray(inp["b_spatial"], f)[i].reshape(1, 512)
        evb[i, :, 1024:1040] = np.asarray(inp["dt_bias"], f)[i].reshape(1, 16)
        evb[i, :, 1040:1056] = np.asarray(inp["a_log"], f)[i].reshape(1, 16)
    sh["evv"] = np.ascontiguousarray(evv.reshape(128, 2 * NEV))
    sh["evb"] = evb
    ws = np.asarray(inp["w_spatial"], f)
    sh["wsT"] = np.ascontiguousarray(ws.transpose(0, 3, 1, 2)).reshape(2, 128, 512)
    jj = np.arange(128)[:, None]
    ii = np.arange(128)[None, :]
    U = (jj <= ii).astype(f)
    L = (jj >= ii).astype(f)
    cst = np.stack([U, L, NEG * (1 - U), NEG * (1 - L), np.eye(128, dtype=f), np.ones((128, 128), f)], axis=1)
    sh["cst"] = np.ascontiguousarray(cst.reshape(128, 6 * 128))
    return sh


def _rope_tables():
    f = np.float32
    t = np.arange(1024)
    row = (t // 64).astype(f)
    col = (t % 64).astype(f)
    freqs = (np.float32(10000.0) ** (-np.arange(8, dtype=f) / np.float32(8))).astype(f)
    ang = np.stack([row[:, None] * freqs, col[:, None] * freqs], axis=1)
    cos = np.cos(ang).astype(f)
    sin = np.sin(ang).astype(f)
    C = np.zeros((32, 1024), f)
    Sg = np.zeros((32, 1024), f)
    for a in range(2):
        for r in range(2):
            for q in range(8):
                d = a * 16 + r * 8 + q
                C[d] = cos[:, a, q]
                Sg[d] = (-sin[:, a, q]) if r == 0 else sin[:, a, q]
    return np.concatenate([C, C], 0), np.concatenate([Sg, Sg], 0)


def _prep_core(core, inp):
    f = np.float32
    m = {}
    toks = _core_tokens(core, np.asarray(inp["x_prompt"], f), np.asarray(inp["x_sample"], f))
    m["xT"] = np.ascontiguousarray(toks.T).reshape(8, 128, NT)
    c_ctx = np.asarray(inp["c_ctx"], f)
    cA = np.asarray(inp["c"], f)[core] if core < 2 else c_ctx
    cv = np.stack([cA, c_ctx], axis=-1)
    m["cvec"] = np.ascontiguousarray(cv.reshape(8, 128, 2).transpose(1, 0, 2)).reshape(128, 16)
    is_s = core < 2
    fl = np.zeros((128, 4), f)
    fl[:, 0] = 1.0 if is_s else 0.0
    fl[:, 1] = 0.0 if is_s else NEG
    m["flags"] = fl
    if is_s:
        C, Sg = _rope_tables()
        m["rope"] = np.ascontiguousarray(np.stack([C, Sg]))
        ck = np.asarray(inp["cache_mla_ckv"], f)[core]
        m["cckv"] = np.ascontiguousarray(ck.reshape(2, 256, 2, 128).transpose(0, 3, 2, 1))
        kr = np.asarray(inp["cache_mla_krope"], f)[core]
        krT = kr.transpose(0, 2, 1)
        m["ckr"] = np.ascontiguousarray(np.concatenate([krT, krT], axis=1))
        st = np.asarray(inp["state_ssd"], f)[core]
        m["h0T"] = np.ascontiguousarray(st.transpose(0, 4, 1, 2, 3)).reshape(2, 128, 2, 512)
    else:
        m["h0T"] = np.zeros((2, 128, 2, 512), f)
        m["rope"] = np.ascontiguousarray(np.stack([np.ones((64, 1024), f), np.zeros((64, 1024), f)]))
        m["cckv"] = np.zeros((2, 128, 2, 256), f)
        m["ckr"] = np.zeros((2, 64, 256), f)
    return m


_NC_CACHE = {}


def kernel(**inputs):
    if "nc" not in _NC_CACHE:
        _NC_CACHE["nc"] = build_program()
    nc = _NC_CACHE["nc"]
    shared = _prep_shared(inputs)
    in_maps = []
    for core in range(NCORES):
        m = dict(shared)
        m.update(_prep_core(core, inputs))
        in_maps.append(m)
    res = run_bass_kernel_spmd(nc, in_maps, core_ids=list(range(NCORES)))
    return _gather(res.results, inputs)


def _gather(results, inputs):
    f = np.float32
    y_prompt = np.zeros((32, 256, D), f)
    y_sample = np.zeros((2, 1024, D), f)
    for core in range(NCORES):
        y = results[core]["yT"].reshape(D, NT).T
        if core < 2:
            y_sample[core] = y[:1024]
            y_prompt[core] = y[1024:]
        else:
            p0 = 2 + (core - 2) * 5
            y_prompt[p0:p0 + 4] = y[:1024].reshape(4, 256, D)
            y_prompt[p0 + 4] = y[1024:]
    new_ssd = np.zeros((32, 2, 2, 8, 64, 128), f)
    new_ckv = np.zeros((32, 2, 256, 256), f)
    new_kr = np.zeros((32, 2, 256, 32), f)
    for core in range(NCORES):
        ck = results[core]["ckvT"].reshape(2, 256, NT).transpose(0, 2, 1)
        kr = results[core]["krT"].reshape(2, 32, NT).transpose(0, 2, 1)
        if core < 2:
            seqs = [(core, 1024)]
        else:
            p0 = 2 + (core - 2) * 5
            seqs = [(p0 + j, j * 256) for j in range(4)] + [(p0 + 4, 1024)]
        for (b, t0) in seqs:
            new_ckv[b] = ck[:, t0:t0 + 256]
            new_kr[b] = kr[:, t0:t0 + 256]
        sd = results[core]["ssdT"].reshape(2, 5, 2, 128, 8, 64)
        for (b, t0) in seqs:
            new_ssd[b] = sd[:, t0 // 256].transpose(0, 1, 3, 4, 2)
    return (y_prompt, y_sample, new_ssd, new_ckv, new_kr)
```

```python
from contextlib import ExitStack
import numpy as np
import concourse.bass as bass
import concourse.mybir as mybir
from concourse.bass_utils import run_bass_kernel_spmd

F32 = mybir.dt.float32
F32R = mybir.dt.float32r
BF16 = mybir.dt.bfloat16
AF = mybir.ActivationFunctionType
ALU = mybir.AluOpType

D = 1024
DEPTH = 4
NT = 1280
TT = [(0, 512), (512, 512), (1024, 256)]
SLOT_OF_TT = [0, 0, 1]
SLOTS = [(0, 1024), (1024, 256)]
DFF = 2816
NFC = 22
EPS = 1e-6
NCORES = 8
NWSLOT = 3
WSLOT = 4096


def _layout(KC, ns):
    lay = []
    t = 0
    off = 0
    for n in ns:
        sz = KC * n
        if off + sz > WSLOT:
            t += 1
            off = 0
        lay.append((t, off, KC, n))
        off += sz
    return lay, t + 1


def _pack(w, blocks):
    KC = w.shape[0] // 128
    lay, nt = _layout(KC, [len(b) for b in blocks])
    out = np.zeros((nt, 128, WSLOT), np.float32)
    for (t, off, _, n), cols in zip(lay, blocks):
        blk = w[:, cols].reshape(KC, 128, n).transpose(1, 0, 2).reshape(128, KC * n)
        out[t, :, off:off + KC * n] = blk
    return out


def _ar(a, b):
    return list(range(a, b))


def _rope_partner(cols):
    c = np.asarray(cols).reshape(2, 2, 8)
    return list(c[:, ::-1, :].reshape(-1))


def _odd_in_blocks():
    kr = _ar(640, 672)
    blocks = [_ar(0, 128), _ar(128, 256), _ar(256, 384), kr + kr, _rope_partner(kr) * 2,
              _ar(384, 512), _ar(512, 640)]
    for c in range(4):
        blocks.append(_ar(672 + c * 128, 672 + (c + 1) * 128))
        blocks.append(_ar(1184 + c * 128, 1184 + (c + 1) * 128))
    return blocks


def _uq_blocks():
    blocks = []
    for m in range(4):
        blocks.append(_ar(2 * m * 96, 2 * m * 96 + 64) + _ar((2 * m + 1) * 96, (2 * m + 1) * 96 + 64))
    for m in range(4):
        blocks.append(_ar(2 * m * 96 + 64, 2 * m * 96 + 96) + _ar((2 * m + 1) * 96 + 64, (2 * m + 1) * 96 + 96))
    for m in range(4):
        blocks.append(_rope_partner(_ar(2 * m * 96 + 64, 2 * m * 96 + 96))
                      + _rope_partner(_ar((2 * m + 1) * 96 + 64, (2 * m + 1) * 96 + 96)))
    return blocks


def _ukv_blocks():
    blocks = []
    for m in range(4):
        blocks.append(_ar(2 * m * 128, 2 * m * 128 + 64) + _ar((2 * m + 1) * 128, (2 * m + 1) * 128 + 64))
    v = []
    for hh in range(8):
        v += _ar(hh * 128 + 64, hh * 128 + 128)
    blocks.append(v)
    return blocks


def _out_blocks():
    return [_ar(j * 128, (j + 1) * 128) for j in range(8)]


LAY_OIN, NT_OIN = _layout(8, [len(b) for b in _odd_in_blocks()])
LAY_UQ, NT_UQ = _layout(3, [len(b) for b in _uq_blocks()])
LAY_UKV, NT_UKV = _layout(2, [len(b) for b in _ukv_blocks()])
LAY_OUT, NT_OUT = _layout(8, [128] * 8)
NOV = 3 + 2 + 124 + 4 + 4 + 4


def _even_in_blocks():
    blocks = [_ar(c * 128, (c + 1) * 128) for c in range(4)]
    blocks += [_ar(1024 + c * 128, 1024 + (c + 1) * 128) for c in range(4)]
    blocks += [_ar(1536 + c * 128, 1536 + (c + 1) * 128) for c in range(8)]
    blocks.append(_ar(512, 1024))
    blocks.append(_ar(2560, 2576))
    return blocks


LAY_EIN, NT_EIN = _layout(8, [len(b) for b in _even_in_blocks()])
NEV = 24 + 8 + 4 + 4 + 4
NEB = 512 + 512 + 16 + 16
NEG = -30000.0

class Instr:
    __slots__ = ("fn", "waits", "signal", "dma")

    def __init__(self, fn, dma=None):
        self.fn = fn
        self.waits = []
        self.signal = False
        self.dma = dma


class Sched:
    ENGS = ("pe", "act", "dve", "pool", "sp")
    STRICT = 0
    CAP = 3000

    def __init__(self):
        self.q = {e: [] for e in self.ENGS}
        self.clock = {e: {} for e in self.ENGS}
        self.evclock = {}
        self.tok = {}
        self.dma_cnt = {}
        self.total_keys = set()

    def op(self, eng, fn, reads=(), writes=(), dma=None):
        q = self.q[eng]
        idx = len(q)
        ins = Instr(fn, dma)
        deps = set()
        raw = set()
        for t in reads:
            w = self.tok.get(t)
            if w is not None and w[0] is not None:
                deps.add(w[0])
                raw.add(w[0])
        for t in writes:
            w = self.tok.get(t)
            if w is not None:
                if w[0] is not None:
                    deps.add(w[0])
                deps.update(w[1])
        ck = self.clock[eng]
        for ev in sorted(deps, key=lambda e: (e[0], str(e[1]), -e[2])):
            kind, a, b = ev
            if kind == "e" and a == eng and Sched.STRICT != 1:
                if eng == "pe":
                    continue
                if Sched.STRICT == 0:
                    if ev not in raw:
                        continue
                    if idx - b > 3:
                        continue
            key = (kind, a)
            if ck.get(key, -1) >= b:
                continue
            ins.waits.append(ev)
            for k, v in self.evclock[ev].items():
                if ck.get(k, -1) < v:
                    ck[k] = v
            if kind == "e":
                self.q[a][b].signal = True
        if dma is not None:
            n = self.dma_cnt.get(dma, 0) + 1
            self.dma_cnt[dma] = n
            ev = ("d", dma, n)
        else:
            ev = ("e", eng, idx)
        ck2 = dict(ck)
        ck2[(ev[0], ev[1])] = ev[2]
        if dma is None:
            ck2[("e", eng)] = idx
        self.evclock[ev] = ck2
        q.append(ins)
        for t in reads:
            w = self.tok.setdefault(t, [None, []])
            w[1].append(ev)
        for t in writes:
            self.tok[t] = [ev, []]
        return ev

    def wait_all(self, eng, evs):
        ins = Instr(None)
        for ev in evs:
            ins.waits.append(ev)
            if ev[0] == "e":
                self.q[ev[1]][ev[2]].signal = True
        self.q[eng].append(ins)

    def emit(self, nc, stack):
        CAP = Sched.CAP
        sig = {}
        nsig = {}
        for e in self.ENGS:
            c = 0
            arr = []
            for ins in self.q[e]:
                if ins.signal and ins.dma is None:
                    c += 1
                arr.append(c)
            sig[e] = arr
            nsig[e] = c
        esem = {e: [stack.enter_context(nc.semaphore("s_%s%d" % (e, k))) for k in range(nsig[e] // CAP + 1)]
                for e in self.ENGS}
        dsem = {k: stack.enter_context(nc.semaphore("d_%s" % (k,))) for k in self.dma_cnt}
        for k, v in self.dma_cnt.items():
            assert 16 * v < 4000, (k, v)
        block = stack.enter_context(nc.Block())
        q = self.q
        total = self.total_keys
        dma_cnt = self.dma_cnt

        def run(ename, eng):
            for i, ins in enumerate(q[ename]):
                for (kind, a, b) in ins.waits:
                    if kind == "e":
                        c = sig[a][b]
                        eng.wait_ge(esem[a][(c - 1) // CAP], (c - 1) % CAP + 1)
                    else:
                        n = dma_cnt[a] if a in total else b
                        eng.wait_ge(dsem[a], 16 * n)
                if ins.fn is None:
                    continue
                r = ins.fn(eng)
                if ins.dma is not None:
                    r.then_inc(dsem[ins.dma], 16)
                elif ins.signal:
                    c = sig[ename][i]
                    r.then_inc(esem[ename][(c - 1) // CAP], 1)

        @block.tensor
        def _(e):
            run("pe", e)

        @block.scalar
        def _(e):
            run("act", e)

        @block.vector
        def _(e):
            run("dve", e)

        @block.gpsimd
        def _(e):
            run("pool", e)

        @block.sync
        def _(e):
            run("sp", e)


def build_program(cfg=None):
    cfg = cfg or {}
    nlayers = cfg.get("nlayers", DEPTH)
    do_mixer = cfg.get("mixer", True)
    nc = bass.Bass("TRN2", target_bir_lowering=False)
    S = Sched()
    stack = ExitStack()
    dram = {}

    def din(name, shape, dt=F32):
        dram[name] = nc.dram_tensor(name, list(shape), dt, kind="ExternalInput").ap()
        return dram[name]

    def dout(name, shape, dt=F32):
        dram[name] = nc.dram_tensor(name, list(shape), dt, kind="ExternalOutput").ap()
        return dram[name]

    def sb(name, shape, dt=F32):
        return stack.enter_context(nc.sbuf_tensor(name, list(shape), dt))

    xT_d = din("xT", [8, 128, NT])
    cvec_d = din("cvec", [128, 16])
    wmod_d = din("wmod", [DEPTH, 36, 128, 2048])
    bmod_d = din("bmod", [128, DEPTH * 72])
    gnorm_d = din("gnorm", [128, DEPTH * 3 * 8])
    gfin_d = din("gfin", [128, 8])
    wgu_d = din("wgu", [DEPTH, 2, 11, 128, 4096])
    wdn_d = din("wdn", [DEPTH, 2, 8, 128, 2816])
    yT_d = dout("yT", [8, 128, NT])
    flags_d = din("flags", [128, 4])
    woin_d = din("woin", [2, NT_OIN, 128, WSLOT])
    wuq_d = din("wuq", [2, NT_UQ, 128, WSLOT])
    wukv_d = din("wukv", [2, NT_UKV, 128, WSLOT])
    woout_d = din("woout", [2, NT_OUT, 128, WSLOT])
    oddv_d = din("oddv", [128, 2 * NOV])
    rope_d = din("rope", [2, 64, 1024])
    cckv_d = din("cckv", [2, 128, 2, 256])
    ckr_d = din("ckr", [2, 64, 256])
    ckvT_d = dout("ckvT", [2, 2, 128, NT])
    wein_d = din("wein", [2, NT_EIN, 128, WSLOT])
    weout_d = din("weout", [2, NT_OUT, 128, WSLOT])
    evv_d = din("evv", [128, 2 * NEV])
    evb_d = din("evb", [2, 128, NEB])
    wsT_d = din("wsT", [2, 128, 512])
    cst_d = din("cst", [128, 6 * 128])
    h0T_d = din("h0T", [2, 128, 2, 512])
    ssdT_d = dout("ssdT", [2, 5, 2, 128, 512])
    krT_d = dout("krT", [2, 32, NT])

    x = sb("x", [128, 8, NT], F32)
    h = sb("h", [128, 8, NT], BF16)
    big = sb("big", [128, 14336], F32)
    act = big[:, 0:14080].bitcast(BF16).rearrange("p (j t) -> p j t", j=NFC)
    wbuf = [sb("wbuf%d" % i, [128, WSLOT], BF16) for i in range(NWSLOT)]
    rstd = sb("rstd", [128, NT], F32)
    sq = [sb("sq%d" % i, [128, NT], BF16) for i in range(2)]
    tmp = [sb("tmp%d" % i, [128, 1440], F32) for i in range(2)]
    sg = [sb("sg%d" % i, [128, NT], BF16) for i in range(2)]
    ones_bf = sb("ones_bf", [128, 128], BF16)
    cvec = sb("cvec_sb", [128, 16], F32)
    csil = sb("csil", [128, 16], BF16)
    bmod = sb("bmod_sb", [128, DEPTH * 72], F32)
    gnorm = sb("gnorm_sb", [128, DEPTH * 24], F32)
    gfin = sb("gfin_sb", [128, 8], F32)
    modv = [sb("modv%d" % i, [128, 144], F32) for i in range(2)]
    amul = [sb("amul%d" % i, [128, 48], F32) for i in range(2)]
    gate = [sb("gate%d" % i, [128, 48], F32) for i in range(2)]
    eps_t = sb("eps_t", [128, 1], F32)
    flags = sb("flags_sb", [128, 4], F32)
    oddv = sb("oddv_sb", [128, 2 * NOV], F32)
    rstd2 = sb("rstd2", [128, NT], F32)
    ext = sb("ext", [128, 6400], F32)
    evv = sb("evv_sb", [128, 2 * NEV], F32)
    cst = sb("cst_sb", [128, 6 * 128], F32)
    identb = sb("identb", [128, 128], BF16)
    one_t = sb("one_t", [128, 1], F32)
    negub = sb("negub", [128, 256], BF16)
    aneg = sb("aneg", [128, 16], F32)
    s160 = [sb("s160_%d" % i, [128, 160], F32) for i in range(3)]
    sml = sb("sml", [128, 4], F32)
    fence_t = sb("fence_t", [128, 2], F32)
    psb = [stack.enter_context(nc.psum_tensor("ps%d" % i, [128, 512], F32)) for i in range(8)]

    ps_rr = [0]

    ps_held = set()

    def ps_alloc(hold=False):
        while True:
            b = ps_rr[0] % 8
            ps_rr[0] += 1
            if b not in ps_held:
                break
        if hold:
            ps_held.add(b)
        return b

    def ps_release(*bs):
        for b in bs:
            ps_held.discard(b)

    w_rr = [0]

    def load_w(src_ap, n):
        slot = w_rr[0] % NWSLOT
        w_rr[0] += 1
        S.op("pool", lambda e, slot=slot, src_ap=src_ap, n=n: e.dma_start(out=wbuf[slot][:, 0:n], in_=src_ap),
             writes=[("w", slot)], dma="w%d" % slot)
        return slot

    S.op("sp", lambda e: e.dma_start(out=cvec[:], in_=cvec_d), writes=["cvec"], dma="in")
    S.op("sp", lambda e: e.dma_start(out=bmod[:], in_=bmod_d), writes=["bmod"], dma="in")
    S.op("sp", lambda e: e.dma_start(out=gnorm[:], in_=gnorm_d), writes=["gnorm"], dma="in")
    S.op("sp", lambda e: e.dma_start(out=gfin[:], in_=gfin_d), writes=["gfin"], dma="in")
    for kc in range(8):
        S.op("sp", lambda e, kc=kc: e.dma_start(out=x[:, kc, :], in_=xT_d[kc]), writes=[("x", kc)], dma="in")
    S.op("sp", lambda e: e.dma_start(out=flags[:], in_=flags_d), writes=["flags"], dma="in")
    S.op("sp", lambda e: e.dma_start(out=oddv[:], in_=oddv_d), writes=["oddv"], dma="in")
    S.op("sp", lambda e: e.dma_start(out=evv[:], in_=evv_d), writes=["evv"], dma="in")
    S.op("sp", lambda e: e.dma_start(out=cst[:], in_=cst_d), writes=["cst"], dma="in")
    S.total_keys.add("in")
    S.op("dve", lambda e: e.memset(ones_bf[:], 1.0), writes=["ones"])
    S.op("dve", lambda e: e.memset(eps_t[:], EPS), writes=["eps"])
    S.op("dve", lambda e: e.memset(one_t[:], 1.0), writes=["one"])
    S.op("act", lambda e: e.activation(out=identb[:], in_=cst[:, 4 * 128:5 * 128], func=AF.Identity),
         reads=["cst"], writes=["identb"])
    S.op("act", lambda e: e.activation(out=negub[:], in_=cst[:, 2 * 128:4 * 128], func=AF.Identity),
         reads=["cst"], writes=["negub"])
    S.op("act", lambda e: e.activation(out=csil[:], in_=cvec[:], func=AF.Silu), reads=["cvec"], writes=["csil"])

    wm_rr = [0]
    bg = [None]

    bg_subs = [0]
    bg_limit = [0]

    def bg_start(gen, limit):
        bg[0] = gen
        bg_subs[0] = 0
        bg_limit[0] = limit

    def bg_step(n=1):
        for _ in range(n):
            if bg[0] is None or bg_subs[0] >= bg_limit[0]:
                return
            try:
                if next(bg[0]) == "sub":
                    bg_subs[0] += 1
                    if bg_subs[0] >= 3:
                        bg[0] = None
            except StopIteration:
                bg[0] = None

    def bg_flush():
        bg_limit[0] = 3
        while bg[0] is not None:
            bg_step(1)

    bg_done = set()

    def bg_ensure(l, sub):
        bg_limit[0] = max(bg_limit[0], sub + 1)
        while (l, sub) not in bg_done and bg[0] is not None:
            bg_step(1)

    def mod_layer_gen(l):
        mv = modv[l % 2]
        am = amul[l % 2]
        gt = gate[l % 2]
        mv3 = mv[:, :].rearrange("p (j s) -> p j s", s=2)
        NMS = 6
        for sub in range(3):
            pb = ps_alloc(hold=True)
            for t in range(12 * sub, 12 * (sub + 1)):
                slot = wm_rr[0] % NMS
                wm_rr[0] += 1
                wsl = ext[:, slot * 1024:(slot + 1) * 1024].bitcast(BF16)
                S.op("pool", lambda e, wsl=wsl, t=t: e.dma_start(out=wsl, in_=wmod_d[l, t]),
                     reads=["bigown"], writes=[("wm", slot)], dma="wm%d_%d" % (slot, l))
                for fcl in range(2):
                    j = 2 * t + fcl
                    for kc in range(8):
                        o = (fcl * 8 + kc) * 128
                        S.op("pe", lambda e, wsl=wsl, j=j, kc=kc, o=o, pb=pb: e.matmul(
                            psb[pb][:, 2 * j:2 * j + 2], lhsT=wsl[:, o:o + 128],
                            rhs=csil[:, 2 * kc:2 * kc + 2], start=(kc == 0), stop=(kc == 7)),
                            reads=[("wm", slot), "csil", "bigown"], writes=[("ps", pb)])
                yield
            j0, j1 = 24 * sub, 24 * (sub + 1)
            bm = bmod[:, l * 72 + j0:l * 72 + j1]
            for s in range(2):
                S.op("dve", lambda e, s=s, pb=pb, mv=mv, bm=bm, j0=j0, j1=j1: e.tensor_tensor(
                    out=mv[:, 2 * j0:2 * j1].rearrange("p (j s) -> p s j", s=2)[:, s, :],
                    in0=psb[pb][:, 2 * j0:2 * j1].rearrange("p (j s) -> p s j", s=2)[:, s, :],
                    in1=bm, op=ALU.add),
                    reads=[("ps", pb), "bmod"], writes=[("modv", l % 2, sub)])
            g_ap = gnorm[:, (l * 3 + sub) * 8:(l * 3 + sub + 1) * 8]
            for s in range(2):
                a_out = am[:, sub * 16:(sub + 1) * 16].rearrange("p (k s) -> p k s", s=2)[:, :, s]
                g_out = gt[:, sub * 16:(sub + 1) * 16].rearrange("p (k s) -> p k s", s=2)[:, :, s]
                sc_in = mv3[:, (3 * sub + 1) * 8:(3 * sub + 2) * 8, s]
                gt_in = mv3[:, (3 * sub + 2) * 8:(3 * sub + 3) * 8, s]
                S.op("dve", lambda e, a_out=a_out, sc_in=sc_in, g_ap=g_ap: e.scalar_tensor_tensor(
                    out=a_out, in0=sc_in, scalar=1.0, in1=g_ap, op0=ALU.add, op1=ALU.mult),
                    reads=[("modv", l % 2, sub), "gnorm"], writes=[("amul", l % 2, sub)])
                gs = 1.0 if sub == 1 else 0.5
                S.op("dve", lambda e, g_out=g_out, gt_in=gt_in, gs=gs: e.tensor_scalar(
                    out=g_out, in0=gt_in, scalar1=gs, scalar2=None, op0=ALU.mult),
                    reads=[("modv", l % 2, sub)], writes=[("gate", l % 2, sub)])
            bg_done.add((l, sub))
            ps_release(pb)
            yield "sub"

    def A_ap(l, sub, kc, s):
        o = sub * 16 + kc * 2 + s
        return amul[l % 2][:, o:o + 1]

    def G_ap(l, sub, kc, s):
        o = sub * 16 + kc * 2 + s
        return gate[l % 2][:, o:o + 1]

    def B_ap(l, sub, kc, s):
        o = ((3 * sub) * 8 + kc) * 2 + s
        return modv[l % 2][:, o:o + 1]

    pend = {"banks": None, "n": 0}

    def rms_begin():
        pend["banks"] = [ps_alloc(hold=True) for _ in TT]
        pend["n"] = 0

    def rms_chunk(kc):
        banks = pend["banks"]
        n = pend["n"]
        pend["n"] = n + 1
        sqb = sq[kc % 2]
        S.op("act", lambda e: e.activation(out=sqb[:], in_=x[:, kc, :], func=AF.Square),
             reads=[("x", kc)], writes=[("sq", kc % 2)])
        for ti, (t0, tn) in enumerate(TT):
            S.op("pe", lambda e, b=banks[ti], t0=t0, tn=tn: e.matmul(
                psb[b][:, 0:tn], lhsT=ones_bf[:], rhs=sqb[:, t0:t0 + tn], start=(n == 0), stop=(n == 7)),
                reads=[("sq", kc % 2), "ones"], writes=[("ps", banks[ti])])

    def rms_stats():
        if pend["banks"] is None:
            rms_begin()
            for kc in range(8):
                rms_chunk(kc)
        assert pend["n"] == 8
        banks = pend["banks"]
        for ti, (t0, tn) in enumerate(TT):
            S.op("act", lambda e, b=banks[ti], t0=t0, tn=tn: e.activation(
                out=rstd[:, t0:t0 + tn], in_=psb[b][:, 0:tn], func=AF.Ln, bias=eps_t[:], scale=1.0 / D),
                reads=[("ps", banks[ti]), "eps"], writes=[("rstd", ti)])
            S.op("act", lambda e, t0=t0, tn=tn: e.activation(
                out=rstd[:, t0:t0 + tn], in_=rstd[:, t0:t0 + tn], func=AF.Exp, scale=-0.5),
                reads=[("rstd", ti)], writes=[("rstd", ti)])
        ps_release(*banks)
        pend["banks"] = None

    def norm_mod(l, sub):
        rms_stats()
        for kc in range(8):
            tb = tmp[kc % 2]
            for s, (c0, cn) in enumerate(SLOTS):
                S.op("dve", lambda e, kc=kc, s=s, c0=c0, cn=cn, tb=tb: e.scalar_tensor_tensor(
                    out=tb[:, c0:c0 + cn], in0=x[:, kc, c0:c0 + cn], scalar=A_ap(l, sub, kc, s),
                    in1=rstd[:, c0:c0 + cn], op0=ALU.mult, op1=ALU.mult),
                    reads=[("x", kc), ("rstd", 0), ("rstd", 1), ("rstd", 2), ("amul", l % 2, sub)],
                    writes=[("tmp", kc % 2, s)])
                S.op("act", lambda e, kc=kc, s=s, c0=c0, cn=cn, tb=tb: e.activation(
                    out=h[:, kc, c0:c0 + cn], in_=tb[:, c0:c0 + cn], func=AF.Identity,
                    bias=B_ap(l, sub, kc, s), scale=1.0),
                    reads=[("tmp", kc % 2, s), ("modv", l % 2, sub)], writes=[("h", kc)])

    def ffn(l, si):
        sub = 0 if si == 0 else 2
        if pend["banks"] is None:
            rms_begin()
            for kc_ in range(8):
                rms_chunk(kc_)
        bg_ensure(l, sub)
        norm_mod(l, sub)
        hreads = [("h", kc) for kc in range(8)]
        for t in range(11):
            bg_step(3)
            slot = load_w(wgu_d[l, si, t], 4096)
            for fcl in range(2):
                j = 2 * t + fcl
                gb = [ps_alloc() for _ in TT]
                for kc in range(8):
                    off = ((0 * 2 + fcl) * 8 + kc) * 128
                    for ti, (t0, tn) in enumerate(TT):
                        S.op("pe", lambda e, b=gb[ti], slot=slot, off=off, kc=kc, t0=t0, tn=tn: e.matmul(
                            psb[b][:, 0:tn], lhsT=wbuf[slot][:, off:off + 128], rhs=h[:, kc, t0:t0 + tn],
                            start=(kc == 0), stop=(kc == 7)),
                            reads=[("w", slot), ("h", kc)], writes=[("ps", gb[ti])])
                sgb = sg[j % 2]
                for ti, (t0, tn) in enumerate(TT):
                    S.op("act", lambda e, b=gb[ti], t0=t0, tn=tn, sgb=sgb: e.activation(
                        out=sgb[:, t0:t0 + tn], in_=psb[b][:, 0:tn], func=AF.Silu),
                        reads=[("ps", gb[ti])], writes=[("sg", j % 2, ti)])
                ub = [ps_alloc() for _ in TT]
                for kc in range(8):
                    off = ((1 * 2 + fcl) * 8 + kc) * 128
                    for ti, (t0, tn) in enumerate(TT):
                        S.op("pe", lambda e, b=ub[ti], slot=slot, off=off, kc=kc, t0=t0, tn=tn: e.matmul(
                            psb[b][:, 0:tn], lhsT=wbuf[slot][:, off:off + 128], rhs=h[:, kc, t0:t0 + tn],
                            start=(kc == 0), stop=(kc == 7)),
                            reads=[("w", slot), ("h", kc)], writes=[("ps", ub[ti])])
                for ti, (t0, tn) in enumerate(TT):
                    S.op("dve", lambda e, b=ub[ti], t0=t0, tn=tn, sgb=sgb, j=j: e.tensor_tensor(
                        out=act[:, j, t0:t0 + tn], in0=psb[b][:, 0:tn], in1=sgb[:, t0:t0 + tn], op=ALU.mult),
                        reads=[("ps", ub[ti]), ("sg", j % 2, ti), "bigown"], writes=[("act", j)])
        areads = [("act", j) for j in range(NFC)] + ["bigown"]
        rms_begin()
        for j in range(8):
            slot = load_w(wdn_d[l, si, j], 2816)
            ob = [ps_alloc() for _ in TT]
            for kc in range(NFC):
                for ti, (t0, tn) in enumerate(TT):
                    S.op("pe", lambda e, b=ob[ti], slot=slot, kc=kc, t0=t0, tn=tn: e.matmul(
                        psb[b][:, 0:tn], lhsT=wbuf[slot][:, kc * 128:(kc + 1) * 128], rhs=act[:, kc, t0:t0 + tn],
                        start=(kc == 0), stop=(kc == NFC - 1)),
                        reads=[("w", slot)] + areads, writes=[("ps", ob[ti])])
            for ti, (t0, tn) in enumerate(TT):
                s = SLOT_OF_TT[ti]
                S.op("dve", lambda e, b=ob[ti], t0=t0, tn=tn, j=j, s=s: e.scalar_tensor_tensor(
                    out=x[:, j, t0:t0 + tn], in0=psb[b][:, 0:tn], scalar=G_ap(l, sub, j, s),
                    in1=x[:, j, t0:t0 + tn], op0=ALU.mult, op1=ALU.add),
                    reads=[("ps", ob[ti]), ("gate", l % 2, sub), ("x", j)], writes=[("x", j)])
            if j >= 1:
                rms_chunk(j - 1)
        rms_chunk(7)


    in_mixer = [False]

    def _r(reads):
        return list(reads) + (["bigown"] if in_mixer[0] else [])

    def MM(bank, out_ap, lhsT, rhs, start, stop, reads):
        reads = _r(reads)
        S.op("pe", lambda e: e.matmul(out_ap, lhsT=lhsT, rhs=rhs, start=start, stop=stop),
             reads=reads, writes=[("ps", bank)])

    def ACTF(out, in_, func, reads, writes, bias=None, scale=None):
        kw = {}
        if bias is not None:
            kw["bias"] = bias
        if scale is not None:
            kw["scale"] = scale
        reads = _r(reads)
        return S.op("act", lambda e: e.activation(out=out, in_=in_, func=func, **kw), reads=reads, writes=writes)

    def TTO(eng, out, in0, in1, op, reads, writes):
        reads = _r(reads)
        return S.op(eng, lambda e: e.tensor_tensor(out=out, in0=in0, in1=in1, op=op), reads=reads, writes=writes)

    def STT(out, in0, scalar, in1, op0, op1, reads, writes):
        reads = _r(reads)
        return S.op("dve", lambda e: e.scalar_tensor_tensor(out=out, in0=in0, scalar=scalar, in1=in1, op0=op0, op1=op1),
                    reads=reads, writes=writes)

    def TS(eng, out, in0, s1, s2, op0, op1, reads, writes):
        reads = _r(reads)
        if op1 is None:
            return S.op(eng, lambda e: e.tensor_scalar(out=out, in0=in0, scalar1=s1, scalar2=None, op0=op0),
                        reads=reads, writes=writes)
        return S.op(eng, lambda e: e.tensor_scalar(out=out, in0=in0, scalar1=s1, scalar2=s2, op0=op0, op1=op1),
                    reads=reads, writes=writes)

    def RECIP(out, in_, reads, writes):
        reads = _r(reads)
        return S.op("dve", lambda e: e.reciprocal(out=out, in_=in_), reads=reads, writes=writes)

    def MEMSET(eng, ap, val, reads, writes):
        reads = _r(reads)
        return S.op(eng, lambda e: e.memset(ap, val), reads=reads, writes=writes)

    def DMA(eng, out, in_, reads, writes, key):
        reads = _r(reads)
        return S.op(eng, lambda e: e.dma_start(out=out, in_=in_), reads=reads, writes=writes, dma=key)

    class WStream:
        def __init__(self, dram_ap, lay):
            self.d = dram_ap
            self.lay = lay
            self.loaded = {}
            self.used = {}
            for (t, off, KC, n) in lay:
                self.used[t] = max(self.used.get(t, 0), off + KC * n)

        def blk(self, b, kc):
            t, off, KC, n = self.lay[b]
            if t not in self.loaded:
                u = self.used[t]
                self.loaded[t] = load_w(self.d[t][:, 0:u], u)
            slot = self.loaded[t]
            return slot, wbuf[slot][:, off + kc * n:off + (kc + 1) * n]

    def proj(ws, b, KC, rhs_fn, rreads, M, tiles):
        banks = [ps_alloc(hold=True) for _ in tiles]
        for kc in range(KC):
            slot, lw = ws.blk(b, kc)
            for ti, (t0, tn) in enumerate(tiles):
                rr = rreads(kc) if callable(rreads) else rreads
                MM(banks[ti], psb[banks[ti]][0:M, 0:tn], lw, rhs_fn(kc, t0, tn), kc == 0, kc == KC - 1,
                   [("w", slot)] + rr)
        bg_step(1)
        return banks

    out_evs = []
    fence_n = [0]

    def fence():
        k = fence_n[0] % 2
        fence_n[0] += 1
        in_mixer[0] = False
        MEMSET("dve", fence_t[:, k:k + 1], 0.0, [], ["bigown"])

    def gated_out(ws, l, hreads):
        rms_begin()
        for j in range(8):
            banks = proj(ws, j, 8, lambda kc, t0, tn: h[:, kc, t0:t0 + tn], hreads, 128, TT)
            for ti, (t0, tn) in enumerate(TT):
                s_ = SLOT_OF_TT[ti]
                STT(x[:, j, t0:t0 + tn], psb[banks[ti]][:, 0:tn], G_ap(l, 1, j, s_), x[:, j, t0:t0 + tn],
                    ALU.mult, ALU.add, [("ps", banks[ti]), ("gate", l % 2, 1), ("x", j)], [("x", j)])
            ps_release(*banks)
            if j >= 1:
                rms_chunk(j - 1)
        rms_chunk(7)

    def odd_mixer(l):
        i = l // 2

        def ov(o, n=1):
            return oddv[:, i * NOV + o:i * NOV + o + n]

        bg_flush()
        norm_mod(l, 1)
        fence()
        in_mixer[0] = True
        BO = ["bigown"]
        hreads = lambda kc: [("h", kc)]
        hall = [("h", kc) for kc in range(8)]
        off = [0]

        def carve(nw):
            r = big[:, off[0]:off[0] + nw]
            off[0] += nw
            return r

        ckvn = carve(1536).bitcast(BF16).rearrange("p (c t) -> p c t", c=2)
        QN = carve(2560).bitcast(BF16).rearrange("p (m t) -> p m t", m=4)
        QR = carve(2560).bitcast(BF16).rearrange("p (m t) -> p m t", m=4)
        KN = carve(3072).bitcast(BF16).rearrange("p (m t) -> p m t", m=4)
        Vt = carve(3072).bitcast(BF16).rearrange("p (k f) -> p k f", k=12)
        ropeC = carve(512).bitcast(BF16)
        ropeS = carve(512).bitcast(BF16)
        eoff = [0]

        def ecarve(nw):
            r = ext[:, eoff[0]:eoff[0] + nw]
            eoff[0] += nw
            return r

        cqg = ecarve(1920).bitcast(BF16).rearrange("p (c t) -> p c t", c=3)
        KR2 = ecarve(768).bitcast(BF16)
        cpadB = ecarve(2860).bitcast(BF16).rearrange("p (c t) -> p c t", c=4)
        rcb = rstd2[:, 0:512]
        dgr = ecarve(256).bitcast(BF16).rearrange("p (r i) -> p r i", r=4)
        PT = [ecarve(256).bitcast(BF16) for _ in range(2)]
        tmp0t = [("tmp", 0, 0), ("tmp", 0, 1)]
        tmp1t = [("tmp", 1, 0), ("tmp", 1, 1)]
        tmpt = [tmp0t, tmp1t]
        rst = [("rstd", 0), ("rstd", 1), ("rstd", 2)]

        DMA("pool", ckvn[:, :, 1280:1536], cckv_d[i], BO, ["ckvn_c"], "mA")
        DMA("pool", KR2[0:64, 1280:1536], ckr_d[i], [], ["KR2_c"], "mB")
        DMA("pool", ropeC[0:64, :], rope_d[0], BO, ["ropeC"], "mC")
        DMA("pool", ropeS[0:64, :], rope_d[1], BO, ["ropeS"], "mD")

        WI = WStream(woin_d[i], LAY_OIN)

        def hrhs(kc, t0, tn):
            return h[:, kc, t0:t0 + tn]

        def rms_finish(statb, dst, scale, toks):
            for ti, (t0, tn) in enumerate(TT):
                ACTF(dst[:, t0:t0 + tn], psb[statb[ti]][:, 0:tn], AF.Ln, [("ps", statb[ti]), "eps"], [toks[ti]],
                     bias=eps_t[:], scale=scale)
                ACTF(dst[:, t0:t0 + tn], dst[:, t0:t0 + tn], AF.Exp, [toks[ti]], [toks[ti]], scale=-0.5)
            ps_release(*statb)

        statb = [ps_alloc(hold=True) for _ in TT]
        for c in range(3):
            banks = proj(WI, c, 8, hrhs, hreads, 128, TT)
            for ti, (t0, tn) in enumerate(TT):
                ACTF(cqg[:, c, t0:t0 + tn], psb[banks[ti]][:, 0:tn], AF.Identity, [("ps", banks[ti]), "oddv"],
                     [("cqg", c)], scale=ov(c))
                ACTF(sq[c % 2][:, t0:t0 + tn], psb[banks[ti]][:, 0:tn], AF.Square, [("ps", banks[ti])],
                     [("sq", c % 2)])
            ps_release(*banks)
            for ti, (t0, tn) in enumerate(TT):
                MM(statb[ti], psb[statb[ti]][:, 0:tn], ones_bf[:], sq[c % 2][:, t0:t0 + tn], c == 0, c == 2,
                   [("sq", c % 2), "ones"])
        rms_finish(statb, rstd, 1.0 / 384, rst)

        banks = proj(WI, 3, 8, hrhs, hreads, 64, TT)
        for ti, (t0, tn) in enumerate(TT):
            ACTF(tmp[0][0:64, t0:t0 + tn], psb[banks[ti]][0:64, 0:tn], AF.Identity, [("ps", banks[ti])], tmp0t)
        ps_release(*banks)
        out_evs.append(DMA("sp", krT_d[i], tmp[0][0:32, 0:NT], tmp0t, [], "o_tmp0"))
        banks = proj(WI, 4, 8, hrhs, hreads, 64, TT[0:2])
        for ti, (t0, tn) in enumerate(TT[0:2]):
            TTO("dve", tmp[1][0:64, t0:t0 + tn], tmp[0][0:64, t0:t0 + tn], ropeC[0:64, t0:t0 + tn], ALU.mult,
                tmp0t + ["ropeC"] + BO, tmp1t)
            TTO("dve", rstd2[0:64, t0:t0 + tn], psb[banks[ti]][0:64, 0:tn], ropeS[0:64, t0:t0 + tn], ALU.mult,
                [("ps", banks[ti]), "ropeS"] + BO, ["rstd2"])
            TTO("dve", KR2[0:64, t0:t0 + tn], tmp[1][0:64, t0:t0 + tn], rstd2[0:64, t0:t0 + tn], ALU.add,
                tmp1t + ["rstd2"], ["KR2"])
        ps_release(*banks)
        ACTF(KR2[0:64, 1024:1280], tmp[0][0:64, 1024:1280], AF.Identity, tmp0t, ["KR2"])

        statb = [ps_alloc(hold=True) for _ in TT]
        for c in range(2):
            banks = proj(WI, 5 + c, 8, hrhs, hreads, 128, TT)
            for ti, (t0, tn) in enumerate(TT):
                ACTF(tmp[c][:, t0:t0 + tn], psb[banks[ti]][:, 0:tn], AF.Identity, [("ps", banks[ti]), "oddv"],
                     tmpt[c], scale=ov(3 + c))
                ACTF(sq[c % 2][:, t0:t0 + tn], psb[banks[ti]][:, 0:tn], AF.Square, [("ps", banks[ti])],
                     [("sq", c % 2)])
            ps_release(*banks)
            for ti, (t0, tn) in enumerate(TT):
                MM(statb[ti], psb[statb[ti]][:, 0:tn], ones_bf[:], sq[c % 2][:, t0:t0 + tn], c == 0, c == 1,
                   [("sq", c % 2), "ones"])
        rms_finish(statb, rstd2, 1.0 / 256, ["rstd2", "rstd2", "rstd2"])
        for c in range(2):
            TTO("dve", tmp[c][:, 0:NT], tmp[c][:, 0:NT], rstd2[:, :], ALU.mult, tmpt[c] + ["rstd2"], tmpt[c])
            out_evs.append(DMA("sp", ckvT_d[i, c], tmp[c][:, 0:NT], tmpt[c], [], "o_tmp%d" % c))
            ACTF(ckvn[:, c, 0:NT], tmp[c][:, 0:NT], AF.Identity, tmpt[c] + BO, ["ckvn"])

        def cp5(c):
            return cpadB[:, c, :].rearrange("p (s w) -> p s w", s=5)

        for c in range(4):
            MEMSET("dve", cp5(c)[:, :, 0:15], 0.0, [], [("cpad", c)])
            MEMSET("dve", cp5(c)[:, :, 271:286], 0.0, [], [("cpad", c)])
            ba = proj(WI, 7 + 2 * c, 8, hrhs, hreads, 128, TT)
            bg = proj(WI, 8 + 2 * c, 8, hrhs, hreads, 128, TT)
            segs = [(0, 2), (2, 4), (4, 5)]
            for ti, (t0, tn) in enumerate(TT):
                ACTF(sg[c % 2][:, t0:t0 + tn], psb[bg[ti]][:, 0:tn], AF.Sigmoid, [("ps", bg[ti])], [("sg", c % 2, ti)])
                s0, s1 = segs[ti]
                TTO("dve", cp5(c)[:, s0:s1, 15:271], psb[ba[ti]][:, 0:tn].rearrange("p (s w) -> p s w", w=256),
                    sg[c % 2][:, t0:t0 + tn].rearrange("p (s w) -> p s w", w=256), ALU.mult,
                    [("ps", ba[ti]), ("sg", c % 2, ti)], [("cpad", c)])
            ps_release(*ba)
            ps_release(*bg)
            TS("dve", cp5(c)[:, 1:4, 0:15], cp5(c)[:, 0:3, 256:271], flags[:, 0:1], None, ALU.mult, None,
               [("cpad", c), "flags"], [("cpad", c)])
            TS("dve", cp5(c)[:, 0:3, 271:286], cp5(c)[:, 1:4, 15:30], flags[:, 0:1], None, ALU.mult, None,
               [("cpad", c), "flags"], [("cpad", c)])

        WQ = WStream(wuq_d[i], LAY_UQ)

        def qrhs(kc, t0, tn):
            return cqg[:, kc, t0:t0 + tn]

        cqr = [("cqg", c) for c in range(3)]
        for m in range(4):
            banks = proj(WQ, m, 3, qrhs, cqr, 128, TT)
            for ti, (t0, tn) in enumerate(TT):
                TTO("dve", QN[:, m, t0:t0 + tn], psb[banks[ti]][:, 0:tn], rstd[:, t0:t0 + tn], ALU.mult,
                    [("ps", banks[ti]), rst[ti]] + BO, ["QN"])
            ps_release(*banks)
        for m in range(4):
            bq = proj(WQ, 4 + m, 3, qrhs, cqr, 64, TT)
            bp = proj(WQ, 8 + m, 3, qrhs, cqr, 64, TT[0:2])
            for ti, (t0, tn) in enumerate(TT[0:2]):
                TTO("dve", tmp[0][0:64, t0:t0 + tn], psb[bq[ti]][0:64, 0:tn], ropeC[0:64, t0:t0 + tn], ALU.mult,
                    [("ps", bq[ti]), "ropeC"] + BO, tmp0t)
                TTO("dve", tmp[1][0:64, t0:t0 + tn], psb[bp[ti]][0:64, 0:tn], ropeS[0:64, t0:t0 + tn], ALU.mult,
                    [("ps", bp[ti]), "ropeS"] + BO, tmp1t)
                TTO("dve", tmp[0][0:64, t0:t0 + tn], tmp[0][0:64, t0:t0 + tn], tmp[1][0:64, t0:t0 + tn], ALU.add,
                    tmp0t + tmp1t, tmp0t)
                TTO("dve", QR[0:64, m, t0:t0 + tn], tmp[0][0:64, t0:t0 + tn], rstd[0:64, t0:t0 + tn], ALU.mult,
                    tmp0t + [rst[ti]] + BO, ["QR"])
            t0, tn = TT[2]
            TTO("dve", QR[0:64, m, t0:t0 + tn], psb[bq[2]][0:64, 0:tn], rstd[0:64, t0:t0 + tn], ALU.mult,
                [("ps", bq[2]), rst[2]] + BO, ["QR"])
            ps_release(*bq)
            ps_release(*bp)

        WK = WStream(wukv_d[i], LAY_UKV)
        KT = [(0, 512), (512, 512), (1024, 512)]
        ckr_ = ["ckvn", "ckvn_c"] + BO

        def krhs(kc, t0, tn):
            return ckvn[:, kc, t0:t0 + tn]

        for m in range(4):
            banks = proj(WK, m, 2, krhs, ckr_, 128, KT)
            for ti, (t0, tn) in enumerate(KT):
                ACTF(KN[:, m, t0:t0 + tn], psb[banks[ti]][:, 0:tn], AF.Identity, [("ps", banks[ti])] + BO, ["KN"])
            ps_release(*banks)
        for kb in range(12):
            bk = ps_alloc(hold=True)
            for kc in range(2):
                slot, rw = WK.blk(4, kc)
                MM(bk, psb[bk][:, 0:512], ckvn[:, kc, kb * 128:(kb + 1) * 128], rw, kc == 0, kc == 1,
                   [("w", slot)] + ckr_)
            if kb % 2 == 0:
                ACTF(Vt[:, kb, :], psb[bk][:, 0:512], AF.Identity, [("ps", bk)] + BO, ["Vt"])
            else:
                S.op("dve", lambda e, kb=kb, bk=bk: e.tensor_copy(out=Vt[:, kb, :], in_=psb[bk][:, 0:512]),
                     reads=[("ps", bk)] + BO, writes=["Vt"])
            ps_release(bk)

        dbuf = [sq[0], sq[1], sg[0], sg[1]]
        dtok = [[("sq", 0)], [("sq", 1)], [("sg", 0, t_) for t_ in range(3)], [("sg", 1, t_) for t_ in range(3)]]
        conv_ops = []

        def mk_conv(c):
            accf = tmp[c % 2][:, 0:NT]
            acc = accf.rearrange("p (s w) -> p s w", s=5)
            c5 = cp5(c)
            tt_ = tmpt[c % 2]
            conv_ops.append(lambda: TS("dve", acc, c5[:, :, 0:256], ov(5 + c * 31), ov(129 + c), ALU.mult, ALU.add,
                                       [("cpad", c), "oddv"], tt_))
            for k in range(1, 31):
                conv_ops.append(lambda k=k: STT(acc, c5[:, :, k:k + 256], ov(5 + c * 31 + k), acc, ALU.mult, ALU.add,
                                                [("cpad", c), "oddv"] + tt_, tt_))
            conv_ops.append(lambda: S.op("pool", lambda e: e.tensor_copy(out=dbuf[c][:, :], in_=accf),
                                         reads=_r(tt_), writes=dtok[c]))

        for c in range(4):
            mk_conv(c)
        conv_pos = [0]

        def conv_some(n):
            for _ in range(n):
                if conv_pos[0] < len(conv_ops):
                    conv_ops[conv_pos[0]]()
                    conv_pos[0] += 1

        sm_scale = 96.0 ** -0.5
        zb, nb_ = flags[:, 2:3], flags[:, 1:2]
        PT4 = PT + [carve(256).bitcast(BF16) for _ in range(2)]
        steps = []
        for m in range(4):
            for (qc, qn, kbs) in [(0, 512, list(range(8)) + [10, 11]), (512, 512, list(range(8)) + [10, 11]),
                                  (1024, 256, [8, 9])]:
                for ki, kb in enumerate(kbs):
                    halves = []
                    for hc in range(0, qn, 256):
                        qseq = (qc + hc) // 256
                        halves.append(zb if (kb // 2) == qseq else nb_)
                    steps.append((m, qc, qn, kb, ki == 0, ki == len(kbs) - 1, halves))
        sbk_of = {}
        grp = {}

        def stageS(k):
            m, qc, qn, kb, first, last, halves = steps[k]
            k0 = kb * 128
            sb2 = [ps_alloc(hold=True) for _ in range(2)]
            sbk_of[k] = sb2
            for hh in range(2):
                r0 = hh * 64
                MM(sb2[hh], psb[sb2[hh]][:, 0:qn], KN[r0:r0 + 64, m, k0:k0 + 128], QN[r0:r0 + 64, m, qc:qc + qn],
                   True, False, ["KN", "QN"])
            for hh in range(2):
                q0 = hh * 32
                MM(sb2[hh], psb[sb2[hh]][:, 0:qn], KR2[q0:q0 + 32, k0:k0 + 128], QR[q0:q0 + 32, m, qc:qc + qn],
                   False, True, ["KR2", "KR2_c", "QR"])

        def stageE(k):
            m, qc, qn, kb, first, last, halves = steps[k]
            sb2 = sbk_of.pop(k)
            for hh in range(2):
                pi = (k % 2) * 2 + hh
                pt = PT4[pi]
                ptt = ("PT", pi)
                if len(halves) == 1 or halves[0] is halves[1]:
                    ACTF(pt[:, 0:qn], psb[sb2[hh]][:, 0:qn], AF.Exp, [("ps", sb2[hh]), "flags"], [ptt], bias=halves[0],
                         scale=sm_scale)
                else:
                    for hi, hb in enumerate(halves):
                        ACTF(pt[:, hi * 256:(hi + 1) * 256], psb[sb2[hh]][:, hi * 256:(hi + 1) * 256], AF.Exp,
                             [("ps", sb2[hh]), "flags"], [ptt], bias=hb, scale=sm_scale)
            ps_release(*sb2)

        def stagePV(k):
            m, qc, qn, kb, first, last, halves = steps[k]
            if first:
                grp["ob"] = [ps_alloc(hold=True) for _ in range(2)]
                grp["smb"] = [ps_alloc(hold=True) for _ in range(2)]
            for hh in range(2):
                r0 = hh * 64
                ob, smb = grp["ob"][hh], grp["smb"][hh]
                pi = (k % 2) * 2 + hh
                pt = PT4[pi]
                ptt = ("PT", pi)
                MM(ob, psb[ob][:, 0:qn], Vt[:, kb, m * 128:(m + 1) * 128], pt[:, 0:qn], first, last, ["Vt", ptt])
                MM(smb, psb[smb][:, 0:qn], ones_bf[:], pt[:, 0:qn], first, last, [ptt, "ones"])
                if last:
                    ACTF(rcb[r0:r0 + 64, 0:qn], psb[smb][r0:r0 + 64, 0:qn], AF.Ln, [("ps", smb)], ["rstd2"])
                    ACTF(rcb[r0:r0 + 64, 0:qn], rcb[r0:r0 + 64, 0:qn], AF.Exp, ["rstd2"], ["rstd2"], scale=-1.0)
                    TTO("dve", h[r0:r0 + 64, m, qc:qc + qn], psb[ob][r0:r0 + 64, 0:qn], rcb[r0:r0 + 64, 0:qn], ALU.mult,
                        [("ps", ob), "rstd2"], [("h", m)])
            if last:
                ps_release(*grp["ob"])
                ps_release(*grp["smb"])

        stageS(0)
        for k in range(len(steps)):
            if k + 1 < len(steps):
                stageS(k + 1)
            stageE(k)
            stagePV(k)
            conv_some(2 if k % 2 == 0 else 1)
        conv_some(len(conv_ops))

        s1 = [ps_alloc(hold=True) for _ in TT]
        s2 = [ps_alloc(hold=True) for _ in TT]
        for c in range(4):
            dc = dbuf[c]
            sqc = cpadB[:, c, 0:NT]
            ACTF(sqc, dc[:, :], AF.Square, dtok[c], [("cpad", c)])
            for ti, (t0, tn) in enumerate(TT):
                MM(s1[ti], psb[s1[ti]][:, 0:tn], ones_bf[:], dc[:, t0:t0 + tn], c == 0, c == 3, dtok[c] + ["ones"])
                MM(s2[ti], psb[s2[ti]][:, 0:tn], ones_bf[:], sqc[:, t0:t0 + tn], c == 0, c == 3,
                   [("cpad", c), "ones"])
        for ti, (t0, tn) in enumerate(TT):
            ACTF(rstd[:, t0:t0 + tn], psb[s1[ti]][:, 0:tn], AF.Identity, [("ps", s1[ti])], [rst[ti]], scale=1.0 / 512)
            ACTF(tmp[0][:, t0:t0 + tn], psb[s1[ti]][:, 0:tn], AF.Square, [("ps", s1[ti])], tmp0t, scale=1.0 / 512)
            STT(tmp[0][:, t0:t0 + tn], psb[s2[ti]][:, 0:tn], 1.0 / 512, tmp[0][:, t0:t0 + tn], ALU.mult, ALU.subtract,
                [("ps", s2[ti])] + tmp0t, tmp0t)
            ACTF(rstd2[:, t0:t0 + tn], tmp[0][:, t0:t0 + tn], AF.Ln, tmp0t + ["eps"], ["rstd2"], bias=eps_t[:], scale=1.0)
            ACTF(rstd2[:, t0:t0 + tn], rstd2[:, t0:t0 + tn], AF.Exp, ["rstd2"], ["rstd2"], scale=-0.5)
        ps_release(*s1)
        ps_release(*s2)
        for c in range(4):
            dc = dbuf[c][:, :]
            TTO("dve", tmp[1][:, 0:NT], dc, rstd[:, :], ALU.subtract, dtok[c] + rst, tmp1t)
            TTO("dve", tmp[1][:, 0:NT], tmp[1][:, 0:NT], rstd2[:, :], ALU.mult, tmp1t + ["rstd2"], tmp1t)
            ACTF(h[:, 4 + c, :], tmp[1][:, 0:NT], AF.Silu, tmp1t + ["oddv"], [("h", 4 + c)],
                 bias=ov(137 + c), scale=ov(133 + c))

        WO = WStream(woout_d[i], LAY_OUT)
        gated_out(WO, l, hreads)
        fence()


    Umat = cst[:, 0:128]
    Lmat = cst[:, 128:256]
    NEGU = cst[:, 256:384]
    NEGL = cst[:, 384:512]
    identf = cst[:, 512:640]
    onesf = cst[:, 640:768]

    def even_mixer(l):
        i = l // 2

        def ev(o, n=1):
            return evv[:, i * NEV + o:i * NEV + o + n]

        bg_flush()
        norm_mod(l, 1)
        fence()
        in_mixer[0] = True
        hreads = lambda kc: [("h", kc)]
        hall = [("h", kc) for kc in range(8)]
        off = [0]

        def carve(nw):
            r = big[:, off[0]:off[0] + nw]
            off[0] += nw
            return r

        Ub = carve(2560).bitcast(BF16).rearrange("p (m t) -> p m t", m=4)
        Zb = carve(2560).bitcast(BF16).rearrange("p (m t) -> p m t", m=4)
        XS = carve(2560).bitcast(BF16).rearrange("p (m t) -> p m t", m=4)
        BC = carve(2560).bitcast(BF16).rearrange("p (m t) -> p m t", m=4)
        Hb = carve(2560).bitcast(BF16).rearrange("p (c f) -> p c f", c=10)
        SC = carve(960).rearrange("p (k c f) -> p k c f", k=6, c=10)
        Mb_b = carve(512).bitcast(BF16).rearrange("p (h i) -> p h i", h=8)
        eoff = [0]

        def ecarve(nw):
            r = ext[:, eoff[0]:eoff[0] + nw]
            eoff[0] += nw
            return r

        evb = ecarve(NEB)
        WsT = ecarve(256).bitcast(BF16).rearrange("p (g i) -> p g i", g=4)
        REf = ecarve(1024).rearrange("p (h i) -> p h i", h=8)
        REb = ecarve(1024).rearrange("p (h i) -> p h i", h=8)
        Mb = ecarve(512).bitcast(BF16).rearrange("p (h i) -> p h i", h=8)
        Xt = ecarve(256).bitcast(BF16)
        Btok = ecarve(128).bitcast(BF16)
        Xw = ecarve(256).bitcast(BF16)
        vtm = ecarve(256).bitcast(BF16)
        Wtok = ecarve(256).bitcast(BF16)
        Hfc = ecarve(512)
        Hbc = ecarve(512)
        Hfb = ecarve(256).bitcast(BF16)
        gv_bc = evb[:, 0:512]
        bs_bc = evb[:, 512:1024]
        dtb_bc = evb[:, 1024:1040]
        alog_bc = evb[:, 1040:1056]
        flagA = flags[:, 0:1]
        tmp0t = [("tmp", 0, 0), ("tmp", 0, 1)]
        tmp1t = [("tmp", 1, 0), ("tmp", 1, 1)]
        tmpt = [tmp0t, tmp1t]

        DMA("sp", evb, evb_d[i], [], ["evb"], "mE")
        DMA("pool", WsT, wsT_d[i].rearrange("p (g i) -> p g i", g=4), [], ["WsT"], "mF")
        ACTF(aneg[:], alog_bc, AF.Exp, ["evb"], ["aneg"])
        TS("dve", aneg[:], aneg[:], -1.0, None, ALU.mult, None, ["aneg"], ["aneg"])

        TTO("dve", ev(32, 4), ev(32, 4), ev(36, 4), ALU.add, ["evv"], ["dsk"])
        WE = WStream(wein_d[i], LAY_EIN)

        def hrhs(kc, t0, tn):
            return h[:, kc, t0:t0 + tn]

        for c in range(4):
            banks = proj(WE, c, 8, hrhs, hreads, 128, TT)
            for ti, (t0, tn) in enumerate(TT):
                ACTF(Ub[:, c, t0:t0 + tn], psb[banks[ti]][:, 0:tn], AF.Gelu_apprx_tanh, [("ps", banks[ti])], [("Ub", c)])
            ps_release(*banks)
        for c in range(4):
            banks = proj(WE, 4 + c, 8, hrhs, hreads, 128, TT)
            for ti, (t0, tn) in enumerate(TT):
                ACTF(Zb[:, c, t0:t0 + tn], psb[banks[ti]][:, 0:tn], AF.Silu, [("ps", banks[ti])], [("Zb", c)])
            ps_release(*banks)
        segs = [(0, 2), (2, 4), (4, 5)]
        for c in range(8):
            xp = tmp[c % 2][:, 0:1290].rearrange("p (s w) -> p s w", s=5)
            tt_ = tmpt[c % 2]
            MEMSET("dve", xp[:, :, 0:1], 0.0, [], tt_)
            MEMSET("dve", xp[:, :, 257:258], 0.0, [], tt_)
            banks = proj(WE, 8 + c, 8, hrhs, hreads, 128, TT)
            for ti, (t0, tn) in enumerate(TT):
                s0, s1 = segs[ti]
                ACTF(xp[:, s0:s1, 1:257], psb[banks[ti]][:, 0:tn].rearrange("p (s w) -> p s w", w=256), AF.Identity,
                     [("ps", banks[ti])], tt_)
            ps_release(*banks)
            TS("dve", xp[:, 1:4, 0:1], xp[:, 0:3, 256:257], flagA, None, ALU.mult, None, tt_ + ["flags"], tt_)
            TS("dve", xp[:, 0:3, 257:258], xp[:, 1:4, 1:2], flagA, None, ALU.mult, None, tt_ + ["flags"], tt_)
            accb = [rstd2, rstd][c % 2]
            acct = [["rstd2"], [("rstd", 0), ("rstd", 1), ("rstd", 2)]][c % 2]
            acc = accb[:, :].rearrange("p (s w) -> p s w", s=5)
            TS("dve", acc, xp[:, :, 0:256], ev(c * 3 + 0), ev(24 + c), ALU.mult, ALU.add, tt_ + ["evv"], acct)
            STT(acc, xp[:, :, 1:257], ev(c * 3 + 1), acc, ALU.mult, ALU.add, tt_ + ["evv"] + acct, acct)
            STT(acc, xp[:, :, 2:258], ev(c * 3 + 2), acc, ALU.mult, ALU.add, tt_ + ["evv"] + acct, acct)
            if c < 4:
                dst, dtok = XS[:, c, :], ("XS", c)
            else:
                dst, dtok = BC[:, c - 4, :], ("BC", c - 4)
            ACTF(dst, accb[:, :], AF.Silu, acct, [dtok])

        def bc8(ap8):
            return ap8.unsqueeze(2).broadcast_to([128, 8, 64])

        def v3(ap):
            return ap.rearrange("p (h q) -> p h q", h=8)

        def tok_major(c):
            cs = slice(c * 128, (c + 1) * 128)
            bk = ps_alloc(hold=True)
            pb = psb[bk][:, :].bitcast(BF16)
            for m in range(4):
                S.op("pe", lambda e, m=m: e.transpose(pb[:, m * 128:(m + 1) * 128], XS[:, m, cs], identb[:]),
                     reads=_r([("XS", m), "identb"]), writes=[("ps", bk)])
            for g in range(2):
                S.op("pe", lambda e, g=g: e.transpose(pb[:, 512 + g * 128:512 + (g + 1) * 128], BC[:, g, cs], identb[:]),
                     reads=_r([("BC", g), "identb"]), writes=[("ps", bk)])
            ACTF(Xt[:, :], pb[:, 0:512], AF.Identity, [("ps", bk)], ["Xt"])
            S.op("dve", lambda e: e.tensor_copy(out=Btok[:, :], in_=pb[:, 512:768]), reads=_r([("ps", bk)]), writes=["Btok"])
            ps_release(bk)

        def seq_of(c):
            return c // 2

        sct = "SC"

        def f160(ap3):
            return ap3.rearrange("p c f -> p (c f)")

        def v160(ap2):
            return ap2.rearrange("p (c f) -> p c f", f=16)

        P0 = cfg.get('ev_p0', 2)
        if P0 >= 1:
            bkd = ps_alloc(hold=True)
            for c in range(10):
                cs = slice(c * 128, (c + 1) * 128)
                for kc in range(8):
                    slot, rw = WE.blk(17, kc)
                    MM(bkd, psb[bkd][:, c * 16:(c + 1) * 16], h[:, kc, cs], rw, kc == 0, kc == 7, [("w", slot), ("h", kc)])
            TTO("dve", v160(s160[0][:, :]), v160(psb[bkd][:, 0:160]), dtb_bc.unsqueeze(1).broadcast_to([128, 10, 16]), ALU.add,
                [("ps", bkd), "evb"], [("s160", 0)])
            ps_release(bkd)
            ACTF(s160[0][:, :], s160[0][:, :], AF.Exp, [("s160", 0)], [("s160", 0)])
            ACTF(f160(SC[:, 0, :, :]), s160[0][:, :], AF.Ln, [("s160", 0), "one"], [sct], bias=one_t[:], scale=1.0)
            ACTF(s160[1][:, :], f160(SC[:, 0, :, :]), AF.Ln, [sct], [("s160", 1)])
            TTO("dve", SC[:, 1, :, :], SC[:, 0, :, :], aneg[:, :].unsqueeze(1).broadcast_to([128, 10, 16]), ALU.mult,
                [sct, "aneg"], [sct])
        if P0 >= 2:
            bkc = ps_alloc(hold=True)
            for c in range(10):
                MM(bkc, psb[bkc][:, c * 32:c * 32 + 8], Umat, SC[:, 1, c, 0:8], True, True, [sct, "cst"])
                MM(bkc, psb[bkc][:, c * 32 + 8:c * 32 + 16], Lmat, SC[:, 1, c, 8:16], True, True, [sct, "cst"])
                MM(bkc, psb[bkc][:, c * 32 + 16:c * 32 + 32], onesf, SC[:, 1, c, :], True, True, [sct, "cst"])
            if P0 == 3:
                ps_release(bkc)
            else:
                ACTF(tmp[0][:, 0:320], psb[bkc][:, 0:320], AF.Identity, [("ps", bkc)], tmp0t)
                ps_release(bkc)
                pc = tmp[0][:, 0:320].rearrange("p (c f) -> p c f", f=32)
                cumv = pc[:, :, 0:16]
                totv = pc[:, :, 16:32]
                TTO("dve", SC[:, 2, :, :], v160(s160[1][:, :]), cumv, ALU.subtract, [("s160", 1)] + tmp0t, [sct])
                ACTF(SC[:, 4, :, :], cumv, AF.Exp, tmp0t, [sct])
                ACTF(SC[:, 5, :, :], totv, AF.Exp, tmp0t, [sct])
                TTO("dve", v160(s160[0][:, :]), totv, cumv, ALU.subtract, tmp0t, [("s160", 0)])
                ACTF(s160[0][:, :], s160[0][:, :], AF.Exp, [("s160", 0)], [("s160", 0)])
                TTO("dve", SC[:, 3, :, :], v160(s160[0][:, :]), SC[:, 0, :, :], ALU.mult, [("s160", 0), sct], [sct])

        def chunk_state(c, d):
            TTO("dve", v3(Xw[:, :]), v3(Xt[:, :]), bc8(SC[:, 3, c, d * 8:(d + 1) * 8]), ALU.mult, ["Xt", sct], ["Xw"])
            bk = ps_alloc(hold=True)
            for g in range(2):
                MM(bk, psb[bk][:, g * 256:(g + 1) * 256], Btok[:, g * 128:(g + 1) * 128], Xw[:, g * 256:(g + 1) * 256],
                   True, True, ["Btok", "Xw"])
            return bk

        for c in ([9, 8, 7, 6, 5, 4, 3, 2, 1, 0] if cfg.get('ev_p1', True) else []):
            cs = slice(c * 128, (c + 1) * 128)
            par = c % 2
            tpt = tmpt[par]
            if c == 9:
                MEMSET("dve", Hbc[:, :], 0.0, [], ["Hbc"])
            if c == 7:
                DMA("sp", Hbc[:, :], h0T_d[i][:, 1, :], [], ["Hbc"], "mG")
            bk = ps_alloc(hold=True)
            for kc in range(8):
                slot, rw = WE.blk(16, kc)
                MM(bk, psb[bk][:, 0:512], h[:, kc, cs], rw, kc == 0, kc == 7, [("w", slot), ("h", kc)])
            ACTF(tmp[par][:, 0:512], psb[bk][:, 0:512], AF.Gelu_apprx_tanh, [("ps", bk)], tpt)
            ps_release(bk)
            S.op("act", lambda e, par=par: e.activation(out=sq[0][:, 0:512], in_=tmp[par][:, 0:512], func=AF.Square,
                                                        accum_out=sml[:, 2 * par:2 * par + 1]),
                 reads=_r(tpt), writes=[("sq", 0), ("sml", par)])
            ACTF(sml[:, 2 * par + 1:2 * par + 2], sml[:, 2 * par:2 * par + 1], AF.Ln, [("sml", par), "eps"], [("sml1", par)],
                 bias=eps_t[:], scale=1.0 / 512)
            ACTF(sml[:, 2 * par + 1:2 * par + 2], sml[:, 2 * par + 1:2 * par + 2], AF.Exp, [("sml1", par)], [("sml1", par)],
                 scale=-0.5)
            STT(vtm[:, :], tmp[par][:, 0:512], sml[:, 2 * par + 1:2 * par + 2], gv_bc, ALU.mult, ALU.mult,
                tpt + [("sml1", par), "evb"], ["vtm"])
            tok_major(c)
            bk = chunk_state(c, 1)
            ACTF(Hb[:, c, :], Hbc[:, :], AF.Identity, ["Hbc"], [("Hb", c)])
            TTO("dve", v3(Hbc[:, :]), v3(Hbc[:, :]), bc8(SC[:, 5, c, 8:16]), ALU.mult, ["Hbc", sct], ["Hbc"])
            TTO("dve", Hbc[:, :], Hbc[:, :], psb[bk][:, 0:512], ALU.add, ["Hbc", ("ps", bk)], ["Hbc"])
            ps_release(bk)
            bk = ps_alloc(hold=True)
            for g in range(4):
                MM(bk, psb[bk][:, g * 128:(g + 1) * 128], vtm[:, g * 128:(g + 1) * 128], WsT[:, g, :], True, True,
                   ["vtm", "WsT"])
            TTO("dve", tmp[par][:, 512:1024], psb[bk][:, 0:512], bs_bc, ALU.add, [("ps", bk), "evb"], tpt)
            ps_release(bk)
            TTO("dve", Ub[:, :, cs], tmp[par][:, 512:1024].rearrange("p (g i) -> p g i", g=4), Ub[:, :, cs], ALU.mult,
                tpt + [("Ub", m) for m in range(4)], [("Ub", m) for m in range(4)])
            if c % 2 == 0:
                out_evs.append(DMA("sp", ssdT_d[i, seq_of(c), 1], Hbc[:, :], ["Hbc"], [], "o_Hbc"))
                if c in (2, 4, 6):
                    TS("dve", Hbc[:, :], Hbc[:, :], flagA, None, ALU.mult, None, ["Hbc", "flags"], ["Hbc"])

        Mbs = [Mb, Mb_b]
        grp_s = {}

        def stageP(c):
            cs = slice(c * 128, (c + 1) * 128)
            Mc = Mbs[c % 2]
            bf_ = [ps_alloc(hold=True) for _ in range(2)]
            bb_ = [ps_alloc(hold=True) for _ in range(2)]
            for hh in range(8):
                o_ = (hh % 4) * 128
                MM(bf_[hh // 4], psb[bf_[hh // 4]][:, o_:o_ + 128], SC[:, 1, c, hh:hh + 1].broadcast_to([128, 128]), Umat,
                   True, False, [sct, "cst"])
                MM(bf_[hh // 4], psb[bf_[hh // 4]][:, o_:o_ + 128], identb[:], negub[:, 0:128], False, True, ["identb", "negub"])
                MM(bb_[hh // 4], psb[bb_[hh // 4]][:, o_:o_ + 128], SC[:, 1, c, 8 + hh:9 + hh].broadcast_to([128, 128]), Lmat,
                   True, False, [sct, "cst"])
                MM(bb_[hh // 4], psb[bb_[hh // 4]][:, o_:o_ + 128], identb[:], negub[:, 128:256], False, True, ["identb", "negub"])
            for hh in range(8):
                ACTF(REf[:, hh, :], psb[bf_[hh // 4]][:, (hh % 4) * 128:(hh % 4 + 1) * 128], AF.Exp,
                     [("ps", bf_[hh // 4]), sct], ["REf"], bias=SC[:, 2, c, hh:hh + 1], scale=1.0)
                ACTF(REb[:, hh, :], psb[bb_[hh // 4]][:, (hh % 4) * 128:(hh % 4 + 1) * 128], AF.Exp,
                     [("ps", bb_[hh // 4]), sct], ["REb"], bias=SC[:, 2, c, 8 + hh:9 + hh], scale=1.0)
            ps_release(*bf_)
            ps_release(*bb_)
            bg_ = ps_alloc(hold=True)
            for g in range(2):
                MM(bg_, psb[bg_][:, g * 128:(g + 1) * 128], BC[:, g, cs], BC[:, 2 + g, cs], True, True,
                   [("BC", g), ("BC", 2 + g)])
            TTO("dve", REf, REf, REb, ALU.add, ["REf", "REb"], ["REf"])
            for g in range(2):
                TTO("dve", Mc[:, 4 * g:4 * g + 4, :], REf[:, 4 * g:4 * g + 4, :],
                    psb[bg_][:, g * 128:(g + 1) * 128].unsqueeze(1).broadcast_to([128, 4, 128]), ALU.mult,
                    ["REf", ("ps", bg_)], [("Mb", c % 2)])
            ps_release(bg_)

        def stageQ(c):
            cs = slice(c * 128, (c + 1) * 128)
            Mc = Mbs[c % 2]
            if c == 8:
                MEMSET("dve", Hfc[:, :], 0.0, [], ["Hfc"])
            tok_major(c)
            ACTF(Hfb[:, :], Hfc[:, :], AF.Identity, ["Hfc"], ["Hfb"])
            bwf = ps_alloc(hold=True)
            bwb = ps_alloc(hold=True)
            for g in range(2):
                MM(bwf, psb[bwf][:, g * 256:(g + 1) * 256], BC[:, 2 + g, cs], Hfb[:, g * 256:(g + 1) * 256], True, True,
                   [("BC", 2 + g), "Hfb"])
                MM(bwb, psb[bwb][:, g * 256:(g + 1) * 256], BC[:, 2 + g, cs], Hb[:, c, g * 256:(g + 1) * 256], True, True,
                   [("BC", 2 + g), ("Hb", c)])
            TTO("dve", v3(tmp[0][:, 0:512]), v3(psb[bwf][:, 0:512]), bc8(SC[:, 4, c, 0:8]), ALU.mult,
                [("ps", bwf), sct], tmp0t)
            TTO("dve", v3(tmp[1][:, 0:512]), v3(psb[bwb][:, 0:512]), bc8(SC[:, 4, c, 8:16]), ALU.mult,
                [("ps", bwb), sct], tmp1t)
            ps_release(bwf, bwb)
            TTO("dve", Wtok[:, :], tmp[0][:, 0:512], tmp[1][:, 0:512], ALU.add, tmp0t + tmp1t, ["Wtok"])
            by = [ps_alloc(hold=True) for _ in range(2)]
            for m in range(4):
                for hh in range(2):
                    hd = 2 * m + hh
                    col = ((m % 2) * 2 + hh) * 128
                    bk = by[m // 2]
                    MM(bk, psb[bk][:, col:col + 128], Xt[:, m * 128:(m + 1) * 128], Mc[:, hd, :], True, False,
                       ["Xt", ("Mb", c % 2)])
                    MM(bk, psb[bk][:, col:col + 128], Wtok[:, m * 128:(m + 1) * 128], identb[:], False, True,
                       ["Wtok", "identb"])
            grp_s["bk"] = chunk_state(c, 0)
            return by

        def stageQ2(c, by):
            cs = slice(c * 128, (c + 1) * 128)
            yz = tmp[0][:, 0:512].rearrange("p (m i) -> p m i", m=4)
            for m in range(4):
                for hh in range(2):
                    r0 = hh * 64
                    col = ((m % 2) * 2 + hh) * 128
                    bk = by[m // 2]
                    STT(yz[r0:r0 + 64, m, :], XS[r0:r0 + 64, m, cs], ev(32 + m)[r0:r0 + 64, :],
                        psb[bk][r0:r0 + 64, col:col + 128], ALU.mult, ALU.add, [("XS", m), ("ps", bk), "dsk"], tmp0t)
            ps_release(*by)
            TTO("dve", yz, yz, Zb[:, :, cs], ALU.mult, tmp0t + [("Zb", m) for m in range(4)], tmp0t)
            sqv = sq[1][:, 0:512].rearrange("p (m i) -> p m i", m=4)
            ACTF(sq[1][:, 0:512], tmp[0][:, 0:512], AF.Square, tmp0t, [("sq", 1)])
            bn = ps_alloc(hold=True)
            for m in range(4):
                MM(bn, psb[bn][:, 0:128], ones_bf[:], sqv[:, m, :], m == 0, m == 3, [("sq", 1), "ones"])
            ACTF(tmp[1][:, 0:128], psb[bn][:, 0:128], AF.Ln, [("ps", bn), "eps"], tmp1t, bias=eps_t[:], scale=1.0 / 512)
            ps_release(bn)
            ACTF(tmp[1][:, 0:128], tmp[1][:, 0:128], AF.Exp, tmp1t, tmp1t, scale=-0.5)
            for m in range(4):
                STT(Zb[:, m, cs], yz[:, m, :], ev(40 + m), tmp[1][:, 0:128], ALU.mult, ALU.mult,
                    tmp0t + tmp1t + ["evv"], [("Zb", m)])
            bk = grp_s["bk"]
            TTO("dve", v3(Hfc[:, :]), v3(Hfc[:, :]), bc8(SC[:, 5, c, 0:8]), ALU.mult, ["Hfc", sct], ["Hfc"])
            TTO("dve", Hfc[:, :], Hfc[:, :], psb[bk][:, 0:512], ALU.add, ["Hfc", ("ps", bk)], ["Hfc"])
            ps_release(bk)
            if c % 2 == 1:
                out_evs.append(DMA("sp", ssdT_d[i, seq_of(c), 0], Hfc[:, :], ["Hfc"], [], "o_Hfc"))
                if c in (1, 3, 5):
                    TS("dve", Hfc[:, :], Hfc[:, :], flagA, None, ALU.mult, None, ["Hfc", "flags"], ["Hfc"])

        DMA("sp", Hfc[:, :], h0T_d[i][:, 0, :], [], ["Hfc"], "mH")
        PIPE = cfg.get('ev_pipe', False)
        if cfg.get('ev_p3', True) and PIPE:
            stageP(0)
        for c in (range(10) if cfg.get('ev_p3', True) else []):
            if PIPE:
                by_ = stageQ(c)
                if c + 1 < 10:
                    stageP(c + 1)
                stageQ2(c, by_)
            else:
                stageP(c)
                by_ = stageQ(c)
                stageQ2(c, by_)

        WO = WStream(weout_d[i], LAY_OUT)
        mreads = [("Ub", m) for m in range(4)] + [("Zb", m) for m in range(4)]
        rms_begin()
        for j in range(8):
            banks = proj(WO, j, 8, lambda kc, t0, tn: (Ub[:, kc, t0:t0 + tn] if kc < 4 else Zb[:, kc - 4, t0:t0 + tn]),
                         mreads, 128, TT)
            for ti, (t0, tn) in enumerate(TT):
                s_ = SLOT_OF_TT[ti]
                STT(x[:, j, t0:t0 + tn], psb[banks[ti]][:, 0:tn], G_ap(l, 1, j, s_), x[:, j, t0:t0 + tn],
                    ALU.mult, ALU.add, [("ps", banks[ti]), ("gate", l % 2, 1), ("x", j)], [("x", j)])
            ps_release(*banks)
            if j >= 1:
                rms_chunk(j - 1)
        rms_chunk(7)
        fence()

    bg_start(mod_layer_gen(0), 1)
    for l in range(nlayers):
        if l == 0:
            bg_ensure(0, 0)
            bg_limit[0] = 3
        else:
            bg_limit[0] = 3
        ffn(l, 0)
        if l % 2 == 1 and cfg.get("odd", True):
            odd_mixer(l)
        if l % 2 == 0 and cfg.get("even", True):
            even_mixer(l)
        bg_flush()
        if l + 1 < nlayers:
            bg_start(mod_layer_gen(l + 1), 2)
        ffn(l, 1)

    rms_stats()
    for kc in range(8):
        tb = tmp[kc % 2]
        S.op("dve", lambda e, kc=kc, tb=tb: e.scalar_tensor_tensor(
            out=tb[:, 0:NT], in0=x[:, kc, :], scalar=gfin[:, kc:kc + 1], in1=rstd[:], op0=ALU.mult, op1=ALU.mult),
            reads=[("x", kc), ("rstd", 0), ("rstd", 1), ("rstd", 2), "gfin"], writes=[("tmp", kc % 2, 0), ("tmp", kc % 2, 1)])
        ev = S.op("sp", lambda e, kc=kc, tb=tb: e.dma_start(out=yT_d[kc], in_=tb[:, 0:NT]),
                  reads=[("tmp", kc % 2, 0), ("tmp", kc % 2, 1)], dma="o_tmp%d" % (kc % 2))
        out_evs.append(ev)
    S.wait_all("sp", out_evs)

    S.emit(nc, stack)
    stack.close()
    return nc


def _core_tokens(core, x_prompt, x_sample):
    if core < 2:
        a = x_sample[core]
        b = x_prompt[core]
    else:
        p0 = 2 + (core - 2) * 5
        a = x_prompt[p0:p0 + 4].reshape(1024, D)
        b = x_prompt[p0 + 4]
    return np.concatenate([a, b], axis=0)


def _prep_shared(inp):
    f = np.float32
    sh = {}
    w_mod = np.asarray(inp["w_mod"], f)
    sh["wmod"] = np.ascontiguousarray(
        w_mod.reshape(DEPTH, 8, 128, 36, 2, 128).transpose(0, 3, 2, 4, 1, 5)).reshape(DEPTH, 36, 128, 2048)
    b_mod = np.asarray(inp["b_mod"], f)
    sh["bmod"] = np.ascontiguousarray(b_mod.reshape(DEPTH, 72, 128).transpose(2, 0, 1)).reshape(128, DEPTH * 72)
    g_norm = np.asarray(inp["g_norm"], f)
    sh["gnorm"] = np.ascontiguousarray(g_norm.reshape(DEPTH, 3, 8, 128).transpose(3, 0, 1, 2)).reshape(128, DEPTH * 24)
    sh["gfin"] = np.ascontiguousarray(np.asarray(inp["g_final"], f).reshape(8, 128).T)
    w_gu = np.asarray(inp["w_ff_gu"], f)
    sh["wgu"] = np.ascontiguousarray(
        w_gu.reshape(DEPTH, 2, 8, 128, 2, 11, 2, 128).transpose(0, 1, 5, 3, 4, 6, 2, 7)).reshape(DEPTH, 2, 11, 128, 4096)
    w_dn = np.asarray(inp["w_ff_down"], f)
    sh["wdn"] = np.ascontiguousarray(
        w_dn.reshape(DEPTH, 2, NFC, 128, 8, 128).transpose(0, 1, 4, 3, 2, 5)).reshape(DEPTH, 2, 8, 128, 2816)
    w_in_odd = np.asarray(inp["w_in_odd"], f)
    w_uq = np.asarray(inp["w_uq"], f)
    w_ukv = np.asarray(inp["w_ukv"], f)
    w_out_odd = np.asarray(inp["w_out_odd"], f)
    sh["woin"] = np.stack([_pack(w_in_odd[i], _odd_in_blocks()) for i in range(2)])
    sh["wuq"] = np.stack([_pack(w_uq[i], _uq_blocks()) for i in range(2)])
    sh["wukv"] = np.stack([_pack(w_ukv[i], _ukv_blocks()) for i in range(2)])
    sh["woout"] = np.stack([_pack(w_out_odd[i], _out_blocks()) for i in range(2)])
    ov = np.zeros((128, 2, NOV), f)
    for i in range(2):
        ov[:, i, 0:3] = np.asarray(inp["g_cq"], f)[i].reshape(3, 128).T
        ov[:, i, 3:5] = np.asarray(inp["g_ckv"], f)[i].reshape(2, 128).T
        wdw = np.asarray(inp["w_dwconv"], f)[i]
        ov[:, i, 5:129] = wdw.reshape(31, 4, 128).transpose(2, 1, 0).reshape(128, 124)
        ov[:, i, 129:133] = np.asarray(inp["b_dwconv"], f)[i].reshape(4, 128).T
        ov[:, i, 133:137] = np.asarray(inp["g_conv_ln"], f)[i].reshape(4, 128).T
        ov[:, i, 137:141] = np.asarray(inp["b_conv_ln"], f)[i].reshape(4, 128).T
    sh["oddv"] = np.ascontiguousarray(ov.reshape(128, 2 * NOV))
    w_in_even = np.asarray(inp["w_in_even"], f)
    w_out_even = np.asarray(inp["w_out_even"], f)
    sh["wein"] = np.stack([_pack(w_in_even[i], _even_in_blocks()) for i in range(2)])
    sh["weout"] = np.stack([_pack(w_out_even[i], _out_blocks()) for i in range(2)])
    evv = np.zeros((128, 2, NEV), f)
    evb = np.zeros((2, 128, NEB), f)
    for i in range(2):
        wc = np.asarray(inp["w_conv_ssm"], f)[i]
        evv[:, i, 0:24] = wc.reshape(3, 8, 128).transpose(2, 1, 0).reshape(128, 24)
        evv[:, i, 24:32] = np.asarray(inp["b_conv_ssm"], f)[i].reshape(8, 128).T
        dsk = np.asarray(inp["d_skip"], f)[i]
        for m in range(4):
            evv[0:64, i, 32 + m] = dsk[0, 2 * m]
            evv[64:128, i, 32 + m] = dsk[0, 2 * m + 1]
            evv[0:64, i, 36 + m] = dsk[1, 2 * m]
            evv[64:128, i, 36 + m] = dsk[1, 2 * m + 1]
        evv[:, i, 40:44] = np.asarray(inp["g_ssm_out"], f)[i].reshape(4, 128).T
        evb[i, :, 0:512] = np.asarray(inp["g_gmlp_v"], f)[i][None, :]
        evb[i, :, 512:1024] = np.asarray(inp["b_spatial"], f)[i].reshape(1, 512)
        evb[i, :, 1024:1040] = np.asarray(inp["dt_bias"], f)[i].reshape(1, 16)
        evb[i, :, 1040:1056] = np.asarray(inp["a_log"], f)[i].reshape(1, 16)
    sh["evv"] = np.ascontiguousarray(evv.reshape(128, 2 * NEV))
    sh["evb"] = evb
    ws = np.asarray(inp["w_spatial"], f)
    sh["wsT"] = np.ascontiguousarray(ws.transpose(0, 3, 1, 2)).reshape(2, 128, 512)
    jj = np.arange(128)[:, None]
    ii = np.arange(128)[None, :]
    U = (jj <= ii).astype(f)
    L = (jj >= ii).astype(f)
    cst = np.stack([U, L, NEG * (1 - U), NEG * (1 - L), np.eye(128, dtype=f), np.ones((128, 128), f)], axis=1)
    sh["cst"] = np.ascontiguousarray(cst.reshape(128, 6 * 128))
    return sh


def _rope_tables():
    f = np.float32
    t = np.arange(1024)
    row = (t // 64).astype(f)
    col = (t % 64).astype(f)
    freqs = (np.float32(10000.0) ** (-np.arange(8, dtype=f) / np.float32(8))).astype(f)
    ang = np.stack([row[:, None] * freqs, col[:, None] * freqs], axis=1)
    cos = np.cos(ang).astype(f)
    sin = np.sin(ang).astype(f)
    C = np.zeros((32, 1024), f)
    Sg = np.zeros((32, 1024), f)
    for a in range(2):
        for r in range(2):
            for q in range(8):
                d = a * 16 + r * 8 + q
                C[d] = cos[:, a, q]
                Sg[d] = (-sin[:, a, q]) if r == 0 else sin[:, a, q]
    return np.concatenate([C, C], 0), np.concatenate([Sg, Sg], 0)


def _prep_core(core, inp):
    f = np.float32
    m = {}
    toks = _core_tokens(core, np.asarray(inp["x_prompt"], f), np.asarray(inp["x_sample"], f))
    m["xT"] = np.ascontiguousarray(toks.T).reshape(8, 128, NT)
    c_ctx = np.asarray(inp["c_ctx"], f)
    cA = np.asarray(inp["c"], f)[core] if core < 2 else c_ctx
    cv = np.stack([cA, c_ctx], axis=-1)
    m["cvec"] = np.ascontiguousarray(cv.reshape(8, 128, 2).transpose(1, 0, 2)).reshape(128, 16)
    is_s = core < 2
    fl = np.zeros((128, 4), f)
    fl[:, 0] = 1.0 if is_s else 0.0
    fl[:, 1] = 0.0 if is_s else NEG
    m["flags"] = fl
    if is_s:
        C, Sg = _rope_tables()
        m["rope"] = np.ascontiguousarray(np.stack([C, Sg]))
        ck = np.asarray(inp["cache_mla_ckv"], f)[core]
        m["cckv"] = np.ascontiguousarray(ck.reshape(2, 256, 2, 128).transpose(0, 3, 2, 1))
        kr = np.asarray(inp["cache_mla_krope"], f)[core]
        krT = kr.transpose(0, 2, 1)
        m["ckr"] = np.ascontiguousarray(np.concatenate([krT, krT], axis=1))
        st = np.asarray(inp["state_ssd"], f)[core]
        m["h0T"] = np.ascontiguousarray(st.transpose(0, 4, 1, 2, 3)).reshape(2, 128, 2, 512)
    else:
        m["h0T"] = np.zeros((2, 128, 2, 512), f)
        m["rope"] = np.ascontiguousarray(np.stack([np.ones((64, 1024), f), np.zeros((64, 1024), f)]))
        m["cckv"] = np.zeros((2, 128, 2, 256), f)
        m["ckr"] = np.zeros((2, 64, 256), f)
    return m


_NC_CACHE = {}


def kernel(**inputs):
    if "nc" not in _NC_CACHE:
        _NC_CACHE["nc"] = build_program()
    nc = _NC_CACHE["nc"]
    shared = _prep_shared(inputs)
    in_maps = []
    for core in range(NCORES):
        m = dict(shared)
        m.update(_prep_core(core, inputs))
        in_maps.append(m)
    res = run_bass_kernel_spmd(nc, in_maps, core_ids=list(range(NCORES)))
    return _gather(res.results, inputs)


def _gather(results, inputs):
    f = np.float32
    y_prompt = np.zeros((32, 256, D), f)
    y_sample = np.zeros((2, 1024, D), f)
    for core in range(NCORES):
        y = results[core]["yT"].reshape(D, NT).T
        if core < 2:
            y_sample[core] = y[:1024]
            y_prompt[core] = y[1024:]
        else:
            p0 = 2 + (core - 2) * 5
            y_prompt[p0:p0 + 4] = y[:1024].reshape(4, 256, D)
            y_prompt[p0 + 4] = y[1024:]
    new_ssd = np.zeros((32, 2, 2, 8, 64, 128), f)
    new_ckv = np.zeros((32, 2, 256, 256), f)
    new_kr = np.zeros((32, 2, 256, 32), f)
    for core in range(NCORES):
        ck = results[core]["ckvT"].reshape(2, 256, NT).transpose(0, 2, 1)
        kr = results[core]["krT"].reshape(2, 32, NT).transpose(0, 2, 1)
        if core < 2:
            seqs = [(core, 1024)]
        else:
            p0 = 2 + (core - 2) * 5
            seqs = [(p0 + j, j * 256) for j in range(4)] + [(p0 + 4, 1024)]
        for (b, t0) in seqs:
            new_ckv[b] = ck[:, t0:t0 + 256]
            new_kr[b] = kr[:, t0:t0 + 256]
        sd = results[core]["ssdT"].reshape(2, 5, 2, 128, 8, 64)
        for (b, t0) in seqs:
            new_ssd[b] = sd[:, t0 // 256].transpose(0, 1, 3, 4, 2)
    return (y_prompt, y_sample, new_ssd, new_ckv, new_kr)
```

```python
from contextlib import ExitStack
import numpy as np
import concourse.bass as bass
import concourse.mybir as mybir
from concourse.bass_utils import run_bass_kernel_spmd

F32 = mybir.dt.float32
F32R = mybir.dt.float32r
BF16 = mybir.dt.bfloat16
AF = mybir.ActivationFunctionType
ALU = mybir.AluOpType

D = 1024
DEPTH = 4
NT = 1280
TT = [(0, 512), (512, 512), (1024, 256)]
SLOT_OF_TT = [0, 0, 1]
SLOTS = [(0, 1024), (1024, 256)]
DFF = 2816
NFC = 22
EPS = 1e-6
NCORES = 8
NWSLOT = 3
WSLOT = 4096


def _layout(KC, ns):
    lay = []
    t = 0
    off = 0
    for n in ns:
        sz = KC * n
        if off + sz > WSLOT:
            t += 1
            off = 0
        lay.append((t, off, KC, n))
        off += sz
    return lay, t + 1


def _pack(w, blocks):
    KC = w.shape[0] // 128
    lay, nt = _layout(KC, [len(b) for b in blocks])
    out = np.zeros((nt, 128, WSLOT), np.float32)
    for (t, off, _, n), cols in zip(lay, blocks):
        blk = w[:, cols].reshape(KC, 128, n).transpose(1, 0, 2).reshape(128, KC * n)
        out[t, :, off:off + KC * n] = blk
    return out


def _ar(a, b):
    return list(range(a, b))


def _rope_partner(cols):
    c = np.asarray(cols).reshape(2, 2, 8)
    return list(c[:, ::-1, :].reshape(-1))


def _odd_in_blocks():
    kr = _ar(640, 672)
    blocks = [_ar(0, 128), _ar(128, 256), _ar(256, 384), kr + kr, _rope_partner(kr) * 2,
              _ar(384, 512), _ar(512, 640)]
    for c in range(4):
        blocks.append(_ar(672 + c * 128, 672 + (c + 1) * 128))
        blocks.append(_ar(1184 + c * 128, 1184 + (c + 1) * 128))
    return blocks


def _uq_blocks():
    blocks = []
    for m in range(4):
        blocks.append(_ar(2 * m * 96, 2 * m * 96 + 64) + _ar((2 * m + 1) * 96, (2 * m + 1) * 96 + 64))
    for m in range(4):
        blocks.append(_ar(2 * m * 96 + 64, 2 * m * 96 + 96) + _ar((2 * m + 1) * 96 + 64, (2 * m + 1) * 96 + 96))
    for m in range(4):
        blocks.append(_rope_partner(_ar(2 * m * 96 + 64, 2 * m * 96 + 96))
                      + _rope_partner(_ar((2 * m + 1) * 96 + 64, (2 * m + 1) * 96 + 96)))
    return blocks


def _ukv_blocks():
    blocks = []
    for m in range(4):
        blocks.append(_ar(2 * m * 128, 2 * m * 128 + 64) + _ar((2 * m + 1) * 128, (2 * m + 1) * 128 + 64))
    v = []
    for hh in range(8):
        v += _ar(hh * 128 + 64, hh * 128 + 128)
    blocks.append(v)
    return blocks


def _out_blocks():
    return [_ar(j * 128, (j + 1) * 128) for j in range(8)]


LAY_OIN, NT_OIN = _layout(8, [len(b) for b in _odd_in_blocks()])
LAY_UQ, NT_UQ = _layout(3, [len(b) for b in _uq_blocks()])
LAY_UKV, NT_UKV = _layout(2, [len(b) for b in _ukv_blocks()])
LAY_OUT, NT_OUT = _layout(8, [128] * 8)
NOV = 3 + 2 + 124 + 4 + 4 + 4


def _even_in_blocks():
    blocks = [_ar(c * 128, (c + 1) * 128) for c in range(4)]
    blocks += [_ar(1024 + c * 128, 1024 + (c + 1) * 128) for c in range(4)]
    blocks += [_ar(1536 + c * 128, 1536 + (c + 1) * 128) for c in range(8)]
    blocks.append(_ar(512, 1024))
    blocks.append(_ar(2560, 2576))
    return blocks


LAY_EIN, NT_EIN = _layout(8, [len(b) for b in _even_in_blocks()])
NEV = 24 + 8 + 4 + 4 + 4
NEB = 512 + 512 + 16 + 16
NEG = -30000.0

class Instr:
    __slots__ = ("fn", "waits", "signal", "dma")

    def __init__(self, fn, dma=None):
        self.fn = fn
        self.waits = []
        self.signal = False
        self.dma = dma


class Sched:
    ENGS = ("pe", "act", "dve", "pool", "sp")
    STRICT = 2
    CAP = 3000

    def __init__(self):
        self.q = {e: [] for e in self.ENGS}
        self.clock = {e: {} for e in self.ENGS}
        self.evclock = {}
        self.tok = {}
        self.dma_cnt = {}
        self.total_keys = set()

    def op(self, eng, fn, reads=(), writes=(), dma=None):
        q = self.q[eng]
        idx = len(q)
        ins = Instr(fn, dma)
        deps = set()
        raw = set()
        for t in reads:
            w = self.tok.get(t)
            if w is not None and w[0] is not None:
                deps.add(w[0])
                raw.add(w[0])
        for t in writes:
            w = self.tok.get(t)
            if w is not None:
                if w[0] is not None:
                    deps.add(w[0])
                deps.update(w[1])
        ck = self.clock[eng]
        for ev in sorted(deps, key=lambda e: (e[0], str(e[1]), -e[2])):
            kind, a, b = ev
            if kind == "e" and a == eng and Sched.STRICT != 1:
                if eng == "pe":
                    continue
                if Sched.STRICT == 0:
                    if ev not in raw:
                        continue
                    if idx - b > 3:
                        continue
            key = (kind, a)
            if ck.get(key, -1) >= b:
                continue
            ins.waits.append(ev)
            for k, v in self.evclock[ev].items():
                if ck.get(k, -1) < v:
                    ck[k] = v
            if kind == "e":
                self.q[a][b].signal = True
        if dma is not None:
            n = self.dma_cnt.get(dma, 0) + 1
            self.dma_cnt[dma] = n
            ev = ("d", dma, n)
        else:
            ev = ("e", eng, idx)
        ck2 = dict(ck)
        ck2[(ev[0], ev[1])] = ev[2]
        if dma is None:
            ck2[("e", eng)] = idx
        self.evclock[ev] = ck2
        q.append(ins)
        for t in reads:
            w = self.tok.setdefault(t, [None, []])
            w[1].append(ev)
        for t in writes:
            self.tok[t] = [ev, []]
        return ev

    def wait_all(self, eng, evs):
        ins = Instr(None)
        for ev in evs:
            ins.waits.append(ev)
            if ev[0] == "e":
                self.q[ev[1]][ev[2]].signal = True
        self.q[eng].append(ins)

    def emit(self, nc, stack):
        CAP = Sched.CAP
        sig = {}
        nsig = {}
        for e in self.ENGS:
            c = 0
            arr = []
            for ins in self.q[e]:
                if ins.signal and ins.dma is None:
                    c += 1
                arr.append(c)
            sig[e] = arr
            nsig[e] = c
        esem = {e: [stack.enter_context(nc.semaphore("s_%s%d" % (e, k))) for k in range(nsig[e] // CAP + 1)]
                for e in self.ENGS}
        dsem = {k: stack.enter_context(nc.semaphore("d_%s" % (k,))) for k in self.dma_cnt}
        for k, v in self.dma_cnt.items():
            assert 16 * v < 4000, (k, v)
        block = stack.enter_context(nc.Block())
        q = self.q
        total = self.total_keys
        dma_cnt = self.dma_cnt

        def run(ename, eng):
            for i, ins in enumerate(q[ename]):
                for (kind, a, b) in ins.waits:
                    if kind == "e":
                        c = sig[a][b]
                        eng.wait_ge(esem[a][(c - 1) // CAP], (c - 1) % CAP + 1)
                    else:
                        n = dma_cnt[a] if a in total else b
                        eng.wait_ge(dsem[a], 16 * n)
                if ins.fn is None:
                    continue
                r = ins.fn(eng)
                if ins.dma is not None:
                    r.then_inc(dsem[ins.dma], 16)
                elif ins.signal:
                    c = sig[ename][i]
                    r.then_inc(esem[ename][(c - 1) // CAP], 1)

        @block.tensor
        def _(e):
            run("pe", e)

        @block.scalar
        def _(e):
            run("act", e)

        @block.vector
        def _(e):
            run("dve", e)

        @block.gpsimd
        def _(e):
            run("pool", e)

        @block.sync
        def _(e):
            run("sp", e)


def build_program(cfg=None):
    cfg = cfg or {}
    nlayers = cfg.get("nlayers", DEPTH)
    do_mixer = cfg.get("mixer", True)
    nc = bass.Bass("TRN2", target_bir_lowering=False)
    S = Sched()
    stack = ExitStack()
    dram = {}

    def din(name, shape, dt=F32):
        dram[name] = nc.dram_tensor(name, list(shape), dt, kind="ExternalInput").ap()
        return dram[name]

    def dout(name, shape, dt=F32):
        dram[name] = nc.dram_tensor(name, list(shape), dt, kind="ExternalOutput").ap()
        return dram[name]

    def sb(name, shape, dt=F32):
        return stack.enter_context(nc.sbuf_tensor(name, list(shape), dt))

    xT_d = din("xT", [8, 128, NT])
    cvec_d = din("cvec", [128, 16])
    wmod_d = din("wmod", [DEPTH, 36, 128, 2048])
    bmod_d = din("bmod", [128, DEPTH * 72])
    gnorm_d = din("gnorm", [128, DEPTH * 3 * 8])
    gfin_d = din("gfin", [128, 8])
    wgu_d = din("wgu", [DEPTH, 2, 11, 128, 4096])
    wdn_d = din("wdn", [DEPTH, 2, 8, 128, 2816])
    yT_d = dout("yT", [8, 128, NT])
    flags_d = din("flags", [128, 4])
    woin_d = din("woin", [2, NT_OIN, 128, WSLOT])
    wuq_d = din("wuq", [2, NT_UQ, 128, WSLOT])
    wukv_d = din("wukv", [2, NT_UKV, 128, WSLOT])
    woout_d = din("woout", [2, NT_OUT, 128, WSLOT])
    oddv_d = din("oddv", [128, 2 * NOV])
    rope_d = din("rope", [2, 64, 1024])
    cckv_d = din("cckv", [2, 128, 2, 256])
    ckr_d = din("ckr", [2, 64, 256])
    ckvT_d = dout("ckvT", [2, 2, 128, NT])
    wein_d = din("wein", [2, NT_EIN, 128, WSLOT])
    weout_d = din("weout", [2, NT_OUT, 128, WSLOT])
    evv_d = din("evv", [128, 2 * NEV])
    evb_d = din("evb", [2, 128, NEB])
    wsT_d = din("wsT", [2, 128, 512])
    cst_d = din("cst", [128, 6 * 128])
    h0T_d = din("h0T", [2, 128, 2, 512])
    ssdT_d = dout("ssdT", [2, 5, 2, 128, 512])
    krT_d = dout("krT", [2, 32, NT])

    x = sb("x", [128, 8, NT], F32)
    h = sb("h", [128, 8, NT], BF16)
    big = sb("big", [128, 14336], F32)
    act = big[:, 0:14080].bitcast(BF16).rearrange("p (j t) -> p j t", j=NFC)
    wbuf = [sb("wbuf%d" % i, [128, WSLOT], BF16) for i in range(NWSLOT)]
    rstd = sb("rstd", [128, NT], F32)
    sq = [sb("sq%d" % i, [128, NT], BF16) for i in range(2)]
    tmp = [sb("tmp%d" % i, [128, 1440], F32) for i in range(2)]
    sg = [sb("sg%d" % i, [128, NT], BF16) for i in range(2)]
    ones_bf = sb("ones_bf", [128, 128], BF16)
    cvec = sb("cvec_sb", [128, 16], F32)
    csil = sb("csil", [128, 16], BF16)
    bmod = sb("bmod_sb", [128, DEPTH * 72], F32)
    gnorm = sb("gnorm_sb", [128, DEPTH * 24], F32)
    gfin = sb("gfin_sb", [128, 8], F32)
    modv = [sb("modv%d" % i, [128, 144], F32) for i in range(2)]
    amul = [sb("amul%d" % i, [128, 48], F32) for i in range(2)]
    gate = [sb("gate%d" % i, [128, 48], F32) for i in range(2)]
    eps_t = sb("eps_t", [128, 1], F32)
    flags = sb("flags_sb", [128, 4], F32)
    oddv = sb("oddv_sb", [128, 2 * NOV], F32)
    rstd2 = sb("rstd2", [128, NT], F32)
    ext = sb("ext", [128, 6400], F32)
    evv = sb("evv_sb", [128, 2 * NEV], F32)
    cst = sb("cst_sb", [128, 6 * 128], F32)
    identb = sb("identb", [128, 128], BF16)
    one_t = sb("one_t", [128, 1], F32)
    negub = sb("negub", [128, 256], BF16)
    aneg = sb("aneg", [128, 16], F32)
    s160 = [sb("s160_%d" % i, [128, 160], F32) for i in range(3)]
    sml = sb("sml", [128, 4], F32)
    fence_t = sb("fence_t", [128, 2], F32)
    psb = [stack.enter_context(nc.psum_tensor("ps%d" % i, [128, 512], F32)) for i in range(8)]

    ps_rr = [0]

    ps_held = set()

    def ps_alloc(hold=False):
        while True:
            b = ps_rr[0] % 8
            ps_rr[0] += 1
            if b not in ps_held:
                break
        if hold:
            ps_held.add(b)
        return b

    def ps_release(*bs):
        for b in bs:
            ps_held.discard(b)

    w_rr = [0]

    def load_w(src_ap, n):
        slot = w_rr[0] % NWSLOT
        w_rr[0] += 1
        S.op("pool", lambda e, slot=slot, src_ap=src_ap, n=n: e.dma_start(out=wbuf[slot][:, 0:n], in_=src_ap),
             writes=[("w", slot)], dma="w%d" % slot)
        return slot

    S.op("sp", lambda e: e.dma_start(out=cvec[:], in_=cvec_d), writes=["cvec"], dma="in")
    S.op("sp", lambda e: e.dma_start(out=bmod[:], in_=bmod_d), writes=["bmod"], dma="in")
    S.op("sp", lambda e: e.dma_start(out=gnorm[:], in_=gnorm_d), writes=["gnorm"], dma="in")
    S.op("sp", lambda e: e.dma_start(out=gfin[:], in_=gfin_d), writes=["gfin"], dma="in")
    for kc in range(8):
        S.op("sp", lambda e, kc=kc: e.dma_start(out=x[:, kc, :], in_=xT_d[kc]), writes=[("x", kc)], dma="in")
    S.op("sp", lambda e: e.dma_start(out=flags[:], in_=flags_d), writes=["flags"], dma="in")
    S.op("sp", lambda e: e.dma_start(out=oddv[:], in_=oddv_d), writes=["oddv"], dma="in")
    S.op("sp", lambda e: e.dma_start(out=evv[:], in_=evv_d), writes=["evv"], dma="in")
    S.op("sp", lambda e: e.dma_start(out=cst[:], in_=cst_d), writes=["cst"], dma="in")
    S.total_keys.add("in")
    S.op("dve", lambda e: e.memset(ones_bf[:], 1.0), writes=["ones"])
    S.op("dve", lambda e: e.memset(eps_t[:], EPS), writes=["eps"])
    S.op("dve", lambda e: e.memset(one_t[:], 1.0), writes=["one"])
    S.op("act", lambda e: e.activation(out=identb[:], in_=cst[:, 4 * 128:5 * 128], func=AF.Identity),
         reads=["cst"], writes=["identb"])
    S.op("act", lambda e: e.activation(out=negub[:], in_=cst[:, 2 * 128:4 * 128], func=AF.Identity),
         reads=["cst"], writes=["negub"])
    S.op("act", lambda e: e.activation(out=csil[:], in_=cvec[:], func=AF.Silu), reads=["cvec"], writes=["csil"])

    wm_rr = [0]
    bg = [None]

    bg_subs = [0]
    bg_limit = [0]

    def bg_start(gen, limit):
        bg[0] = gen
        bg_subs[0] = 0
        bg_limit[0] = limit

    def bg_step(n=1):
        for _ in range(n):
            if bg[0] is None or bg_subs[0] >= bg_limit[0]:
                return
            try:
                if next(bg[0]) == "sub":
                    bg_subs[0] += 1
                    if bg_subs[0] >= 3:
                        bg[0] = None
            except StopIteration:
                bg[0] = None

    def bg_flush():
        bg_limit[0] = 3
        while bg[0] is not None:
            bg_step(1)

    bg_done = set()

    def bg_ensure(l, sub):
        bg_limit[0] = max(bg_limit[0], sub + 1)
        while (l, sub) not in bg_done and bg[0] is not None:
            bg_step(1)

    def mod_layer_gen(l):
        mv = modv[l % 2]
        am = amul[l % 2]
        gt = gate[l % 2]
        mv3 = mv[:, :].rearrange("p (j s) -> p j s", s=2)
        NMS = 6
        for sub in range(3):
            pb = ps_alloc(hold=True)
            for t in range(12 * sub, 12 * (sub + 1)):
                slot = wm_rr[0] % NMS
                wm_rr[0] += 1
                wsl = ext[:, slot * 1024:(slot + 1) * 1024].bitcast(BF16)
                S.op("pool", lambda e, wsl=wsl, t=t: e.dma_start(out=wsl, in_=wmod_d[l, t]),
                     reads=["bigown"], writes=[("wm", slot)], dma="wm%d_%d" % (slot, l))
                for fcl in range(2):
                    j = 2 * t + fcl
                    for kc in range(8):
                        o = (fcl * 8 + kc) * 128
                        S.op("pe", lambda e, wsl=wsl, j=j, kc=kc, o=o, pb=pb: e.matmul(
                            psb[pb][:, 2 * j:2 * j + 2], lhsT=wsl[:, o:o + 128],
                            rhs=csil[:, 2 * kc:2 * kc + 2], start=(kc == 0), stop=(kc == 7)),
                            reads=[("wm", slot), "csil", "bigown"], writes=[("ps", pb)])
                yield
            j0, j1 = 24 * sub, 24 * (sub + 1)
            bm = bmod[:, l * 72 + j0:l * 72 + j1]
            for s in range(2):
                S.op("dve", lambda e, s=s, pb=pb, mv=mv, bm=bm, j0=j0, j1=j1: e.tensor_tensor(
                    out=mv[:, 2 * j0:2 * j1].rearrange("p (j s) -> p s j", s=2)[:, s, :],
                    in0=psb[pb][:, 2 * j0:2 * j1].rearrange("p (j s) -> p s j", s=2)[:, s, :],
                    in1=bm, op=ALU.add),
                    reads=[("ps", pb), "bmod"], writes=[("modv", l % 2, sub)])
            g_ap = gnorm[:, (l * 3 + sub) * 8:(l * 3 + sub + 1) * 8]
            for s in range(2):
                a_out = am[:, sub * 16:(sub + 1) * 16].rearrange("p (k s) -> p k s", s=2)[:, :, s]
                g_out = gt[:, sub * 16:(sub + 1) * 16].rearrange("p (k s) -> p k s", s=2)[:, :, s]
                sc_in = mv3[:, (3 * sub + 1) * 8:(3 * sub + 2) * 8, s]
                gt_in = mv3[:, (3 * sub + 2) * 8:(3 * sub + 3) * 8, s]
                S.op("dve", lambda e, a_out=a_out, sc_in=sc_in, g_ap=g_ap: e.scalar_tensor_tensor(
                    out=a_out, in0=sc_in, scalar=1.0, in1=g_ap, op0=ALU.add, op1=ALU.mult),
                    reads=[("modv", l % 2, sub), "gnorm"], writes=[("amul", l % 2, sub)])
                gs = 1.0 if sub == 1 else 0.5
                S.op("dve", lambda e, g_out=g_out, gt_in=gt_in, gs=gs: e.tensor_scalar(
                    out=g_out, in0=gt_in, scalar1=gs, scalar2=None, op0=ALU.mult),
                    reads=[("modv", l % 2, sub)], writes=[("gate", l % 2, sub)])
            bg_done.add((l, sub))
            ps_release(pb)
            yield "sub"

    def A_ap(l, sub, kc, s):
        o = sub * 16 + kc * 2 + s
        return amul[l % 2][:, o:o + 1]

    def G_ap(l, sub, kc, s):
        o = sub * 16 + kc * 2 + s
        return gate[l % 2][:, o:o + 1]

    def B_ap(l, sub, kc, s):
        o = ((3 * sub) * 8 + kc) * 2 + s
        return modv[l % 2][:, o:o + 1]

    pend = {"banks": None, "n": 0}

    def rms_begin():
        pend["banks"] = [ps_alloc(hold=True) for _ in TT]
        pend["n"] = 0

    def rms_chunk(kc):
        banks = pend["banks"]
        n = pend["n"]
        pend["n"] = n + 1
        sqb = sq[kc % 2]
        S.op("act", lambda e: e.activation(out=sqb[:], in_=x[:, kc, :], func=AF.Square),
             reads=[("x", kc)], writes=[("sq", kc % 2)])
        for ti, (t0, tn) in enumerate(TT):
            S.op("pe", lambda e, b=banks[ti], t0=t0, tn=tn: e.matmul(
                psb[b][:, 0:tn], lhsT=ones_bf[:], rhs=sqb[:, t0:t0 + tn], start=(n == 0), stop=(n == 7)),
                reads=[("sq", kc % 2), "ones"], writes=[("ps", banks[ti])])

    def rms_stats():
        if pend["banks"] is None:
            rms_begin()
            for kc in range(8):
                rms_chunk(kc)
        assert pend["n"] == 8
        banks = pend["banks"]
        for ti, (t0, tn) in enumerate(TT):
            S.op("act", lambda e, b=banks[ti], t0=t0, tn=tn: e.activation(
                out=rstd[:, t0:t0 + tn], in_=psb[b][:, 0:tn], func=AF.Ln, bias=eps_t[:], scale=1.0 / D),
                reads=[("ps", banks[ti]), "eps"], writes=[("rstd", ti)])
            S.op("act", lambda e, t0=t0, tn=tn: e.activation(
                out=rstd[:, t0:t0 + tn], in_=rstd[:, t0:t0 + tn], func=AF.Exp, scale=-0.5),
                reads=[("rstd", ti)], writes=[("rstd", ti)])
        ps_release(*banks)
        pend["banks"] = None

    def norm_mod(l, sub):
        rms_stats()
        for kc in range(8):
            tb = tmp[kc % 2]
            for s, (c0, cn) in enumerate(SLOTS):
                S.op("dve", lambda e, kc=kc, s=s, c0=c0, cn=cn, tb=tb: e.scalar_tensor_tensor(
                    out=tb[:, c0:c0 + cn], in0=x[:, kc, c0:c0 + cn], scalar=A_ap(l, sub, kc, s),
                    in1=rstd[:, c0:c0 + cn], op0=ALU.mult, op1=ALU.mult),
                    reads=[("x", kc), ("rstd", 0), ("rstd", 1), ("rstd", 2), ("amul", l % 2, sub)],
                    writes=[("tmp", kc % 2, s)])
                S.op("act", lambda e, kc=kc, s=s, c0=c0, cn=cn, tb=tb: e.activation(
                    out=h[:, kc, c0:c0 + cn], in_=tb[:, c0:c0 + cn], func=AF.Identity,
                    bias=B_ap(l, sub, kc, s), scale=1.0),
                    reads=[("tmp", kc % 2, s), ("modv", l % 2, sub)], writes=[("h", kc)])

    def ffn(l, si):
        sub = 0 if si == 0 else 2
        if pend["banks"] is None:
            rms_begin()
            for kc_ in range(8):
                rms_chunk(kc_)
        bg_ensure(l, sub)
        norm_mod(l, sub)
        hreads = [("h", kc) for kc in range(8)]
        for t in range(11):
            bg_step(3)
            slot = load_w(wgu_d[l, si, t], 4096)
            for fcl in range(2):
                j = 2 * t + fcl
                gb = [ps_alloc() for _ in TT]
                for kc in range(8):
                    off = ((0 * 2 + fcl) * 8 + kc) * 128
                    for ti, (t0, tn) in enumerate(TT):
                        S.op("pe", lambda e, b=gb[ti], slot=slot, off=off, kc=kc, t0=t0, tn=tn: e.matmul(
                            psb[b][:, 0:tn], lhsT=wbuf[slot][:, off:off + 128], rhs=h[:, kc, t0:t0 + tn],
                            start=(kc == 0), stop=(kc == 7)),
                            reads=[("w", slot), ("h", kc)], writes=[("ps", gb[ti])])
                sgb = sg[j % 2]
                for ti, (t0, tn) in enumerate(TT):
                    S.op("act", lambda e, b=gb[ti], t0=t0, tn=tn, sgb=sgb: e.activation(
                        out=sgb[:, t0:t0 + tn], in_=psb[b][:, 0:tn], func=AF.Silu),
                        reads=[("ps", gb[ti])], writes=[("sg", j % 2, ti)])
                ub = [ps_alloc() for _ in TT]
                for kc in range(8):
                    off = ((1 * 2 + fcl) * 8 + kc) * 128
                    for ti, (t0, tn) in enumerate(TT):
                        S.op("pe", lambda e, b=ub[ti], slot=slot, off=off, kc=kc, t0=t0, tn=tn: e.matmul(
                            psb[b][:, 0:tn], lhsT=wbuf[slot][:, off:off + 128], rhs=h[:, kc, t0:t0 + tn],
                            start=(kc == 0), stop=(kc == 7)),
                            reads=[("w", slot), ("h", kc)], writes=[("ps", ub[ti])])
                for ti, (t0, tn) in enumerate(TT):
                    S.op("dve", lambda e, b=ub[ti], t0=t0, tn=tn, sgb=sgb, j=j: e.tensor_tensor(
                        out=act[:, j, t0:t0 + tn], in0=psb[b][:, 0:tn], in1=sgb[:, t0:t0 + tn], op=ALU.mult),
                        reads=[("ps", ub[ti]), ("sg", j % 2, ti), "bigown"], writes=[("act", j)])
        areads = [("act", j) for j in range(NFC)] + ["bigown"]
        rms_begin()
        for j in range(8):
            slot = load_w(wdn_d[l, si, j], 2816)
            ob = [ps_alloc() for _ in TT]
            for kc in range(NFC):
                for ti, (t0, tn) in enumerate(TT):
                    S.op("pe", lambda e, b=ob[ti], slot=slot, kc=kc, t0=t0, tn=tn: e.matmul(
                        psb[b][:, 0:tn], lhsT=wbuf[slot][:, kc * 128:(kc + 1) * 128], rhs=act[:, kc, t0:t0 + tn],
                        start=(kc == 0), stop=(kc == NFC - 1)),
                        reads=[("w", slot)] + areads, writes=[("ps", ob[ti])])
            for ti, (t0, tn) in enumerate(TT):
                s = SLOT_OF_TT[ti]
                S.op("dve", lambda e, b=ob[ti], t0=t0, tn=tn, j=j, s=s: e.scalar_tensor_tensor(
                    out=x[:, j, t0:t0 + tn], in0=psb[b][:, 0:tn], scalar=G_ap(l, sub, j, s),
                    in1=x[:, j, t0:t0 + tn], op0=ALU.mult, op1=ALU.add),
                    reads=[("ps", ob[ti]), ("gate", l % 2, sub), ("x", j)], writes=[("x", j)])
            if j >= 1:
                rms_chunk(j - 1)
        rms_chunk(7)


    in_mixer = [False]

    def _r(reads):
        return list(reads) + (["bigown"] if in_mixer[0] else [])

    def MM(bank, out_ap, lhsT, rhs, start, stop, reads):
        reads = _r(reads)
        S.op("pe", lambda e: e.matmul(out_ap, lhsT=lhsT, rhs=rhs, start=start, stop=stop),
             reads=reads, writes=[("ps", bank)])

    def ACTF(out, in_, func, reads, writes, bias=None, scale=None):
        kw = {}
        if bias is not None:
            kw["bias"] = bias
        if scale is not None:
            kw["scale"] = scale
        reads = _r(reads)
        return S.op("act", lambda e: e.activation(out=out, in_=in_, func=func, **kw), reads=reads, writes=writes)

    def TTO(eng, out, in0, in1, op, reads, writes):
        reads = _r(reads)
        return S.op(eng, lambda e: e.tensor_tensor(out=out, in0=in0, in1=in1, op=op), reads=reads, writes=writes)

    def STT(out, in0, scalar, in1, op0, op1, reads, writes):
        reads = _r(reads)
        return S.op("dve", lambda e: e.scalar_tensor_tensor(out=out, in0=in0, scalar=scalar, in1=in1, op0=op0, op1=op1),
                    reads=reads, writes=writes)

    def TS(eng, out, in0, s1, s2, op0, op1, reads, writes):
        reads = _r(reads)
        if op1 is None:
            return S.op(eng, lambda e: e.tensor_scalar(out=out, in0=in0, scalar1=s1, scalar2=None, op0=op0),
                        reads=reads, writes=writes)
        return S.op(eng, lambda e: e.tensor_scalar(out=out, in0=in0, scalar1=s1, scalar2=s2, op0=op0, op1=op1),
                    reads=reads, writes=writes)

    def RECIP(out, in_, reads, writes):
        reads = _r(reads)
        return S.op("dve", lambda e: e.reciprocal(out=out, in_=in_), reads=reads, writes=writes)

    def MEMSET(eng, ap, val, reads, writes):
        reads = _r(reads)
        return S.op(eng, lambda e: e.memset(ap, val), reads=reads, writes=writes)

    def DMA(eng, out, in_, reads, writes, key):
        reads = _r(reads)
        return S.op(eng, lambda e: e.dma_start(out=out, in_=in_), reads=reads, writes=writes, dma=key)

    class WStream:
        def __init__(self, dram_ap, lay):
            self.d = dram_ap
            self.lay = lay
            self.loaded = {}
            self.used = {}
            for (t, off, KC, n) in lay:
                self.used[t] = max(self.used.get(t, 0), off + KC * n)

        def blk(self, b, kc):
            t, off, KC, n = self.lay[b]
            if t not in self.loaded:
                u = self.used[t]
                self.loaded[t] = load_w(self.d[t][:, 0:u], u)
            slot = self.loaded[t]
            return slot, wbuf[slot][:, off + kc * n:off + (kc + 1) * n]

    def proj(ws, b, KC, rhs_fn, rreads, M, tiles):
        banks = [ps_alloc(hold=True) for _ in tiles]
        for kc in range(KC):
            slot, lw = ws.blk(b, kc)
            for ti, (t0, tn) in enumerate(tiles):
                rr = rreads(kc) if callable(rreads) else rreads
                MM(banks[ti], psb[banks[ti]][0:M, 0:tn], lw, rhs_fn(kc, t0, tn), kc == 0, kc == KC - 1,
                   [("w", slot)] + rr)
        bg_step(1)
        return banks

    out_evs = []
    fence_n = [0]

    def fence():
        k = fence_n[0] % 2
        fence_n[0] += 1
        in_mixer[0] = False
        MEMSET("dve", fence_t[:, k:k + 1], 0.0, [], ["bigown"])

    def gated_out(ws, l, hreads):
        for j in range(8):
            banks = proj(ws, j, 8, lambda kc, t0, tn: h[:, kc, t0:t0 + tn], hreads, 128, TT)
            for ti, (t0, tn) in enumerate(TT):
                s_ = SLOT_OF_TT[ti]
                STT(x[:, j, t0:t0 + tn], psb[banks[ti]][:, 0:tn], G_ap(l, 1, j, s_), x[:, j, t0:t0 + tn],
                    ALU.mult, ALU.add, [("ps", banks[ti]), ("gate", l % 2, 1), ("x", j)], [("x", j)])
            ps_release(*banks)
            if j == 4:
                rms_begin()
                for jj in range(4):
                    rms_chunk(jj)
            elif j > 4:
                rms_chunk(j - 1)
        rms_chunk(7)

    def odd_mixer(l):
        i = l // 2

        def ov(o, n=1):
            return oddv[:, i * NOV + o:i * NOV + o + n]

        bg_flush()
        norm_mod(l, 1)
        fence()
        in_mixer[0] = True
        BO = ["bigown"]
        hreads = lambda kc: [("h", kc)]
        hall = [("h", kc) for kc in range(8)]
        off = [0]

        def carve(nw):
            r = big[:, off[0]:off[0] + nw]
            off[0] += nw
            return r

        ckvn = carve(1536).bitcast(BF16).rearrange("p (c t) -> p c t", c=2)
        QN = carve(2560).bitcast(BF16).rearrange("p (m t) -> p m t", m=4)
        QR = carve(2560).bitcast(BF16).rearrange("p (m t) -> p m t", m=4)
        KN = carve(3072).bitcast(BF16).rearrange("p (m t) -> p m t", m=4)
        Vt = carve(3072).bitcast(BF16).rearrange("p (k f) -> p k f", k=12)
        ropeC = carve(512).bitcast(BF16)
        ropeS = carve(512).bitcast(BF16)
        eoff = [0]

        def ecarve(nw):
            r = ext[:, eoff[0]:eoff[0] + nw]
            eoff[0] += nw
            return r

        cqg = ecarve(1920).bitcast(BF16).rearrange("p (c t) -> p c t", c=3)
        KR2 = ecarve(768).bitcast(BF16)
        cpadB = ecarve(2860).bitcast(BF16).rearrange("p (c t) -> p c t", c=4)
        rcb = rstd2[:, 0:512]
        dgr = ecarve(256).bitcast(BF16).rearrange("p (r i) -> p r i", r=4)
        PT = [ecarve(256).bitcast(BF16) for _ in range(2)]
        tmp0t = [("tmp", 0, 0), ("tmp", 0, 1)]
        tmp1t = [("tmp", 1, 0), ("tmp", 1, 1)]
        tmpt = [tmp0t, tmp1t]
        rst = [("rstd", 0), ("rstd", 1), ("rstd", 2)]

        DMA("pool", ckvn[:, :, 1280:1536], cckv_d[i], BO, ["ckvn_c"], "mA")
        DMA("pool", KR2[0:64, 1280:1536], ckr_d[i], [], ["KR2_c"], "mB")
        DMA("pool", ropeC[0:64, :], rope_d[0], BO, ["ropeC"], "mC")
        DMA("pool", ropeS[0:64, :], rope_d[1], BO, ["ropeS"], "mD")

        WI = WStream(woin_d[i], LAY_OIN)

        def hrhs(kc, t0, tn):
            return h[:, kc, t0:t0 + tn]

        def rms_finish(statb, dst, scale, toks):
            for ti, (t0, tn) in enumerate(TT):
                ACTF(dst[:, t0:t0 + tn], psb[statb[ti]][:, 0:tn], AF.Ln, [("ps", statb[ti]), "eps"], [toks[ti]],
                     bias=eps_t[:], scale=scale)
                ACTF(dst[:, t0:t0 + tn], dst[:, t0:t0 + tn], AF.Exp, [toks[ti]], [toks[ti]], scale=-0.5)
            ps_release(*statb)

        statb = [ps_alloc(hold=True) for _ in TT]
        for c in range(3):
            banks = proj(WI, c, 8, hrhs, hreads, 128, TT)
            for ti, (t0, tn) in enumerate(TT):
                ACTF(cqg[:, c, t0:t0 + tn], psb[banks[ti]][:, 0:tn], AF.Identity, [("ps", banks[ti]), "oddv"],
                     [("cqg", c)], scale=ov(c))
                ACTF(sq[c % 2][:, t0:t0 + tn], psb[banks[ti]][:, 0:tn], AF.Square, [("ps", banks[ti])],
                     [("sq", c % 2)])
            ps_release(*banks)
            for ti, (t0, tn) in enumerate(TT):
                MM(statb[ti], psb[statb[ti]][:, 0:tn], ones_bf[:], sq[c % 2][:, t0:t0 + tn], c == 0, c == 2,
                   [("sq", c % 2), "ones"])
        rms_finish(statb, rstd, 1.0 / 384, rst)

        banks = proj(WI, 3, 8, hrhs, hreads, 64, TT)
        for ti, (t0, tn) in enumerate(TT):
            ACTF(tmp[0][0:64, t0:t0 + tn], psb[banks[ti]][0:64, 0:tn], AF.Identity, [("ps", banks[ti])], tmp0t)
        ps_release(*banks)
        out_evs.append(DMA("sp", krT_d[i], tmp[0][0:32, 0:NT], tmp0t, [], "o_tmp0"))
        banks = proj(WI, 4, 8, hrhs, hreads, 64, TT[0:2])
        for ti, (t0, tn) in enumerate(TT[0:2]):
            TTO("dve", tmp[1][0:64, t0:t0 + tn], tmp[0][0:64, t0:t0 + tn], ropeC[0:64, t0:t0 + tn], ALU.mult,
                tmp0t + ["ropeC"] + BO, tmp1t)
            TTO("dve", rstd2[0:64, t0:t0 + tn], psb[banks[ti]][0:64, 0:tn], ropeS[0:64, t0:t0 + tn], ALU.mult,
                [("ps", banks[ti]), "ropeS"] + BO, ["rstd2"])
            TTO("dve", KR2[0:64, t0:t0 + tn], tmp[1][0:64, t0:t0 + tn], rstd2[0:64, t0:t0 + tn], ALU.add,
                tmp1t + ["rstd2"], ["KR2"])
        ps_release(*banks)
        ACTF(KR2[0:64, 1024:1280], tmp[0][0:64, 1024:1280], AF.Identity, tmp0t, ["KR2"])

        statb = [ps_alloc(hold=True) for _ in TT]
        for c in range(2):
            banks = proj(WI, 5 + c, 8, hrhs, hreads, 128, TT)
            for ti, (t0, tn) in enumerate(TT):
                ACTF(tmp[c][:, t0:t0 + tn], psb[banks[ti]][:, 0:tn], AF.Identity, [("ps", banks[ti]), "oddv"],
                     tmpt[c], scale=ov(3 + c))
                ACTF(sq[c % 2][:, t0:t0 + tn], psb[banks[ti]][:, 0:tn], AF.Square, [("ps", banks[ti])],
                     [("sq", c % 2)])
            ps_release(*banks)
            for ti, (t0, tn) in enumerate(TT):
                MM(statb[ti], psb[statb[ti]][:, 0:tn], ones_bf[:], sq[c % 2][:, t0:t0 + tn], c == 0, c == 1,
                   [("sq", c % 2), "ones"])
        rms_finish(statb, rstd2, 1.0 / 256, ["rstd2", "rstd2", "rstd2"])
        for c in range(2):
            TTO("dve", tmp[c][:, 0:NT], tmp[c][:, 0:NT], rstd2[:, :], ALU.mult, tmpt[c] + ["rstd2"], tmpt[c])
            out_evs.append(DMA("sp", ckvT_d[i, c], tmp[c][:, 0:NT], tmpt[c], [], "o_tmp%d" % c))
            ACTF(ckvn[:, c, 0:NT], tmp[c][:, 0:NT], AF.Identity, tmpt[c] + BO, ["ckvn"])

        def cp5(c):
            return cpadB[:, c, :].rearrange("p (s w) -> p s w", s=5)

        for c in range(4):
            MEMSET("dve", cp5(c)[:, :, 0:15], 0.0, [], [("cpad", c)])
            MEMSET("dve", cp5(c)[:, :, 271:286], 0.0, [], [("cpad", c)])
            ba = proj(WI, 7 + 2 * c, 8, hrhs, hreads, 128, TT)
            bg = proj(WI, 8 + 2 * c, 8, hrhs, hreads, 128, TT)
            segs = [(0, 2), (2, 4), (4, 5)]
            for ti, (t0, tn) in enumerate(TT):
                ACTF(sg[c % 2][:, t0:t0 + tn], psb[bg[ti]][:, 0:tn], AF.Sigmoid, [("ps", bg[ti])], [("sg", c % 2, ti)])
                s0, s1 = segs[ti]
                TTO("dve", cp5(c)[:, s0:s1, 15:271], psb[ba[ti]][:, 0:tn].rearrange("p (s w) -> p s w", w=256),
                    sg[c % 2][:, t0:t0 + tn].rearrange("p (s w) -> p s w", w=256), ALU.mult,
                    [("ps", ba[ti]), ("sg", c % 2, ti)], [("cpad", c)])
            ps_release(*ba)
            ps_release(*bg)
            TS("dve", cp5(c)[:, 1:4, 0:15], cp5(c)[:, 0:3, 256:271], flags[:, 0:1], None, ALU.mult, None,
               [("cpad", c), "flags"], [("cpad", c)])
            TS("dve", cp5(c)[:, 0:3, 271:286], cp5(c)[:, 1:4, 15:30], flags[:, 0:1], None, ALU.mult, None,
               [("cpad", c), "flags"], [("cpad", c)])

        WQ = WStream(wuq_d[i], LAY_UQ)

        def qrhs(kc, t0, tn):
            return cqg[:, kc, t0:t0 + tn]

        cqr = [("cqg", c) for c in range(3)]
        for m in range(4):
            banks = proj(WQ, m, 3, qrhs, cqr, 128, TT)
            for ti, (t0, tn) in enumerate(TT):
                TTO("dve", QN[:, m, t0:t0 + tn], psb[banks[ti]][:, 0:tn], rstd[:, t0:t0 + tn], ALU.mult,
                    [("ps", banks[ti]), rst[ti]] + BO, ["QN"])
            ps_release(*banks)
        for m in range(4):
            bq = proj(WQ, 4 + m, 3, qrhs, cqr, 64, TT)
            bp = proj(WQ, 8 + m, 3, qrhs, cqr, 64, TT[0:2])
            for ti, (t0, tn) in enumerate(TT[0:2]):
                TTO("dve", tmp[0][0:64, t0:t0 + tn], psb[bq[ti]][0:64, 0:tn], ropeC[0:64, t0:t0 + tn], ALU.mult,
                    [("ps", bq[ti]), "ropeC"] + BO, tmp0t)
                TTO("dve", tmp[1][0:64, t0:t0 + tn], psb[bp[ti]][0:64, 0:tn], ropeS[0:64, t0:t0 + tn], ALU.mult,
                    [("ps", bp[ti]), "ropeS"] + BO, tmp1t)
                TTO("dve", tmp[0][0:64, t0:t0 + tn], tmp[0][0:64, t0:t0 + tn], tmp[1][0:64, t0:t0 + tn], ALU.add,
                    tmp0t + tmp1t, tmp0t)
                TTO("dve", QR[0:64, m, t0:t0 + tn], tmp[0][0:64, t0:t0 + tn], rstd[0:64, t0:t0 + tn], ALU.mult,
                    tmp0t + [rst[ti]] + BO, ["QR"])
            t0, tn = TT[2]
            TTO("dve", QR[0:64, m, t0:t0 + tn], psb[bq[2]][0:64, 0:tn], rstd[0:64, t0:t0 + tn], ALU.mult,
                [("ps", bq[2]), rst[2]] + BO, ["QR"])
            ps_release(*bq)
            ps_release(*bp)

        WK = WStream(wukv_d[i], LAY_UKV)
        KT = [(0, 512), (512, 512), (1024, 512)]
        ckr_ = ["ckvn", "ckvn_c"] + BO

        def krhs(kc, t0, tn):
            return ckvn[:, kc, t0:t0 + tn]

        for m in range(4):
            banks = proj(WK, m, 2, krhs, ckr_, 128, KT)
            for ti, (t0, tn) in enumerate(KT):
                ACTF(KN[:, m, t0:t0 + tn], psb[banks[ti]][:, 0:tn], AF.Identity, [("ps", banks[ti])] + BO, ["KN"])
            ps_release(*banks)
        for kb in range(12):
            bk = ps_alloc(hold=True)
            for kc in range(2):
                slot, rw = WK.blk(4, kc)
                MM(bk, psb[bk][:, 0:512], ckvn[:, kc, kb * 128:(kb + 1) * 128], rw, kc == 0, kc == 1,
                   [("w", slot)] + ckr_)
            if kb % 2 == 0:
                ACTF(Vt[:, kb, :], psb[bk][:, 0:512], AF.Identity, [("ps", bk)] + BO, ["Vt"])
            else:
                S.op("dve", lambda e, kb=kb, bk=bk: e.tensor_copy(out=Vt[:, kb, :], in_=psb[bk][:, 0:512]),
                     reads=[("ps", bk)] + BO, writes=["Vt"])
            ps_release(bk)

        dbuf = [sq[0], sq[1], sg[0], sg[1]]
        dtok = [[("sq", 0)], [("sq", 1)], [("sg", 0, t_) for t_ in range(3)], [("sg", 1, t_) for t_ in range(3)]]
        conv_ops = []

        def mk_conv(c):
            accf = tmp[c % 2][:, 0:NT]
            acc = accf.rearrange("p (s w) -> p s w", s=5)
            c5 = cp5(c)
            tt_ = tmpt[c % 2]
            conv_ops.append(lambda: TS("dve", acc, c5[:, :, 0:256], ov(5 + c * 31), ov(129 + c), ALU.mult, ALU.add,
                                       [("cpad", c), "oddv"], tt_))
            for k in range(1, 31):
                conv_ops.append(lambda k=k: STT(acc, c5[:, :, k:k + 256], ov(5 + c * 31 + k), acc, ALU.mult, ALU.add,
                                                [("cpad", c), "oddv"] + tt_, tt_))
            conv_ops.append(lambda: S.op("pool", lambda e: e.tensor_copy(out=dbuf[c][:, :], in_=accf),
                                         reads=_r(tt_), writes=dtok[c]))

        for c in range(4):
            mk_conv(c)
        conv_pos = [0]

        def conv_some(n):
            for _ in range(n):
                if conv_pos[0] < len(conv_ops):
                    conv_ops[conv_pos[0]]()
                    conv_pos[0] += 1

        sm_scale = 96.0 ** -0.5
        zb, nb_ = flags[:, 2:3], flags[:, 1:2]
        PT4 = PT + [carve(256).bitcast(BF16) for _ in range(2)]
        steps = []
        for m in range(4):
            for (qc, qn, kbs) in [(0, 512, list(range(8)) + [10, 11]), (512, 512, list(range(8)) + [10, 11]),
                                  (1024, 256, [8, 9])]:
                for ki, kb in enumerate(kbs):
                    halves = []
                    for hc in range(0, qn, 256):
                        qseq = (qc + hc) // 256
                        halves.append(zb if (kb // 2) == qseq else nb_)
                    steps.append((m, qc, qn, kb, ki == 0, ki == len(kbs) - 1, halves))
        sbk_of = {}
        grp = {}

        def stageS(k):
            m, qc, qn, kb, first, last, halves = steps[k]
            k0 = kb * 128
            sb2 = [ps_alloc(hold=True) for _ in range(2)]
            sbk_of[k] = sb2
            for hh in range(2):
                r0 = hh * 64
                MM(sb2[hh], psb[sb2[hh]][:, 0:qn], KN[r0:r0 + 64, m, k0:k0 + 128], QN[r0:r0 + 64, m, qc:qc + qn],
                   True, False, ["KN", "QN"])
            for hh in range(2):
                q0 = hh * 32
                MM(sb2[hh], psb[sb2[hh]][:, 0:qn], KR2[q0:q0 + 32, k0:k0 + 128], QR[q0:q0 + 32, m, qc:qc + qn],
                   False, True, ["KR2", "KR2_c", "QR"])

        def stageE(k):
            m, qc, qn, kb, first, last, halves = steps[k]
            sb2 = sbk_of.pop(k)
            for hh in range(2):
                pi = (k % 2) * 2 + hh
                pt = PT4[pi]
                ptt = ("PT", pi)
                if len(halves) == 1 or halves[0] is halves[1]:
                    ACTF(pt[:, 0:qn], psb[sb2[hh]][:, 0:qn], AF.Exp, [("ps", sb2[hh]), "flags"], [ptt], bias=halves[0],
                         scale=sm_scale)
                else:
                    for hi, hb in enumerate(halves):
                        ACTF(pt[:, hi * 256:(hi + 1) * 256], psb[sb2[hh]][:, hi * 256:(hi + 1) * 256], AF.Exp,
                             [("ps", sb2[hh]), "flags"], [ptt], bias=hb, scale=sm_scale)
            ps_release(*sb2)

        def stagePV(k):
            m, qc, qn, kb, first, last, halves = steps[k]
            if first:
                grp["ob"] = [ps_alloc(hold=True) for _ in range(2)]
                grp["smb"] = [ps_alloc(hold=True) for _ in range(2)]
            for hh in range(2):
                r0 = hh * 64
                ob, smb = grp["ob"][hh], grp["smb"][hh]
                pi = (k % 2) * 2 + hh
                pt = PT4[pi]
                ptt = ("PT", pi)
                MM(ob, psb[ob][:, 0:qn], Vt[:, kb, m * 128:(m + 1) * 128], pt[:, 0:qn], first, last, ["Vt", ptt])
                MM(smb, psb[smb][:, 0:qn], ones_bf[:], pt[:, 0:qn], first, last, [ptt, "ones"])
                if last:
                    ACTF(rcb[r0:r0 + 64, 0:qn], psb[smb][r0:r0 + 64, 0:qn], AF.Ln, [("ps", smb)], ["rstd2"])
                    ACTF(rcb[r0:r0 + 64, 0:qn], rcb[r0:r0 + 64, 0:qn], AF.Exp, ["rstd2"], ["rstd2"], scale=-1.0)
                    TTO("dve", h[r0:r0 + 64, m, qc:qc + qn], psb[ob][r0:r0 + 64, 0:qn], rcb[r0:r0 + 64, 0:qn], ALU.mult,
                        [("ps", ob), "rstd2"], [("h", m)])
            if last:
                ps_release(*grp["ob"])
                ps_release(*grp["smb"])

        stageS(0)
        for k in range(len(steps)):
            if k + 1 < len(steps):
                stageS(k + 1)
            stageE(k)
            stagePV(k)
            conv_some(2 if k % 2 == 0 else 1)
        conv_some(len(conv_ops))

        s1 = [ps_alloc(hold=True) for _ in TT]
        s2 = [ps_alloc(hold=True) for _ in TT]
        for c in range(4):
            dc = dbuf[c]
            sqc = cpadB[:, c, 0:NT]
            ACTF(sqc, dc[:, :], AF.Square, dtok[c], [("cpad", c)])
            for ti, (t0, tn) in enumerate(TT):
                MM(s1[ti], psb[s1[ti]][:, 0:tn], ones_bf[:], dc[:, t0:t0 + tn], c == 0, c == 3, dtok[c] + ["ones"])
                MM(s2[ti], psb[s2[ti]][:, 0:tn], ones_bf[:], sqc[:, t0:t0 + tn], c == 0, c == 3,
                   [("cpad", c), "ones"])
        for ti, (t0, tn) in enumerate(TT):
            ACTF(rstd[:, t0:t0 + tn], psb[s1[ti]][:, 0:tn], AF.Identity, [("ps", s1[ti])], [rst[ti]], scale=1.0 / 512)
            ACTF(tmp[0][:, t0:t0 + tn], psb[s1[ti]][:, 0:tn], AF.Square, [("ps", s1[ti])], tmp0t, scale=1.0 / 512)
            STT(tmp[0][:, t0:t0 + tn], psb[s2[ti]][:, 0:tn], 1.0 / 512, tmp[0][:, t0:t0 + tn], ALU.mult, ALU.subtract,
                [("ps", s2[ti])] + tmp0t, tmp0t)
            ACTF(rstd2[:, t0:t0 + tn], tmp[0][:, t0:t0 + tn], AF.Ln, tmp0t + ["eps"], ["rstd2"], bias=eps_t[:], scale=1.0)
            ACTF(rstd2[:, t0:t0 + tn], rstd2[:, t0:t0 + tn], AF.Exp, ["rstd2"], ["rstd2"], scale=-0.5)
        ps_release(*s1)
        ps_release(*s2)
        for c in range(4):
            dc = dbuf[c][:, :]
            TTO("dve", tmp[1][:, 0:NT], dc, rstd[:, :], ALU.subtract, dtok[c] + rst, tmp1t)
            TTO("dve", tmp[1][:, 0:NT], tmp[1][:, 0:NT], rstd2[:, :], ALU.mult, tmp1t + ["rstd2"], tmp1t)
            ACTF(h[:, 4 + c, :], tmp[1][:, 0:NT], AF.Silu, tmp1t + ["oddv"], [("h", 4 + c)],
                 bias=ov(137 + c), scale=ov(133 + c))

        WO = WStream(woout_d[i], LAY_OUT)
        gated_out(WO, l, hreads)
        fence()


    Umat = cst[:, 0:128]
    Lmat = cst[:, 128:256]
    NEGU = cst[:, 256:384]
    NEGL = cst[:, 384:512]
    identf = cst[:, 512:640]
    onesf = cst[:, 640:768]

    def even_mixer(l):
        i = l // 2

        def ev(o, n=1):
            return evv[:, i * NEV + o:i * NEV + o + n]

        bg_flush()
        norm_mod(l, 1)
        fence()
        in_mixer[0] = True
        hreads = lambda kc: [("h", kc)]
        hall = [("h", kc) for kc in range(8)]
        off = [0]

        def carve(nw):
            r = big[:, off[0]:off[0] + nw]
            off[0] += nw
            return r

        Ub = carve(2560).bitcast(BF16).rearrange("p (m t) -> p m t", m=4)
        Zb = carve(2560).bitcast(BF16).rearrange("p (m t) -> p m t", m=4)
        XS = carve(2560).bitcast(BF16).rearrange("p (m t) -> p m t", m=4)
        BC = carve(2560).bitcast(BF16).rearrange("p (m t) -> p m t", m=4)
        Hb = carve(2560).bitcast(BF16).rearrange("p (c f) -> p c f", c=10)
        SC = carve(960).rearrange("p (k c f) -> p k c f", k=6, c=10)
        Mb_b = carve(512).bitcast(BF16).rearrange("p (h i) -> p h i", h=8)
        eoff = [0]

        def ecarve(nw):
            r = ext[:, eoff[0]:eoff[0] + nw]
            eoff[0] += nw
            return r

        evb = ecarve(NEB)
        WsT = ecarve(256).bitcast(BF16).rearrange("p (g i) -> p g i", g=4)
        REf = ecarve(1024).rearrange("p (h i) -> p h i", h=8)
        REb = ecarve(1024).rearrange("p (h i) -> p h i", h=8)
        Mb = ecarve(512).bitcast(BF16).rearrange("p (h i) -> p h i", h=8)
        Xt = ecarve(256).bitcast(BF16)
        Btok = ecarve(128).bitcast(BF16)
        Xw = ecarve(256).bitcast(BF16)
        vtm = ecarve(256).bitcast(BF16)
        Wtok = ecarve(256).bitcast(BF16)
        Hfc = ecarve(512)
        Hbc = ecarve(512)
        Hfb = ecarve(256).bitcast(BF16)
        gv_bc = evb[:, 0:512]
        bs_bc = evb[:, 512:1024]
        dtb_bc = evb[:, 1024:1040]
        alog_bc = evb[:, 1040:1056]
        flagA = flags[:, 0:1]
        tmp0t = [("tmp", 0, 0), ("tmp", 0, 1)]
        tmp1t = [("tmp", 1, 0), ("tmp", 1, 1)]
        tmpt = [tmp0t, tmp1t]

        DMA("sp", evb, evb_d[i], [], ["evb"], "mE")
        DMA("pool", WsT, wsT_d[i].rearrange("p (g i) -> p g i", g=4), [], ["WsT"], "mF")
        ACTF(aneg[:], alog_bc, AF.Exp, ["evb"], ["aneg"])
        TS("dve", aneg[:], aneg[:], -1.0, None, ALU.mult, None, ["aneg"], ["aneg"])

        TTO("dve", ev(32, 4), ev(32, 4), ev(36, 4), ALU.add, ["evv"], ["dsk"])
        WE = WStream(wein_d[i], LAY_EIN)

        def hrhs(kc, t0, tn):
            return h[:, kc, t0:t0 + tn]

        for c in range(4):
            banks = proj(WE, c, 8, hrhs, hreads, 128, TT)
            for ti, (t0, tn) in enumerate(TT):
                ACTF(Ub[:, c, t0:t0 + tn], psb[banks[ti]][:, 0:tn], AF.Gelu_apprx_tanh, [("ps", banks[ti])], [("Ub", c)])
            ps_release(*banks)
        for c in range(4):
            banks = proj(WE, 4 + c, 8, hrhs, hreads, 128, TT)
            for ti, (t0, tn) in enumerate(TT):
                ACTF(Zb[:, c, t0:t0 + tn], psb[banks[ti]][:, 0:tn], AF.Silu, [("ps", banks[ti])], [("Zb", c)])
            ps_release(*banks)
        segs = [(0, 2), (2, 4), (4, 5)]
        for c in range(8):
            xp = tmp[c % 2][:, 0:1290].rearrange("p (s w) -> p s w", s=5)
            tt_ = tmpt[c % 2]
            MEMSET("dve", xp[:, :, 0:1], 0.0, [], tt_)
            MEMSET("dve", xp[:, :, 257:258], 0.0, [], tt_)
            banks = proj(WE, 8 + c, 8, hrhs, hreads, 128, TT)
            for ti, (t0, tn) in enumerate(TT):
                s0, s1 = segs[ti]
                ACTF(xp[:, s0:s1, 1:257], psb[banks[ti]][:, 0:tn].rearrange("p (s w) -> p s w", w=256), AF.Identity,
                     [("ps", banks[ti])], tt_)
            ps_release(*banks)
            TS("dve", xp[:, 1:4, 0:1], xp[:, 0:3, 256:257], flagA, None, ALU.mult, None, tt_ + ["flags"], tt_)
            TS("dve", xp[:, 0:3, 257:258], xp[:, 1:4, 1:2], flagA, None, ALU.mult, None, tt_ + ["flags"], tt_)
            accb = [rstd2, rstd][c % 2]
            acct = [["rstd2"], [("rstd", 0), ("rstd", 1), ("rstd", 2)]][c % 2]
            acc = accb[:, :].rearrange("p (s w) -> p s w", s=5)
            TS("dve", acc, xp[:, :, 0:256], ev(c * 3 + 0), ev(24 + c), ALU.mult, ALU.add, tt_ + ["evv"], acct)
            STT(acc, xp[:, :, 1:257], ev(c * 3 + 1), acc, ALU.mult, ALU.add, tt_ + ["evv"] + acct, acct)
            STT(acc, xp[:, :, 2:258], ev(c * 3 + 2), acc, ALU.mult, ALU.add, tt_ + ["evv"] + acct, acct)
            if c < 4:
                dst, dtok = XS[:, c, :], ("XS", c)
            else:
                dst, dtok = BC[:, c - 4, :], ("BC", c - 4)
            ACTF(dst, accb[:, :], AF.Silu, acct, [dtok])

        def bc8(ap8):
            return ap8.unsqueeze(2).broadcast_to([128, 8, 64])

        def v3(ap):
            return ap.rearrange("p (h q) -> p h q", h=8)

        def tok_major(c):
            cs = slice(c * 128, (c + 1) * 128)
            bk = ps_alloc(hold=True)
            pb = psb[bk][:, :].bitcast(BF16)
            for m in range(4):
                S.op("pe", lambda e, m=m: e.transpose(pb[:, m * 128:(m + 1) * 128], XS[:, m, cs], identb[:]),
                     reads=_r([("XS", m), "identb"]), writes=[("ps", bk)])
            for g in range(2):
                S.op("pe", lambda e, g=g: e.transpose(pb[:, 512 + g * 128:512 + (g + 1) * 128], BC[:, g, cs], identb[:]),
                     reads=_r([("BC", g), "identb"]), writes=[("ps", bk)])
            ACTF(Xt[:, :], pb[:, 0:512], AF.Identity, [("ps", bk)], ["Xt"])
            S.op("dve", lambda e: e.tensor_copy(out=Btok[:, :], in_=pb[:, 512:768]), reads=_r([("ps", bk)]), writes=["Btok"])
            ps_release(bk)

        def seq_of(c):
            return c // 2

        sct = "SC"

        def f160(ap3):
            return ap3.rearrange("p c f -> p (c f)")

        def v160(ap2):
            return ap2.rearrange("p (c f) -> p c f", f=16)

        P0 = cfg.get('ev_p0', 2)
        if P0 >= 1:
            bkd = ps_alloc(hold=True)
            for c in range(10):
                cs = slice(c * 128, (c + 1) * 128)
                for kc in range(8):
                    slot, rw = WE.blk(17, kc)
                    MM(bkd, psb[bkd][:, c * 16:(c + 1) * 16], h[:, kc, cs], rw, kc == 0, kc == 7, [("w", slot), ("h", kc)])
            TTO("dve", v160(s160[0][:, :]), v160(psb[bkd][:, 0:160]), dtb_bc.unsqueeze(1).broadcast_to([128, 10, 16]), ALU.add,
                [("ps", bkd), "evb"], [("s160", 0)])
            ps_release(bkd)
            ACTF(s160[0][:, :], s160[0][:, :], AF.Exp, [("s160", 0)], [("s160", 0)])
            ACTF(f160(SC[:, 0, :, :]), s160[0][:, :], AF.Ln, [("s160", 0), "one"], [sct], bias=one_t[:], scale=1.0)
            ACTF(s160[1][:, :], f160(SC[:, 0, :, :]), AF.Ln, [sct], [("s160", 1)])
            TTO("dve", SC[:, 1, :, :], SC[:, 0, :, :], aneg[:, :].unsqueeze(1).broadcast_to([128, 10, 16]), ALU.mult,
                [sct, "aneg"], [sct])
        if P0 >= 2:
            bkc = ps_alloc(hold=True)
            for c in range(10):
                MM(bkc, psb[bkc][:, c * 32:c * 32 + 8], Umat, SC[:, 1, c, 0:8], True, True, [sct, "cst"])
                MM(bkc, psb[bkc][:, c * 32 + 8:c * 32 + 16], Lmat, SC[:, 1, c, 8:16], True, True, [sct, "cst"])
                MM(bkc, psb[bkc][:, c * 32 + 16:c * 32 + 32], onesf, SC[:, 1, c, :], True, True, [sct, "cst"])
            if P0 == 3:
                ps_release(bkc)
            else:
                ACTF(tmp[0][:, 0:320], psb[bkc][:, 0:320], AF.Identity, [("ps", bkc)], tmp0t)
                ps_release(bkc)
                pc = tmp[0][:, 0:320].rearrange("p (c f) -> p c f", f=32)
                cumv = pc[:, :, 0:16]
                totv = pc[:, :, 16:32]
                TTO("dve", SC[:, 2, :, :], v160(s160[1][:, :]), cumv, ALU.subtract, [("s160", 1)] + tmp0t, [sct])
                ACTF(SC[:, 4, :, :], cumv, AF.Exp, tmp0t, [sct])
                ACTF(SC[:, 5, :, :], totv, AF.Exp, tmp0t, [sct])
                TTO("dve", v160(s160[0][:, :]), totv, cumv, ALU.subtract, tmp0t, [("s160", 0)])
                ACTF(s160[0][:, :], s160[0][:, :], AF.Exp, [("s160", 0)], [("s160", 0)])
                TTO("dve", SC[:, 3, :, :], v160(s160[0][:, :]), SC[:, 0, :, :], ALU.mult, [("s160", 0), sct], [sct])

        def chunk_state(c, d):
            TTO("dve", v3(Xw[:, :]), v3(Xt[:, :]), bc8(SC[:, 3, c, d * 8:(d + 1) * 8]), ALU.mult, ["Xt", sct], ["Xw"])
            bk = ps_alloc(hold=True)
            for g in range(2):
                MM(bk, psb[bk][:, g * 256:(g + 1) * 256], Btok[:, g * 128:(g + 1) * 128], Xw[:, g * 256:(g + 1) * 256],
                   True, True, ["Btok", "Xw"])
            return bk

        for c in ([9, 8, 7, 6, 5, 4, 3, 2, 1, 0] if cfg.get('ev_p1', True) else []):
            cs = slice(c * 128, (c + 1) * 128)
            par = c % 2
            tpt = tmpt[par]
            if c == 9:
                MEMSET("dve", Hbc[:, :], 0.0, [], ["Hbc"])
            if c == 7:
                DMA("sp", Hbc[:, :], h0T_d[i][:, 1, :], [], ["Hbc"], "mG")
            bk = ps_alloc(hold=True)
            for kc in range(8):
                slot, rw = WE.blk(16, kc)
                MM(bk, psb[bk][:, 0:512], h[:, kc, cs], rw, kc == 0, kc == 7, [("w", slot), ("h", kc)])
            ACTF(tmp[par][:, 0:512], psb[bk][:, 0:512], AF.Gelu_apprx_tanh, [("ps", bk)], tpt)
            ps_release(bk)
            S.op("act", lambda e, par=par: e.activation(out=sq[0][:, 0:512], in_=tmp[par][:, 0:512], func=AF.Square,
                                                        accum_out=sml[:, 2 * par:2 * par + 1]),
                 reads=_r(tpt), writes=[("sq", 0), ("sml", par)])
            ACTF(sml[:, 2 * par + 1:2 * par + 2], sml[:, 2 * par:2 * par + 1], AF.Ln, [("sml", par), "eps"], [("sml1", par)],
                 bias=eps_t[:], scale=1.0 / 512)
            ACTF(sml[:, 2 * par + 1:2 * par + 2], sml[:, 2 * par + 1:2 * par + 2], AF.Exp, [("sml1", par)], [("sml1", par)],
                 scale=-0.5)
            STT(vtm[:, :], tmp[par][:, 0:512], sml[:, 2 * par + 1:2 * par + 2], gv_bc, ALU.mult, ALU.mult,
                tpt + [("sml1", par), "evb"], ["vtm"])
            tok_major(c)
            bk = chunk_state(c, 1)
            ACTF(Hb[:, c, :], Hbc[:, :], AF.Identity, ["Hbc"], [("Hb", c)])
            TTO("dve", v3(Hbc[:, :]), v3(Hbc[:, :]), bc8(SC[:, 5, c, 8:16]), ALU.mult, ["Hbc", sct], ["Hbc"])
            TTO("dve", Hbc[:, :], Hbc[:, :], psb[bk][:, 0:512], ALU.add, ["Hbc", ("ps", bk)], ["Hbc"])
            ps_release(bk)
            bk = ps_alloc(hold=True)
            for g in range(4):
                MM(bk, psb[bk][:, g * 128:(g + 1) * 128], vtm[:, g * 128:(g + 1) * 128], WsT[:, g, :], True, True,
                   ["vtm", "WsT"])
            TTO("dve", tmp[par][:, 512:1024], psb[bk][:, 0:512], bs_bc, ALU.add, [("ps", bk), "evb"], tpt)
            ps_release(bk)
            TTO("dve", Ub[:, :, cs], tmp[par][:, 512:1024].rearrange("p (g i) -> p g i", g=4), Ub[:, :, cs], ALU.mult,
                tpt + [("Ub", m) for m in range(4)], [("Ub", m) for m in range(4)])
            if c % 2 == 0:
                out_evs.append(DMA("sp", ssdT_d[i, seq_of(c), 1], Hbc[:, :], ["Hbc"], [], "o_Hbc"))
                if c in (2, 4, 6):
                    TS("dve", Hbc[:, :], Hbc[:, :], flagA, None, ALU.mult, None, ["Hbc", "flags"], ["Hbc"])

        Mbs = [Mb, Mb_b]
        grp_s = {}

        def stageP(c):
            cs = slice(c * 128, (c + 1) * 128)
            Mc = Mbs[c % 2]
            bf_ = [ps_alloc(hold=True) for _ in range(2)]
            bb_ = [ps_alloc(hold=True) for _ in range(2)]
            for hh in range(8):
                o_ = (hh % 4) * 128
                MM(bf_[hh // 4], psb[bf_[hh // 4]][:, o_:o_ + 128], SC[:, 1, c, hh:hh + 1].broadcast_to([128, 128]), Umat,
                   True, False, [sct, "cst"])
                MM(bf_[hh // 4], psb[bf_[hh // 4]][:, o_:o_ + 128], identb[:], negub[:, 0:128], False, True, ["identb", "negub"])
                MM(bb_[hh // 4], psb[bb_[hh // 4]][:, o_:o_ + 128], SC[:, 1, c, 8 + hh:9 + hh].broadcast_to([128, 128]), Lmat,
                   True, False, [sct, "cst"])
                MM(bb_[hh // 4], psb[bb_[hh // 4]][:, o_:o_ + 128], identb[:], negub[:, 128:256], False, True, ["identb", "negub"])
            for hh in range(8):
                ACTF(REf[:, hh, :], psb[bf_[hh // 4]][:, (hh % 4) * 128:(hh % 4 + 1) * 128], AF.Exp,
                     [("ps", bf_[hh // 4]), sct], ["REf"], bias=SC[:, 2, c, hh:hh + 1], scale=1.0)
                ACTF(REb[:, hh, :], psb[bb_[hh // 4]][:, (hh % 4) * 128:(hh % 4 + 1) * 128], AF.Exp,
                     [("ps", bb_[hh // 4]), sct], ["REb"], bias=SC[:, 2, c, 8 + hh:9 + hh], scale=1.0)
            ps_release(*bf_)
            ps_release(*bb_)
            bg_ = ps_alloc(hold=True)
            for g in range(2):
                MM(bg_, psb[bg_][:, g * 128:(g + 1) * 128], BC[:, g, cs], BC[:, 2 + g, cs], True, True,
                   [("BC", g), ("BC", 2 + g)])
            TTO("dve", REf, REf, REb, ALU.add, ["REf", "REb"], ["REf"])
            for g in range(2):
                TTO("dve", Mc[:, 4 * g:4 * g + 4, :], REf[:, 4 * g:4 * g + 4, :],
                    psb[bg_][:, g * 128:(g + 1) * 128].unsqueeze(1).broadcast_to([128, 4, 128]), ALU.mult,
                    ["REf", ("ps", bg_)], [("Mb", c % 2)])
            ps_release(bg_)

        def stageQ(c):
            cs = slice(c * 128, (c + 1) * 128)
            Mc = Mbs[c % 2]
            if c == 8:
                MEMSET("dve", Hfc[:, :], 0.0, [], ["Hfc"])
            tok_major(c)
            ACTF(Hfb[:, :], Hfc[:, :], AF.Identity, ["Hfc"], ["Hfb"])
            bwf = ps_alloc(hold=True)
            bwb = ps_alloc(hold=True)
            for g in range(2):
                MM(bwf, psb[bwf][:, g * 256:(g + 1) * 256], BC[:, 2 + g, cs], Hfb[:, g * 256:(g + 1) * 256], True, True,
                   [("BC", 2 + g), "Hfb"])
                MM(bwb, psb[bwb][:, g * 256:(g + 1) * 256], BC[:, 2 + g, cs], Hb[:, c, g * 256:(g + 1) * 256], True, True,
                   [("BC", 2 + g), ("Hb", c)])
            TTO("dve", v3(tmp[0][:, 0:512]), v3(psb[bwf][:, 0:512]), bc8(SC[:, 4, c, 0:8]), ALU.mult,
                [("ps", bwf), sct], tmp0t)
            TTO("dve", v3(tmp[1][:, 0:512]), v3(psb[bwb][:, 0:512]), bc8(SC[:, 4, c, 8:16]), ALU.mult,
                [("ps", bwb), sct], tmp1t)
            ps_release(bwf, bwb)
            TTO("dve", Wtok[:, :], tmp[0][:, 0:512], tmp[1][:, 0:512], ALU.add, tmp0t + tmp1t, ["Wtok"])
            by = [ps_alloc(hold=True) for _ in range(2)]
            for m in range(4):
                for hh in range(2):
                    hd = 2 * m + hh
                    col = ((m % 2) * 2 + hh) * 128
                    bk = by[m // 2]
                    MM(bk, psb[bk][:, col:col + 128], Xt[:, m * 128:(m + 1) * 128], Mc[:, hd, :], True, False,
                       ["Xt", ("Mb", c % 2)])
                    MM(bk, psb[bk][:, col:col + 128], Wtok[:, m * 128:(m + 1) * 128], identb[:], False, True,
                       ["Wtok", "identb"])
            grp_s["bk"] = chunk_state(c, 0)
            return by

        def stageQ2(c, by):
            cs = slice(c * 128, (c + 1) * 128)
            yz = tmp[0][:, 0:512].rearrange("p (m i) -> p m i", m=4)
            for m in range(4):
                for hh in range(2):
                    r0 = hh * 64
                    col = ((m % 2) * 2 + hh) * 128
                    bk = by[m // 2]
                    STT(yz[r0:r0 + 64, m, :], XS[r0:r0 + 64, m, cs], ev(32 + m)[r0:r0 + 64, :],
                        psb[bk][r0:r0 + 64, col:col + 128], ALU.mult, ALU.add, [("XS", m), ("ps", bk), "dsk"], tmp0t)
            ps_release(*by)
            TTO("dve", yz, yz, Zb[:, :, cs], ALU.mult, tmp0t + [("Zb", m) for m in range(4)], tmp0t)
            sqv = sq[1][:, 0:512].rearrange("p (m i) -> p m i", m=4)
            ACTF(sq[1][:, 0:512], tmp[0][:, 0:512], AF.Square, tmp0t, [("sq", 1)])
            bn = ps_alloc(hold=True)
            for m in range(4):
                MM(bn, psb[bn][:, 0:128], ones_bf[:], sqv[:, m, :], m == 0, m == 3, [("sq", 1), "ones"])
            ACTF(tmp[1][:, 0:128], psb[bn][:, 0:128], AF.Ln, [("ps", bn), "eps"], tmp1t, bias=eps_t[:], scale=1.0 / 512)
            ps_release(bn)
            ACTF(tmp[1][:, 0:128], tmp[1][:, 0:128], AF.Exp, tmp1t, tmp1t, scale=-0.5)
            for m in range(4):
                STT(Zb[:, m, cs], yz[:, m, :], ev(40 + m), tmp[1][:, 0:128], ALU.mult, ALU.mult,
                    tmp0t + tmp1t + ["evv"], [("Zb", m)])
            bk = grp_s["bk"]
            TTO("dve", v3(Hfc[:, :]), v3(Hfc[:, :]), bc8(SC[:, 5, c, 0:8]), ALU.mult, ["Hfc", sct], ["Hfc"])
            TTO("dve", Hfc[:, :], Hfc[:, :], psb[bk][:, 0:512], ALU.add, ["Hfc", ("ps", bk)], ["Hfc"])
            ps_release(bk)
            if c % 2 == 1:
                out_evs.append(DMA("sp", ssdT_d[i, seq_of(c), 0], Hfc[:, :], ["Hfc"], [], "o_Hfc"))
                if c in (1, 3, 5):
                    TS("dve", Hfc[:, :], Hfc[:, :], flagA, None, ALU.mult, None, ["Hfc", "flags"], ["Hfc"])

        DMA("sp", Hfc[:, :], h0T_d[i][:, 0, :], [], ["Hfc"], "mH")
        PIPE = cfg.get('ev_pipe', False)
        if cfg.get('ev_p3', True) and PIPE:
            stageP(0)
        for c in (range(10) if cfg.get('ev_p3', True) else []):
            if PIPE:
                by_ = stageQ(c)
                if c + 1 < 10:
                    stageP(c + 1)
                stageQ2(c, by_)
            else:
                stageP(c)
                by_ = stageQ(c)
                stageQ2(c, by_)

        WO = WStream(weout_d[i], LAY_OUT)
        mreads = [("Ub", m) for m in range(4)] + [("Zb", m) for m in range(4)]
        for j in range(8):
            banks = proj(WO, j, 8, lambda kc, t0, tn: (Ub[:, kc, t0:t0 + tn] if kc < 4 else Zb[:, kc - 4, t0:t0 + tn]),
                         mreads, 128, TT)
            for ti, (t0, tn) in enumerate(TT):
                s_ = SLOT_OF_TT[ti]
                STT(x[:, j, t0:t0 + tn], psb[banks[ti]][:, 0:tn], G_ap(l, 1, j, s_), x[:, j, t0:t0 + tn],
                    ALU.mult, ALU.add, [("ps", banks[ti]), ("gate", l % 2, 1), ("x", j)], [("x", j)])
            ps_release(*banks)
            if j == 4:
                rms_begin()
                for jj in range(4):
                    rms_chunk(jj)
            elif j > 4:
                rms_chunk(j - 1)
        rms_chunk(7)
        fence()

    bg_start(mod_layer_gen(0), 1)
    for l in range(nlayers):
        if l == 0:
            bg_ensure(0, 0)
            bg_limit[0] = 3
        else:
            bg_limit[0] = 3
        ffn(l, 0)
        if l % 2 == 1 and cfg.get("odd", True):
            odd_mixer(l)
        if l % 2 == 0 and cfg.get("even", True):
            even_mixer(l)
        bg_flush()
        if l + 1 < nlayers:
            bg_start(mod_layer_gen(l + 1), 2)
        ffn(l, 1)

    rms_stats()
    for kc in range(8):
        tb = tmp[kc % 2]
        S.op("dve", lambda e, kc=kc, tb=tb: e.scalar_tensor_tensor(
            out=tb[:, 0:NT], in0=x[:, kc, :], scalar=gfin[:, kc:kc + 1], in1=rstd[:], op0=ALU.mult, op1=ALU.mult),
            reads=[("x", kc), ("rstd", 0), ("rstd", 1), ("rstd", 2), "gfin"], writes=[("tmp", kc % 2, 0), ("tmp", kc % 2, 1)])
        ev = S.op("sp", lambda e, kc=kc, tb=tb: e.dma_start(out=yT_d[kc], in_=tb[:, 0:NT]),
                  reads=[("tmp", kc % 2, 0), ("tmp", kc % 2, 1)], dma="o_tmp%d" % (kc % 2))
        out_evs.append(ev)
    S.wait_all("sp", out_evs)

    S.emit(nc, stack)
    stack.close()
    return nc


def _core_tokens(core, x_prompt, x_sample):
    if core < 2:
        a = x_sample[core]
        b = x_prompt[core]
    else:
        p0 = 2 + (core - 2) * 5
        a = x_prompt[p0:p0 + 4].reshape(1024, D)
        b = x_prompt[p0 + 4]
    return np.concatenate([a, b], axis=0)


def _prep_shared(inp):
    f = np.float32
    sh = {}
    w_mod = np.asarray(inp["w_mod"], f)
    sh["wmod"] = np.ascontiguousarray(
        w_mod.reshape(DEPTH, 8, 128, 36, 2, 128).transpose(0, 3, 2, 4, 1, 5)).reshape(DEPTH, 36, 128, 2048)
    b_mod = np.asarray(inp["b_mod"], f)
    sh["bmod"] = np.ascontiguousarray(b_mod.reshape(DEPTH, 72, 128).transpose(2, 0, 1)).reshape(128, DEPTH * 72)
    g_norm = np.asarray(inp["g_norm"], f)
    sh["gnorm"] = np.ascontiguousarray(g_norm.reshape(DEPTH, 3, 8, 128).transpose(3, 0, 1, 2)).reshape(128, DEPTH * 24)
    sh["gfin"] = np.ascontiguousarray(np.asarray(inp["g_final"], f).reshape(8, 128).T)
    w_gu = np.asarray(inp["w_ff_gu"], f)
    sh["wgu"] = np.ascontiguousarray(
        w_gu.reshape(DEPTH, 2, 8, 128, 2, 11, 2, 128).transpose(0, 1, 5, 3, 4, 6, 2, 7)).reshape(DEPTH, 2, 11, 128, 4096)
    w_dn = np.asarray(inp["w_ff_down"], f)
    sh["wdn"] = np.ascontiguousarray(
        w_dn.reshape(DEPTH, 2, NFC, 128, 8, 128).transpose(0, 1, 4, 3, 2, 5)).reshape(DEPTH, 2, 8, 128, 2816)
    w_in_odd = np.asarray(inp["w_in_odd"], f)
    w_uq = np.asarray(inp["w_uq"], f)
    w_ukv = np.asarray(inp["w_ukv"], f)
    w_out_odd = np.asarray(inp["w_out_odd"], f)
    sh["woin"] = np.stack([_pack(w_in_odd[i], _odd_in_blocks()) for i in range(2)])
    sh["wuq"] = np.stack([_pack(w_uq[i], _uq_blocks()) for i in range(2)])
    sh["wukv"] = np.stack([_pack(w_ukv[i], _ukv_blocks()) for i in range(2)])
    sh["woout"] = np.stack([_pack(w_out_odd[i], _out_blocks()) for i in range(2)])
    ov = np.zeros((128, 2, NOV), f)
    for i in range(2):
        ov[:, i, 0:3] = np.asarray(inp["g_cq"], f)[i].reshape(3, 128).T
        ov[:, i, 3:5] = np.asarray(inp["g_ckv"], f)[i].reshape(2, 128).T
        wdw = np.asarray(inp["w_dwconv"], f)[i]
        ov[:, i, 5:129] = wdw.reshape(31, 4, 128).transpose(2, 1, 0).reshape(128, 124)
        ov[:, i, 129:133] = np.asarray(inp["b_dwconv"], f)[i].reshape(4, 128).T
        ov[:, i, 133:137] = np.asarray(inp["g_conv_ln"], f)[i].reshape(4, 128).T
        ov[:, i, 137:141] = np.asarray(inp["b_conv_ln"], f)[i].reshape(4, 128).T
    sh["oddv"] = np.ascontiguousarray(ov.reshape(128, 2 * NOV))
    w_in_even = np.asarray(inp["w_in_even"], f)
    w_out_even = np.asarray(inp["w_out_even"], f)
    sh["wein"] = np.stack([_pack(w_in_even[i], _even_in_blocks()) for i in range(2)])
    sh["weout"] = np.stack([_pack(w_out_even[i], _out_blocks()) for i in range(2)])
    evv = np.zeros((128, 2, NEV), f)
    evb = np.zeros((2, 128, NEB), f)
    for i in range(2):
        wc = np.asarray(inp["w_conv_ssm"], f)[i]
        evv[:, i, 0:24] = wc.reshape(3, 8, 128).transpose(2, 1, 0).reshape(128, 24)
        evv[:, i, 24:32] = np.asarray(inp["b_conv_ssm"], f)[i].reshape(8, 128).T
        dsk = np.asarray(inp["d_skip"], f)[i]
        for m in range(4):
            evv[0:64, i, 32 + m] = dsk[0, 2 * m]
            evv[64:128, i, 32 + m] = dsk[0, 2 * m + 1]
            evv[0:64, i, 36 + m] = dsk[1, 2 * m]
            evv[64:128, i, 36 + m] = dsk[1, 2 * m + 1]
        evv[:, i, 40:44] = np.asarray(inp["g_ssm_out"], f)[i].reshape(4, 128).T
        evb[i, :, 0:512] = np.asarray(inp["g_gmlp_v"], f)[i][None, :]
        evb[i, :, 512:1024] = np.asarray(inp["b_spatial"], f)[i].reshape(1, 512)
        evb[i, :, 1024:1040] = np.asarray(inp["dt_bias"], f)[i].reshape(1, 16)
        evb[i, :, 1040:1056] = np.asarray(inp["a_log"], f)[i].reshape(1, 16)
    sh["evv"] = np.ascontiguousarray(evv.reshape(128, 2 * NEV))
    sh["evb"] = evb
    ws = np.asarray(inp["w_spatial"], f)
    sh["wsT"] = np.ascontiguousarray(ws.transpose(0, 3, 1, 2)).reshape(2, 128, 512)
    jj = np.arange(128)[:, None]
    ii = np.arange(128)[None, :]
    U = (jj <= ii).astype(f)
    L = (jj >= ii).astype(f)
    cst = np.stack([U, L, NEG * (1 - U), NEG * (1 - L), np.eye(128, dtype=f), np.ones((128, 128), f)], axis=1)
    sh["cst"] = np.ascontiguousarray(cst.reshape(128, 6 * 128))
    return sh


def _rope_tables():
    f = np.float32
    t = np.arange(1024)
    row = (t // 64).astype(f)
    col = (t % 64).astype(f)
    freqs = (np.float32(10000.0) ** (-np.arange(8, dtype=f) / np.float32(8))).astype(f)
    ang = np.stack([row[:, None] * freqs, col[:, None] * freqs], axis=1)
    cos = np.cos(ang).astype(f)
    sin = np.sin(ang).astype(f)
    C = np.zeros((32, 1024), f)
    Sg = np.zeros((32, 1024), f)
    for a in range(2):
        for r in range(2):
            for q in range(8):
                d = a * 16 + r * 8 + q
                C[d] = cos[:, a, q]
                Sg[d] = (-sin[:, a, q]) if r == 0 else sin[:, a, q]
    return np.concatenate([C, C], 0), np.concatenate([Sg, Sg], 0)


def _prep_core(core, inp):
    f = np.float32
    m = {}
    toks = _core_tokens(core, np.asarray(inp["x_prompt"], f), np.asarray(inp["x_sample"], f))
    m["xT"] = np.ascontiguousarray(toks.T).reshape(8, 128, NT)
    c_ctx = np.asarray(inp["c_ctx"], f)
    cA = np.asarray(inp["c"], f)[core] if core < 2 else c_ctx
    cv = np.stack([cA, c_ctx], axis=-1)
    m["cvec"] = np.ascontiguousarray(cv.reshape(8, 128, 2).transpose(1, 0, 2)).reshape(128, 16)
    is_s = core < 2
    fl = np.zeros((128, 4), f)
    fl[:, 0] = 1.0 if is_s else 0.0
    fl[:, 1] = 0.0 if is_s else NEG
    m["flags"] = fl
    if is_s:
        C, Sg = _rope_tables()
        m["rope"] = np.ascontiguousarray(np.stack([C, Sg]))
        ck = np.asarray(inp["cache_mla_ckv"], f)[core]
        m["cckv"] = np.ascontiguousarray(ck.reshape(2, 256, 2, 128).transpose(0, 3, 2, 1))
        kr = np.asarray(inp["cache_mla_krope"], f)[core]
        krT = kr.transpose(0, 2, 1)
        m["ckr"] = np.ascontiguousarray(np.concatenate([krT, krT], axis=1))
        st = np.asarray(inp["state_ssd"], f)[core]
        m["h0T"] = np.ascontiguousarray(st.transpose(0, 4, 1, 2, 3)).reshape(2, 128, 2, 512)
    else:
        m["h0T"] = np.zeros((2, 128, 2, 512), f)
        m["rope"] = np.ascontiguousarray(np.stack([np.ones((64, 1024), f), np.zeros((64, 1024), f)]))
        m["cckv"] = np.zeros((2, 128, 2, 256), f)
        m["ckr"] = np.zeros((2, 64, 256), f)
    return m


_NC_CACHE = {}


def kernel(**inputs):
    if "nc" not in _NC_CACHE:
        _NC_CACHE["nc"] = build_program()
    nc = _NC_CACHE["nc"]
    shared = _prep_shared(inputs)
    in_maps = []
    for core in range(NCORES):
        m = dict(shared)
        m.update(_prep_core(core, inputs))
        in_maps.append(m)
    res = run_bass_kernel_spmd(nc, in_maps, core_ids=list(range(NCORES)))
    return _gather(res.results, inputs)


def _gather(results, inputs):
    f = np.float32
    y_prompt = np.zeros((32, 256, D), f)
    y_sample = np.zeros((2, 1024, D), f)
    for core in range(NCORES):
        y = results[core]["yT"].reshape(D, NT).T
        if core < 2:
            y_sample[core] = y[:1024]
            y_prompt[core] = y[1024:]
        else:
            p0 = 2 + (core - 2) * 5
            y_prompt[p0:p0 + 4] = y[:1024].reshape(4, 256, D)
            y_prompt[p0 + 4] = y[1024:]
    new_ssd = np.zeros((32, 2, 2, 8, 64, 128), f)
    new_ckv = np.zeros((32, 2, 256, 256), f)
    new_kr = np.zeros((32, 2, 256, 32), f)
    for core in range(NCORES):
        ck = results[core]["ckvT"].reshape(2, 256, NT).transpose(0, 2, 1)
        kr = results[core]["krT"].reshape(2, 32, NT).transpose(0, 2, 1)
        if core < 2:
            seqs = [(core, 1024)]
        else:
            p0 = 2 + (core - 2) * 5
            seqs = [(p0 + j, j * 256) for j in range(4)] + [(p0 + 4, 1024)]
        for (b, t0) in seqs:
            new_ckv[b] = ck[:, t0:t0 + 256]
            new_kr[b] = kr[:, t0:t0 + 256]
        sd = results[core]["ssdT"].reshape(2, 5, 2, 128, 8, 64)
        for (b, t0) in seqs:
            new_ssd[b] = sd[:, t0 // 256].transpose(0, 1, 3, 4, 2)
    return (y_prompt, y_sample, new_ssd, new_ckv, new_kr)
```

```python
from contextlib import ExitStack
import numpy as np
import concourse.bass as bass
import concourse.mybir as mybir
from concourse.bass_utils import run_bass_kernel_spmd

F32 = mybir.dt.float32
F32R = mybir.dt.float32r
BF16 = mybir.dt.bfloat16
AF = mybir.ActivationFunctionType
ALU = mybir.AluOpType

D = 1024
DEPTH = 4
NT = 1280
TT = [(0, 512), (512, 512), (1024, 256)]
SLOT_OF_TT = [0, 0, 1]
SLOTS = [(0, 1024), (1024, 256)]
DFF = 2816
NFC = 22
EPS = 1e-6
NCORES = 8
NWSLOT = 3
WSLOT = 4096


def _layout(KC, ns):
    lay = []
    t = 0
    off = 0
    for n in ns:
        sz = KC * n
        if off + sz > WSLOT:
            t += 1
            off = 0
        lay.append((t, off, KC, n))
        off += sz
    return lay, t + 1


def _pack(w, blocks):
    KC = w.shape[0] // 128
    lay, nt = _layout(KC, [len(b) for b in blocks])
    out = np.zeros((nt, 128, WSLOT), np.float32)
    for (t, off, _, n), cols in zip(lay, blocks):
        blk = w[:, cols].reshape(KC, 128, n).transpose(1, 0, 2).reshape(128, KC * n)
        out[t, :, off:off + KC * n] = blk
    return out


def _ar(a, b):
    return list(range(a, b))


def _rope_partner(cols):
    c = np.asarray(cols).reshape(2, 2, 8)
    return list(c[:, ::-1, :].reshape(-1))


def _odd_in_blocks():
    kr = _ar(640, 672)
    blocks = [_ar(0, 128), _ar(128, 256), _ar(256, 384), kr + kr, _rope_partner(kr) * 2,
              _ar(384, 512), _ar(512, 640)]
    for c in range(4):
        blocks.append(_ar(672 + c * 128, 672 + (c + 1) * 128))
        blocks.append(_ar(1184 + c * 128, 1184 + (c + 1) * 128))
    return blocks


def _uq_blocks():
    blocks = []
    for m in range(4):
        blocks.append(_ar(2 * m * 96, 2 * m * 96 + 64) + _ar((2 * m + 1) * 96, (2 * m + 1) * 96 + 64))
    for m in range(4):
        blocks.append(_ar(2 * m * 96 + 64, 2 * m * 96 + 96) + _ar((2 * m + 1) * 96 + 64, (2 * m + 1) * 96 + 96))
    for m in range(4):
        blocks.append(_rope_partner(_ar(2 * m * 96 + 64, 2 * m * 96 + 96))
                      + _rope_partner(_ar((2 * m + 1) * 96 + 64, (2 * m + 1) * 96 + 96)))
    return blocks


def _ukv_blocks():
    blocks = []
    for m in range(4):
        blocks.append(_ar(2 * m * 128, 2 * m * 128 + 64) + _ar((2 * m + 1) * 128, (2 * m + 1) * 128 + 64))
    v = []
    for hh in range(8):
        v += _ar(hh * 128 + 64, hh * 128 + 128)
    blocks.append(v)
    return blocks


def _out_blocks():
    return [_ar(j * 128, (j + 1) * 128) for j in range(8)]


LAY_OIN, NT_OIN = _layout(8, [len(b) for b in _odd_in_blocks()])
LAY_UQ, NT_UQ = _layout(3, [len(b) for b in _uq_blocks()])
LAY_UKV, NT_UKV = _layout(2, [len(b) for b in _ukv_blocks()])
LAY_OUT, NT_OUT = _layout(8, [128] * 8)
NOV = 3 + 2 + 124 + 4 + 4 + 4


def _even_in_blocks():
    blocks = [_ar(c * 128, (c + 1) * 128) for c in range(4)]
    blocks += [_ar(1024 + c * 128, 1024 + (c + 1) * 128) for c in range(4)]
    blocks += [_ar(1536 + c * 128, 1536 + (c + 1) * 128) for c in range(8)]
    blocks.append(_ar(512, 1024))
    blocks.append(_ar(2560, 2576))
    return blocks


LAY_EIN, NT_EIN = _layout(8, [len(b) for b in _even_in_blocks()])
NEV = 24 + 8 + 4 + 4 + 4
NEB = 512 + 512 + 16 + 16
NEG = -30000.0

class Instr:
    __slots__ = ("fn", "waits", "signal", "dma")

    def __init__(self, fn, dma=None):
        self.fn = fn
        self.waits = []
        self.signal = False
        self.dma = dma


class Sched:
    ENGS = ("pe", "act", "dve", "pool", "sp")
    STRICT = 2
    CAP = 3000

    def __init__(self):
        self.q = {e: [] for e in self.ENGS}
        self.clock = {e: {} for e in self.ENGS}
        self.evclock = {}
        self.tok = {}
        self.dma_cnt = {}
        self.total_keys = set()

    def op(self, eng, fn, reads=(), writes=(), dma=None):
        q = self.q[eng]
        idx = len(q)
        ins = Instr(fn, dma)
        deps = set()
        raw = set()
        for t in reads:
            w = self.tok.get(t)
            if w is not None and w[0] is not None:
                deps.add(w[0])
                raw.add(w[0])
        for t in writes:
            w = self.tok.get(t)
            if w is not None:
                if w[0] is not None:
                    deps.add(w[0])
                deps.update(w[1])
        ck = self.clock[eng]
        for ev in sorted(deps, key=lambda e: (e[0], str(e[1]), -e[2])):
            kind, a, b = ev
            if kind == "e" and a == eng and Sched.STRICT != 1:
                if eng == "pe":
                    continue
                if Sched.STRICT == 0:
                    if ev not in raw:
                        continue
                    if idx - b > 3:
                        continue
            key = (kind, a)
            if ck.get(key, -1) >= b:
                continue
            ins.waits.append(ev)
            for k, v in self.evclock[ev].items():
                if ck.get(k, -1) < v:
                    ck[k] = v
            if kind == "e":
                self.q[a][b].signal = True
        if dma is not None:
            n = self.dma_cnt.get(dma, 0) + 1
            self.dma_cnt[dma] = n
            ev = ("d", dma, n)
        else:
            ev = ("e", eng, idx)
        ck2 = dict(ck)
        ck2[(ev[0], ev[1])] = ev[2]
        if dma is None:
            ck2[("e", eng)] = idx
        self.evclock[ev] = ck2
        q.append(ins)
        for t in reads:
            w = self.tok.setdefault(t, [None, []])
            w[1].append(ev)
        for t in writes:
            self.tok[t] = [ev, []]
        return ev

    def wait_all(self, eng, evs):
        ins = Instr(None)
        for ev in evs:
            ins.waits.append(ev)
            if ev[0] == "e":
                self.q[ev[1]][ev[2]].signal = True
        self.q[eng].append(ins)

    def emit(self, nc, stack):
        CAP = Sched.CAP
        sig = {}
        nsig = {}
        for e in self.ENGS:
            c = 0
            arr = []
            for ins in self.q[e]:
                if ins.signal and ins.dma is None:
                    c += 1
                arr.append(c)
            sig[e] = arr
            nsig[e] = c
        esem = {e: [stack.enter_context(nc.semaphore("s_%s%d" % (e, k))) for k in range(nsig[e] // CAP + 1)]
                for e in self.ENGS}
        dsem = {k: stack.enter_context(nc.semaphore("d_%s" % (k,))) for k in self.dma_cnt}
        for k, v in self.dma_cnt.items():
            assert 16 * v < 4000, (k, v)
        block = stack.enter_context(nc.Block())
        q = self.q
        total = self.total_keys
        dma_cnt = self.dma_cnt

        def run(ename, eng):
            for i, ins in enumerate(q[ename]):
                for (kind, a, b) in ins.waits:
                    if kind == "e":
                        c = sig[a][b]
                        eng.wait_ge(esem[a][(c - 1) // CAP], (c - 1) % CAP + 1)
                    else:
                        n = dma_cnt[a] if a in total else b
                        eng.wait_ge(dsem[a], 16 * n)
                if ins.fn is None:
                    continue
                r = ins.fn(eng)
                if ins.dma is not None:
                    r.then_inc(dsem[ins.dma], 16)
                elif ins.signal:
                    c = sig[ename][i]
                    r.then_inc(esem[ename][(c - 1) // CAP], 1)

        @block.tensor
        def _(e):
            run("pe", e)

        @block.scalar
        def _(e):
            run("act", e)

        @block.vector
        def _(e):
            run("dve", e)

        @block.gpsimd
        def _(e):
            run("pool", e)

        @block.sync
        def _(e):
            run("sp", e)


def build_program(cfg=None):
    cfg = cfg or {}
    nlayers = cfg.get("nlayers", DEPTH)
    do_mixer = cfg.get("mixer", True)
    nc = bass.Bass("TRN2", target_bir_lowering=False)
    S = Sched()
    stack = ExitStack()
    dram = {}

    def din(name, shape, dt=F32):
        dram[name] = nc.dram_tensor(name, list(shape), dt, kind="ExternalInput").ap()
        return dram[name]

    def dout(name, shape, dt=F32):
        dram[name] = nc.dram_tensor(name, list(shape), dt, kind="ExternalOutput").ap()
        return dram[name]

    def sb(name, shape, dt=F32):
        return stack.enter_context(nc.sbuf_tensor(name, list(shape), dt))

    xT_d = din("xT", [8, 128, NT])
    cvec_d = din("cvec", [128, 16])
    wmod_d = din("wmod", [DEPTH, 36, 128, 2048])
    bmod_d = din("bmod", [128, DEPTH * 72])
    gnorm_d = din("gnorm", [128, DEPTH * 3 * 8])
    gfin_d = din("gfin", [128, 8])
    wgu_d = din("wgu", [DEPTH, 2, 11, 128, 4096])
    wdn_d = din("wdn", [DEPTH, 2, 8, 128, 2816])
    yT_d = dout("yT", [8, 128, NT])
    flags_d = din("flags", [128, 4])
    woin_d = din("woin", [2, NT_OIN, 128, WSLOT])
    wuq_d = din("wuq", [2, NT_UQ, 128, WSLOT])
    wukv_d = din("wukv", [2, NT_UKV, 128, WSLOT])
    woout_d = din("woout", [2, NT_OUT, 128, WSLOT])
    oddv_d = din("oddv", [128, 2 * NOV])
    rope_d = din("rope", [2, 64, 1024])
    cckv_d = din("cckv", [2, 128, 2, 256])
    ckr_d = din("ckr", [2, 64, 256])
    ckvT_d = dout("ckvT", [2, 2, 128, NT])
    wein_d = din("wein", [2, NT_EIN, 128, WSLOT])
    weout_d = din("weout", [2, NT_OUT, 128, WSLOT])
    evv_d = din("evv", [128, 2 * NEV])
    evb_d = din("evb", [2, 128, NEB])
    wsT_d = din("wsT", [2, 128, 512])
    cst_d = din("cst", [128, 6 * 128])
    h0T_d = din("h0T", [2, 128, 2, 512])
    ssdT_d = dout("ssdT", [2, 5, 2, 128, 512])
    krT_d = dout("krT", [2, 32, NT])

    x = sb("x", [128, 8, NT], F32)
    h = sb("h", [128, 8, NT], BF16)
    big = sb("big", [128, 14336], F32)
    act = big[:, 0:14080].bitcast(BF16).rearrange("p (j t) -> p j t", j=NFC)
    wbuf = [sb("wbuf%d" % i, [128, WSLOT], BF16) for i in range(NWSLOT)]
    rstd = sb("rstd", [128, NT], F32)
    sq = [sb("sq%d" % i, [128, NT], BF16) for i in range(2)]
    tmp = [sb("tmp%d" % i, [128, 1440], F32) for i in range(2)]
    sg = [sb("sg%d" % i, [128, NT], BF16) for i in range(2)]
    ones_bf = sb("ones_bf", [128, 128], BF16)
    cvec = sb("cvec_sb", [128, 16], F32)
    csil = sb("csil", [128, 16], BF16)
    bmod = sb("bmod_sb", [128, DEPTH * 72], F32)
    gnorm = sb("gnorm_sb", [128, DEPTH * 24], F32)
    gfin = sb("gfin_sb", [128, 8], F32)
    modv = [sb("modv%d" % i, [128, 144], F32) for i in range(2)]
    amul = [sb("amul%d" % i, [128, 48], F32) for i in range(2)]
    gate = [sb("gate%d" % i, [128, 48], F32) for i in range(2)]
    eps_t = sb("eps_t", [128, 1], F32)
    flags = sb("flags_sb", [128, 4], F32)
    oddv = sb("oddv_sb", [128, 2 * NOV], F32)
    rstd2 = sb("rstd2", [128, NT], F32)
    ext = sb("ext", [128, 6400], F32)
    evv = sb("evv_sb", [128, 2 * NEV], F32)
    cst = sb("cst_sb", [128, 6 * 128], F32)
    identb = sb("identb", [128, 128], BF16)
    one_t = sb("one_t", [128, 1], F32)
    negub = sb("negub", [128, 256], BF16)
    aneg = sb("aneg", [128, 16], F32)
    s160 = [sb("s160_%d" % i, [128, 160], F32) for i in range(3)]
    sml = sb("sml", [128, 4], F32)
    fence_t = sb("fence_t", [128, 2], F32)
    psb = [stack.enter_context(nc.psum_tensor("ps%d" % i, [128, 512], F32)) for i in range(8)]

    ps_rr = [0]

    ps_held = set()

    def ps_alloc(hold=False):
        while True:
            b = ps_rr[0] % 8
            ps_rr[0] += 1
            if b not in ps_held:
                break
        if hold:
            ps_held.add(b)
        return b

    def ps_release(*bs):
        for b in bs:
            ps_held.discard(b)

    w_rr = [0]

    def load_w(src_ap, n):
        slot = w_rr[0] % NWSLOT
        w_rr[0] += 1
        S.op("pool", lambda e, slot=slot, src_ap=src_ap, n=n: e.dma_start(out=wbuf[slot][:, 0:n], in_=src_ap),
             writes=[("w", slot)], dma="w%d" % slot)
        return slot

    S.op("sp", lambda e: e.dma_start(out=cvec[:], in_=cvec_d), writes=["cvec"], dma="in")
    S.op("sp", lambda e: e.dma_start(out=bmod[:], in_=bmod_d), writes=["bmod"], dma="in")
    S.op("sp", lambda e: e.dma_start(out=gnorm[:], in_=gnorm_d), writes=["gnorm"], dma="in")
    S.op("sp", lambda e: e.dma_start(out=gfin[:], in_=gfin_d), writes=["gfin"], dma="in")
    for kc in range(8):
        S.op("sp", lambda e, kc=kc: e.dma_start(out=x[:, kc, :], in_=xT_d[kc]), writes=[("x", kc)], dma="in")
    S.op("sp", lambda e: e.dma_start(out=flags[:], in_=flags_d), writes=["flags"], dma="in")
    S.op("sp", lambda e: e.dma_start(out=oddv[:], in_=oddv_d), writes=["oddv"], dma="in")
    S.op("sp", lambda e: e.dma_start(out=evv[:], in_=evv_d), writes=["evv"], dma="in")
    S.op("sp", lambda e: e.dma_start(out=cst[:], in_=cst_d), writes=["cst"], dma="in")
    S.total_keys.add("in")
    S.op("dve", lambda e: e.memset(ones_bf[:], 1.0), writes=["ones"])
    S.op("dve", lambda e: e.memset(eps_t[:], EPS), writes=["eps"])
    S.op("dve", lambda e: e.memset(one_t[:], 1.0), writes=["one"])
    S.op("act", lambda e: e.activation(out=identb[:], in_=cst[:, 4 * 128:5 * 128], func=AF.Identity),
         reads=["cst"], writes=["identb"])
    S.op("act", lambda e: e.activation(out=negub[:], in_=cst[:, 2 * 128:4 * 128], func=AF.Identity),
         reads=["cst"], writes=["negub"])
    S.op("act", lambda e: e.activation(out=csil[:], in_=cvec[:], func=AF.Silu), reads=["cvec"], writes=["csil"])

    wm_rr = [0]
    bg = [None]

    bg_subs = [0]
    bg_limit = [0]

    def bg_start(gen, limit):
        bg[0] = gen
        bg_subs[0] = 0
        bg_limit[0] = limit

    def bg_step(n=1):
        for _ in range(n):
            if bg[0] is None or bg_subs[0] >= bg_limit[0]:
                return
            try:
                if next(bg[0]) == "sub":
                    bg_subs[0] += 1
                    if bg_subs[0] >= 3:
                        bg[0] = None
            except StopIteration:
                bg[0] = None

    def bg_flush():
        bg_limit[0] = 3
        while bg[0] is not None:
            bg_step(1)

    bg_done = set()

    def bg_ensure(l, sub):
        bg_limit[0] = max(bg_limit[0], sub + 1)
        while (l, sub) not in bg_done and bg[0] is not None:
            bg_step(1)

    def mod_layer_gen(l):
        mv = modv[l % 2]
        am = amul[l % 2]
        gt = gate[l % 2]
        mv3 = mv[:, :].rearrange("p (j s) -> p j s", s=2)
        NMS = 6
        for sub in range(3):
            pb = ps_alloc(hold=True)
            for t in range(12 * sub, 12 * (sub + 1)):
                slot = wm_rr[0] % NMS
                wm_rr[0] += 1
                wsl = ext[:, slot * 1024:(slot + 1) * 1024].bitcast(BF16)
                S.op("pool", lambda e, wsl=wsl, t=t: e.dma_start(out=wsl, in_=wmod_d[l, t]),
                     reads=["bigown"], writes=[("wm", slot)], dma="wm%d_%d" % (slot, l))
                for fcl in range(2):
                    j = 2 * t + fcl
                    for kc in range(8):
                        o = (fcl * 8 + kc) * 128
                        S.op("pe", lambda e, wsl=wsl, j=j, kc=kc, o=o, pb=pb: e.matmul(
                            psb[pb][:, 2 * j:2 * j + 2], lhsT=wsl[:, o:o + 128],
                            rhs=csil[:, 2 * kc:2 * kc + 2], start=(kc == 0), stop=(kc == 7)),
                            reads=[("wm", slot), "csil", "bigown"], writes=[("ps", pb)])
                yield
            j0, j1 = 24 * sub, 24 * (sub + 1)
            bm = bmod[:, l * 72 + j0:l * 72 + j1]
            for s in range(2):
                S.op("dve", lambda e, s=s, pb=pb, mv=mv, bm=bm, j0=j0, j1=j1: e.tensor_tensor(
                    out=mv[:, 2 * j0:2 * j1].rearrange("p (j s) -> p s j", s=2)[:, s, :],
                    in0=psb[pb][:, 2 * j0:2 * j1].rearrange("p (j s) -> p s j", s=2)[:, s, :],
                    in1=bm, op=ALU.add),
                    reads=[("ps", pb), "bmod"], writes=[("modv", l % 2, sub)])
            g_ap = gnorm[:, (l * 3 + sub) * 8:(l * 3 + sub + 1) * 8]
            for s in range(2):
                a_out = am[:, sub * 16:(sub + 1) * 16].rearrange("p (k s) -> p k s", s=2)[:, :, s]
                g_out = gt[:, sub * 16:(sub + 1) * 16].rearrange("p (k s) -> p k s", s=2)[:, :, s]
                sc_in = mv3[:, (3 * sub + 1) * 8:(3 * sub + 2) * 8, s]
                gt_in = mv3[:, (3 * sub + 2) * 8:(3 * sub + 3) * 8, s]
                S.op("dve", lambda e, a_out=a_out, sc_in=sc_in, g_ap=g_ap: e.scalar_tensor_tensor(
                    out=a_out, in0=sc_in, scalar=1.0, in1=g_ap, op0=ALU.add, op1=ALU.mult),
                    reads=[("modv", l % 2, sub), "gnorm"], writes=[("amul", l % 2, sub)])
                gs = 1.0 if sub == 1 else 0.5
                S.op("dve", lambda e, g_out=g_out, gt_in=gt_in, gs=gs: e.tensor_scalar(
                    out=g_out, in0=gt_in, scalar1=gs, scalar2=None, op0=ALU.mult),
                    reads=[("modv", l % 2, sub)], writes=[("gate", l % 2, sub)])
            bg_done.add((l, sub))
            ps_release(pb)
            yield "sub"

    def A_ap(l, sub, kc, s):
        o = sub * 16 + kc * 2 + s
        return amul[l % 2][:, o:o + 1]

    def G_ap(l, sub, kc, s):
        o = sub * 16 + kc * 2 + s
        return gate[l % 2][:, o:o + 1]

    def B_ap(l, sub, kc, s):
        o = ((3 * sub) * 8 + kc) * 2 + s
        return modv[l % 2][:, o:o + 1]

    pend = {"banks": None, "n": 0}

    def rms_begin():
        pend["banks"] = [ps_alloc(hold=True) for _ in TT]
        pend["n"] = 0

    def rms_chunk(kc):
        banks = pend["banks"]
        n = pend["n"]
        pend["n"] = n + 1
        sqb = sq[kc % 2]
        S.op("act", lambda e: e.activation(out=sqb[:], in_=x[:, kc, :], func=AF.Square),
             reads=[("x", kc)], writes=[("sq", kc % 2)])
        for ti, (t0, tn) in enumerate(TT):
            S.op("pe", lambda e, b=banks[ti], t0=t0, tn=tn: e.matmul(
                psb[b][:, 0:tn], lhsT=ones_bf[:], rhs=sqb[:, t0:t0 + tn], start=(n == 0), stop=(n == 7)),
                reads=[("sq", kc % 2), "ones"], writes=[("ps", banks[ti])])

    def rms_stats():
        if pend["banks"] is None:
            rms_begin()
            for kc in range(8):
                rms_chunk(kc)
        assert pend["n"] == 8
        banks = pend["banks"]
        for ti, (t0, tn) in enumerate(TT):
            S.op("act", lambda e, b=banks[ti], t0=t0, tn=tn: e.activation(
                out=rstd[:, t0:t0 + tn], in_=psb[b][:, 0:tn], func=AF.Ln, bias=eps_t[:], scale=1.0 / D),
                reads=[("ps", banks[ti]), "eps"], writes=[("rstd", ti)])
            S.op("act", lambda e, t0=t0, tn=tn: e.activation(
                out=rstd[:, t0:t0 + tn], in_=rstd[:, t0:t0 + tn], func=AF.Exp, scale=-0.5),
                reads=[("rstd", ti)], writes=[("rstd", ti)])
        ps_release(*banks)
        pend["banks"] = None

    def norm_mod(l, sub):
        rms_stats()
        for kc in range(8):
            tb = tmp[kc % 2]
            for s, (c0, cn) in enumerate(SLOTS):
                S.op("dve", lambda e, kc=kc, s=s, c0=c0, cn=cn, tb=tb: e.scalar_tensor_tensor(
                    out=tb[:, c0:c0 + cn], in0=x[:, kc, c0:c0 + cn], scalar=A_ap(l, sub, kc, s),
                    in1=rstd[:, c0:c0 + cn], op0=ALU.mult, op1=ALU.mult),
                    reads=[("x", kc), ("rstd", 0), ("rstd", 1), ("rstd", 2), ("amul", l % 2, sub)],
                    writes=[("tmp", kc % 2, s)])
                S.op("act", lambda e, kc=kc, s=s, c0=c0, cn=cn, tb=tb: e.activation(
                    out=h[:, kc, c0:c0 + cn], in_=tb[:, c0:c0 + cn], func=AF.Identity,
                    bias=B_ap(l, sub, kc, s), scale=1.0),
                    reads=[("tmp", kc % 2, s), ("modv", l % 2, sub)], writes=[("h", kc)])

    def ffn(l, si):
        sub = 0 if si == 0 else 2
        if pend["banks"] is None:
            rms_begin()
            for kc_ in range(8):
                rms_chunk(kc_)
        bg_ensure(l, sub)
        norm_mod(l, sub)
        hreads = [("h", kc) for kc in range(8)]
        for t in range(11):
            bg_step(3)
            slot = load_w(wgu_d[l, si, t], 4096)
            for fcl in range(2):
                j = 2 * t + fcl
                gb = [ps_alloc() for _ in TT]
                for kc in range(8):
                    off = ((0 * 2 + fcl) * 8 + kc) * 128
                    for ti, (t0, tn) in enumerate(TT):
                        S.op("pe", lambda e, b=gb[ti], slot=slot, off=off, kc=kc, t0=t0, tn=tn: e.matmul(
                            psb[b][:, 0:tn], lhsT=wbuf[slot][:, off:off + 128], rhs=h[:, kc, t0:t0 + tn],
                            start=(kc == 0), stop=(kc == 7)),
                            reads=[("w", slot), ("h", kc)], writes=[("ps", gb[ti])])
                sgb = sg[j % 2]
                for ti, (t0, tn) in enumerate(TT):
                    S.op("act", lambda e, b=gb[ti], t0=t0, tn=tn, sgb=sgb: e.activation(
                        out=sgb[:, t0:t0 + tn], in_=psb[b][:, 0:tn], func=AF.Silu),
                        reads=[("ps", gb[ti])], writes=[("sg", j % 2, ti)])
                ub = [ps_alloc() for _ in TT]
                for kc in range(8):
                    off = ((1 * 2 + fcl) * 8 + kc) * 128
                    for ti, (t0, tn) in enumerate(TT):
                        S.op("pe", lambda e, b=ub[ti], slot=slot, off=off, kc=kc, t0=t0, tn=tn: e.matmul(
                            psb[b][:, 0:tn], lhsT=wbuf[slot][:, off:off + 128], rhs=h[:, kc, t0:t0 + tn],
                            start=(kc == 0), stop=(kc == 7)),
                            reads=[("w", slot), ("h", kc)], writes=[("ps", ub[ti])])
                for ti, (t0, tn) in enumerate(TT):
                    S.op("dve", lambda e, b=ub[ti], t0=t0, tn=tn, sgb=sgb, j=j: e.tensor_tensor(
                        out=act[:, j, t0:t0 + tn], in0=psb[b][:, 0:tn], in1=sgb[:, t0:t0 + tn], op=ALU.mult),
                        reads=[("ps", ub[ti]), ("sg", j % 2, ti), "bigown"], writes=[("act", j)])
        areads = [("act", j) for j in range(NFC)] + ["bigown"]
        for j in range(8):
            slot = load_w(wdn_d[l, si, j], 2816)
            ob = [ps_alloc() for _ in TT]
            for kc in range(NFC):
                for ti, (t0, tn) in enumerate(TT):
                    S.op("pe", lambda e, b=ob[ti], slot=slot, kc=kc, t0=t0, tn=tn: e.matmul(
                        psb[b][:, 0:tn], lhsT=wbuf[slot][:, kc * 128:(kc + 1) * 128], rhs=act[:, kc, t0:t0 + tn],
                        start=(kc == 0), stop=(kc == NFC - 1)),
                        reads=[("w", slot)] + areads, writes=[("ps", ob[ti])])
            for ti, (t0, tn) in enumerate(TT):
                s = SLOT_OF_TT[ti]
                S.op("dve", lambda e, b=ob[ti], t0=t0, tn=tn, j=j, s=s: e.scalar_tensor_tensor(
                    out=x[:, j, t0:t0 + tn], in0=psb[b][:, 0:tn], scalar=G_ap(l, sub, j, s),
                    in1=x[:, j, t0:t0 + tn], op0=ALU.mult, op1=ALU.add),
                    reads=[("ps", ob[ti]), ("gate", l % 2, sub), ("x", j)], writes=[("x", j)])
            if j == 4:
                rms_begin()
                for jj in range(4):
                    rms_chunk(jj)
            elif j > 4:
                rms_chunk(j - 1)
        rms_chunk(7)


    in_mixer = [False]

    def _r(reads):
        return list(reads) + (["bigown"] if in_mixer[0] else [])

    def MM(bank, out_ap, lhsT, rhs, start, stop, reads):
        reads = _r(reads)
        S.op("pe", lambda e: e.matmul(out_ap, lhsT=lhsT, rhs=rhs, start=start, stop=stop),
             reads=reads, writes=[("ps", bank)])

    def ACTF(out, in_, func, reads, writes, bias=None, scale=None):
        kw = {}
        if bias is not None:
            kw["bias"] = bias
        if scale is not None:
            kw["scale"] = scale
        reads = _r(reads)
        return S.op("act", lambda e: e.activation(out=out, in_=in_, func=func, **kw), reads=reads, writes=writes)

    def TTO(eng, out, in0, in1, op, reads, writes):
        reads = _r(reads)
        return S.op(eng, lambda e: e.tensor_tensor(out=out, in0=in0, in1=in1, op=op), reads=reads, writes=writes)

    def STT(out, in0, scalar, in1, op0, op1, reads, writes):
        reads = _r(reads)
        return S.op("dve", lambda e: e.scalar_tensor_tensor(out=out, in0=in0, scalar=scalar, in1=in1, op0=op0, op1=op1),
                    reads=reads, writes=writes)

    def TS(eng, out, in0, s1, s2, op0, op1, reads, writes):
        reads = _r(reads)
        if op1 is None:
            return S.op(eng, lambda e: e.tensor_scalar(out=out, in0=in0, scalar1=s1, scalar2=None, op0=op0),
                        reads=reads, writes=writes)
        return S.op(eng, lambda e: e.tensor_scalar(out=out, in0=in0, scalar1=s1, scalar2=s2, op0=op0, op1=op1),
                    reads=reads, writes=writes)

    def RECIP(out, in_, reads, writes):
        reads = _r(reads)
        return S.op("dve", lambda e: e.reciprocal(out=out, in_=in_), reads=reads, writes=writes)

    def MEMSET(eng, ap, val, reads, writes):
        reads = _r(reads)
        return S.op(eng, lambda e: e.memset(ap, val), reads=reads, writes=writes)

    def DMA(eng, out, in_, reads, writes, key):
        reads = _r(reads)
        return S.op(eng, lambda e: e.dma_start(out=out, in_=in_), reads=reads, writes=writes, dma=key)

    class WStream:
        def __init__(self, dram_ap, lay):
            self.d = dram_ap
            self.lay = lay
            self.loaded = {}
            self.used = {}
            for (t, off, KC, n) in lay:
                self.used[t] = max(self.used.get(t, 0), off + KC * n)

        def blk(self, b, kc):
            t, off, KC, n = self.lay[b]
            if t not in self.loaded:
                u = self.used[t]
                self.loaded[t] = load_w(self.d[t][:, 0:u], u)
            slot = self.loaded[t]
            return slot, wbuf[slot][:, off + kc * n:off + (kc + 1) * n]

    def proj(ws, b, KC, rhs_fn, rreads, M, tiles):
        banks = [ps_alloc(hold=True) for _ in tiles]
        for kc in range(KC):
            slot, lw = ws.blk(b, kc)
            for ti, (t0, tn) in enumerate(tiles):
                rr = rreads(kc) if callable(rreads) else rreads
                MM(banks[ti], psb[banks[ti]][0:M, 0:tn], lw, rhs_fn(kc, t0, tn), kc == 0, kc == KC - 1,
                   [("w", slot)] + rr)
        bg_step(1)
        return banks

    out_evs = []
    fence_n = [0]

    def fence():
        k = fence_n[0] % 2
        fence_n[0] += 1
        in_mixer[0] = False
        MEMSET("dve", fence_t[:, k:k + 1], 0.0, [], ["bigown"])

    def gated_out(ws, l, hreads):
        for j in range(8):
            banks = proj(ws, j, 8, lambda kc, t0, tn: h[:, kc, t0:t0 + tn], hreads, 128, TT)
            for ti, (t0, tn) in enumerate(TT):
                s_ = SLOT_OF_TT[ti]
                STT(x[:, j, t0:t0 + tn], psb[banks[ti]][:, 0:tn], G_ap(l, 1, j, s_), x[:, j, t0:t0 + tn],
                    ALU.mult, ALU.add, [("ps", banks[ti]), ("gate", l % 2, 1), ("x", j)], [("x", j)])
            ps_release(*banks)
            if j == 4:
                rms_begin()
                for jj in range(4):
                    rms_chunk(jj)
            elif j > 4:
                rms_chunk(j - 1)
        rms_chunk(7)

    def odd_mixer(l):
        i = l // 2

        def ov(o, n=1):
            return oddv[:, i * NOV + o:i * NOV + o + n]

        bg_flush()
        norm_mod(l, 1)
        fence()
        in_mixer[0] = True
        BO = ["bigown"]
        hreads = lambda kc: [("h", kc)]
        hall = [("h", kc) for kc in range(8)]
        off = [0]

        def carve(nw):
            r = big[:, off[0]:off[0] + nw]
            off[0] += nw
            return r

        ckvn = carve(1536).bitcast(BF16).rearrange("p (c t) -> p c t", c=2)
        QN = carve(2560).bitcast(BF16).rearrange("p (m t) -> p m t", m=4)
        QR = carve(2560).bitcast(BF16).rearrange("p (m t) -> p m t", m=4)
        KN = carve(3072).bitcast(BF16).rearrange("p (m t) -> p m t", m=4)
        Vt = carve(3072).bitcast(BF16).rearrange("p (k f) -> p k f", k=12)
        ropeC = carve(512).bitcast(BF16)
        ropeS = carve(512).bitcast(BF16)
        eoff = [0]

        def ecarve(nw):
            r = ext[:, eoff[0]:eoff[0] + nw]
            eoff[0] += nw
            return r

        cqg = ecarve(1920).bitcast(BF16).rearrange("p (c t) -> p c t", c=3)
        KR2 = ecarve(768).bitcast(BF16)
        cpadB = ecarve(2860).bitcast(BF16).rearrange("p (c t) -> p c t", c=4)
        rcb = rstd2[:, 0:512]
        dgr = ecarve(256).bitcast(BF16).rearrange("p (r i) -> p r i", r=4)
        PT = [ecarve(256).bitcast(BF16) for _ in range(2)]
        tmp0t = [("tmp", 0, 0), ("tmp", 0, 1)]
        tmp1t = [("tmp", 1, 0), ("tmp", 1, 1)]
        tmpt = [tmp0t, tmp1t]
        rst = [("rstd", 0), ("rstd", 1), ("rstd", 2)]

        DMA("pool", ckvn[:, :, 1280:1536], cckv_d[i], BO, ["ckvn_c"], "mA")
        DMA("pool", KR2[0:64, 1280:1536], ckr_d[i], [], ["KR2_c"], "mB")
        DMA("pool", ropeC[0:64, :], rope_d[0], BO, ["ropeC"], "mC")
        DMA("pool", ropeS[0:64, :], rope_d[1], BO, ["ropeS"], "mD")

        WI = WStream(woin_d[i], LAY_OIN)

        def hrhs(kc, t0, tn):
            return h[:, kc, t0:t0 + tn]

        def rms_finish(statb, dst, scale, toks):
            for ti, (t0, tn) in enumerate(TT):
                ACTF(dst[:, t0:t0 + tn], psb[statb[ti]][:, 0:tn], AF.Ln, [("ps", statb[ti]), "eps"], [toks[ti]],
                     bias=eps_t[:], scale=scale)
                ACTF(dst[:, t0:t0 + tn], dst[:, t0:t0 + tn], AF.Exp, [toks[ti]], [toks[ti]], scale=-0.5)
            ps_release(*statb)

        statb = [ps_alloc(hold=True) for _ in TT]
        for c in range(3):
            banks = proj(WI, c, 8, hrhs, hreads, 128, TT)
            for ti, (t0, tn) in enumerate(TT):
                ACTF(cqg[:, c, t0:t0 + tn], psb[banks[ti]][:, 0:tn], AF.Identity, [("ps", banks[ti]), "oddv"],
                     [("cqg", c)], scale=ov(c))
                ACTF(sq[c % 2][:, t0:t0 + tn], psb[banks[ti]][:, 0:tn], AF.Square, [("ps", banks[ti])],
                     [("sq", c % 2)])
            ps_release(*banks)
            for ti, (t0, tn) in enumerate(TT):
                MM(statb[ti], psb[statb[ti]][:, 0:tn], ones_bf[:], sq[c % 2][:, t0:t0 + tn], c == 0, c == 2,
                   [("sq", c % 2), "ones"])
        rms_finish(statb, rstd, 1.0 / 384, rst)

        banks = proj(WI, 3, 8, hrhs, hreads, 64, TT)
        for ti, (t0, tn) in enumerate(TT):
            ACTF(tmp[0][0:64, t0:t0 + tn], psb[banks[ti]][0:64, 0:tn], AF.Identity, [("ps", banks[ti])], tmp0t)
        ps_release(*banks)
        out_evs.append(DMA("sp", krT_d[i], tmp[0][0:32, 0:NT], tmp0t, [], "o_tmp0"))
        banks = proj(WI, 4, 8, hrhs, hreads, 64, TT[0:2])
        for ti, (t0, tn) in enumerate(TT[0:2]):
            TTO("dve", tmp[1][0:64, t0:t0 + tn], tmp[0][0:64, t0:t0 + tn], ropeC[0:64, t0:t0 + tn], ALU.mult,
                tmp0t + ["ropeC"] + BO, tmp1t)
            TTO("dve", rstd2[0:64, t0:t0 + tn], psb[banks[ti]][0:64, 0:tn], ropeS[0:64, t0:t0 + tn], ALU.mult,
                [("ps", banks[ti]), "ropeS"] + BO, ["rstd2"])
            TTO("dve", KR2[0:64, t0:t0 + tn], tmp[1][0:64, t0:t0 + tn], rstd2[0:64, t0:t0 + tn], ALU.add,
                tmp1t + ["rstd2"], ["KR2"])
        ps_release(*banks)
        ACTF(KR2[0:64, 1024:1280], tmp[0][0:64, 1024:1280], AF.Identity, tmp0t, ["KR2"])

        statb = [ps_alloc(hold=True) for _ in TT]
        for c in range(2):
            banks = proj(WI, 5 + c, 8, hrhs, hreads, 128, TT)
            for ti, (t0, tn) in enumerate(TT):
                ACTF(tmp[c][:, t0:t0 + tn], psb[banks[ti]][:, 0:tn], AF.Identity, [("ps", banks[ti]), "oddv"],
                     tmpt[c], scale=ov(3 + c))
                ACTF(sq[c % 2][:, t0:t0 + tn], psb[banks[ti]][:, 0:tn], AF.Square, [("ps", banks[ti])],
                     [("sq", c % 2)])
            ps_release(*banks)
            for ti, (t0, tn) in enumerate(TT):
                MM(statb[ti], psb[statb[ti]][:, 0:tn], ones_bf[:], sq[c % 2][:, t0:t0 + tn], c == 0, c == 1,
                   [("sq", c % 2), "ones"])
        rms_finish(statb, rstd2, 1.0 / 256, ["rstd2", "rstd2", "rstd2"])
        for c in range(2):
            TTO("dve", tmp[c][:, 0:NT], tmp[c][:, 0:NT], rstd2[:, :], ALU.mult, tmpt[c] + ["rstd2"], tmpt[c])
            out_evs.append(DMA("sp", ckvT_d[i, c], tmp[c][:, 0:NT], tmpt[c], [], "o_tmp%d" % c))
            ACTF(ckvn[:, c, 0:NT], tmp[c][:, 0:NT], AF.Identity, tmpt[c] + BO, ["ckvn"])

        def cp5(c):
            return cpadB[:, c, :].rearrange("p (s w) -> p s w", s=5)

        for c in range(4):
            MEMSET("dve", cp5(c)[:, :, 0:15], 0.0, [], [("cpad", c)])
            MEMSET("dve", cp5(c)[:, :, 271:286], 0.0, [], [("cpad", c)])
            ba = proj(WI, 7 + 2 * c, 8, hrhs, hreads, 128, TT)
            bg = proj(WI, 8 + 2 * c, 8, hrhs, hreads, 128, TT)
            segs = [(0, 2), (2, 4), (4, 5)]
            for ti, (t0, tn) in enumerate(TT):
                ACTF(sg[c % 2][:, t0:t0 + tn], psb[bg[ti]][:, 0:tn], AF.Sigmoid, [("ps", bg[ti])], [("sg", c % 2, ti)])
                s0, s1 = segs[ti]
                TTO("dve", cp5(c)[:, s0:s1, 15:271], psb[ba[ti]][:, 0:tn].rearrange("p (s w) -> p s w", w=256),
                    sg[c % 2][:, t0:t0 + tn].rearrange("p (s w) -> p s w", w=256), ALU.mult,
                    [("ps", ba[ti]), ("sg", c % 2, ti)], [("cpad", c)])
            ps_release(*ba)
            ps_release(*bg)
            TS("dve", cp5(c)[:, 1:4, 0:15], cp5(c)[:, 0:3, 256:271], flags[:, 0:1], None, ALU.mult, None,
               [("cpad", c), "flags"], [("cpad", c)])
            TS("dve", cp5(c)[:, 0:3, 271:286], cp5(c)[:, 1:4, 15:30], flags[:, 0:1], None, ALU.mult, None,
               [("cpad", c), "flags"], [("cpad", c)])

        WQ = WStream(wuq_d[i], LAY_UQ)

        def qrhs(kc, t0, tn):
            return cqg[:, kc, t0:t0 + tn]

        cqr = [("cqg", c) for c in range(3)]
        for m in range(4):
            banks = proj(WQ, m, 3, qrhs, cqr, 128, TT)
            for ti, (t0, tn) in enumerate(TT):
                TTO("dve", QN[:, m, t0:t0 + tn], psb[banks[ti]][:, 0:tn], rstd[:, t0:t0 + tn], ALU.mult,
                    [("ps", banks[ti]), rst[ti]] + BO, ["QN"])
            ps_release(*banks)
        for m in range(4):
            bq = proj(WQ, 4 + m, 3, qrhs, cqr, 64, TT)
            bp = proj(WQ, 8 + m, 3, qrhs, cqr, 64, TT[0:2])
            for ti, (t0, tn) in enumerate(TT[0:2]):
                TTO("dve", tmp[0][0:64, t0:t0 + tn], psb[bq[ti]][0:64, 0:tn], ropeC[0:64, t0:t0 + tn], ALU.mult,
                    [("ps", bq[ti]), "ropeC"] + BO, tmp0t)
                TTO("dve", tmp[1][0:64, t0:t0 + tn], psb[bp[ti]][0:64, 0:tn], ropeS[0:64, t0:t0 + tn], ALU.mult,
                    [("ps", bp[ti]), "ropeS"] + BO, tmp1t)
                TTO("dve", tmp[0][0:64, t0:t0 + tn], tmp[0][0:64, t0:t0 + tn], tmp[1][0:64, t0:t0 + tn], ALU.add,
                    tmp0t + tmp1t, tmp0t)
                TTO("dve", QR[0:64, m, t0:t0 + tn], tmp[0][0:64, t0:t0 + tn], rstd[0:64, t0:t0 + tn], ALU.mult,
                    tmp0t + [rst[ti]] + BO, ["QR"])
            t0, tn = TT[2]
            TTO("dve", QR[0:64, m, t0:t0 + tn], psb[bq[2]][0:64, 0:tn], rstd[0:64, t0:t0 + tn], ALU.mult,
                [("ps", bq[2]), rst[2]] + BO, ["QR"])
            ps_release(*bq)
            ps_release(*bp)

        WK = WStream(wukv_d[i], LAY_UKV)
        KT = [(0, 512), (512, 512), (1024, 512)]
        ckr_ = ["ckvn", "ckvn_c"] + BO

        def krhs(kc, t0, tn):
            return ckvn[:, kc, t0:t0 + tn]

        for m in range(4):
            banks = proj(WK, m, 2, krhs, ckr_, 128, KT)
            for ti, (t0, tn) in enumerate(KT):
                ACTF(KN[:, m, t0:t0 + tn], psb[banks[ti]][:, 0:tn], AF.Identity, [("ps", banks[ti])] + BO, ["KN"])
            ps_release(*banks)
        for kb in range(12):
            bk = ps_alloc(hold=True)
            for kc in range(2):
                slot, rw = WK.blk(4, kc)
                MM(bk, psb[bk][:, 0:512], ckvn[:, kc, kb * 128:(kb + 1) * 128], rw, kc == 0, kc == 1,
                   [("w", slot)] + ckr_)
            if kb % 2 == 0:
                ACTF(Vt[:, kb, :], psb[bk][:, 0:512], AF.Identity, [("ps", bk)] + BO, ["Vt"])
            else:
                S.op("dve", lambda e, kb=kb, bk=bk: e.tensor_copy(out=Vt[:, kb, :], in_=psb[bk][:, 0:512]),
                     reads=[("ps", bk)] + BO, writes=["Vt"])
            ps_release(bk)

        dbuf = [sq[0], sq[1], sg[0], sg[1]]
        dtok = [[("sq", 0)], [("sq", 1)], [("sg", 0, t_) for t_ in range(3)], [("sg", 1, t_) for t_ in range(3)]]
        conv_ops = []

        def mk_conv(c):
            accf = tmp[c % 2][:, 0:NT]
            acc = accf.rearrange("p (s w) -> p s w", s=5)
            c5 = cp5(c)
            tt_ = tmpt[c % 2]
            conv_ops.append(lambda: TS("dve", acc, c5[:, :, 0:256], ov(5 + c * 31), ov(129 + c), ALU.mult, ALU.add,
                                       [("cpad", c), "oddv"], tt_))
            for k in range(1, 31):
                conv_ops.append(lambda k=k: STT(acc, c5[:, :, k:k + 256], ov(5 + c * 31 + k), acc, ALU.mult, ALU.add,
                                                [("cpad", c), "oddv"] + tt_, tt_))
            conv_ops.append(lambda: S.op("pool", lambda e: e.tensor_copy(out=dbuf[c][:, :], in_=accf),
                                         reads=_r(tt_), writes=dtok[c]))

        for c in range(4):
            mk_conv(c)
        conv_pos = [0]

        def conv_some(n):
            for _ in range(n):
                if conv_pos[0] < len(conv_ops):
                    conv_ops[conv_pos[0]]()
                    conv_pos[0] += 1

        sm_scale = 96.0 ** -0.5
        zb, nb_ = flags[:, 2:3], flags[:, 1:2]
        PT4 = PT + [carve(256).bitcast(BF16) for _ in range(2)]
        steps = []
        for m in range(4):
            for (qc, qn, kbs) in [(0, 512, list(range(8)) + [10, 11]), (512, 512, list(range(8)) + [10, 11]),
                                  (1024, 256, [8, 9])]:
                for ki, kb in enumerate(kbs):
                    halves = []
                    for hc in range(0, qn, 256):
                        qseq = (qc + hc) // 256
                        halves.append(zb if (kb // 2) == qseq else nb_)
                    steps.append((m, qc, qn, kb, ki == 0, ki == len(kbs) - 1, halves))
        sbk_of = {}
        grp = {}

        def stageS(k):
            m, qc, qn, kb, first, last, halves = steps[k]
            k0 = kb * 128
            sb2 = [ps_alloc(hold=True) for _ in range(2)]
            sbk_of[k] = sb2
            for hh in range(2):
                r0 = hh * 64
                MM(sb2[hh], psb[sb2[hh]][:, 0:qn], KN[r0:r0 + 64, m, k0:k0 + 128], QN[r0:r0 + 64, m, qc:qc + qn],
                   True, False, ["KN", "QN"])
            for hh in range(2):
                q0 = hh * 32
                MM(sb2[hh], psb[sb2[hh]][:, 0:qn], KR2[q0:q0 + 32, k0:k0 + 128], QR[q0:q0 + 32, m, qc:qc + qn],
                   False, True, ["KR2", "KR2_c", "QR"])

        def stageE(k):
            m, qc, qn, kb, first, last, halves = steps[k]
            sb2 = sbk_of.pop(k)
            for hh in range(2):
                pi = (k % 2) * 2 + hh
                pt = PT4[pi]
                ptt = ("PT", pi)
                if len(halves) == 1 or halves[0] is halves[1]:
                    ACTF(pt[:, 0:qn], psb[sb2[hh]][:, 0:qn], AF.Exp, [("ps", sb2[hh]), "flags"], [ptt], bias=halves[0],
                         scale=sm_scale)
                else:
                    for hi, hb in enumerate(halves):
                        ACTF(pt[:, hi * 256:(hi + 1) * 256], psb[sb2[hh]][:, hi * 256:(hi + 1) * 256], AF.Exp,
                             [("ps", sb2[hh]), "flags"], [ptt], bias=hb, scale=sm_scale)
            ps_release(*sb2)

        def stagePV(k):
            m, qc, qn, kb, first, last, halves = steps[k]
            if first:
                grp["ob"] = [ps_alloc(hold=True) for _ in range(2)]
                grp["smb"] = [ps_alloc(hold=True) for _ in range(2)]
            for hh in range(2):
                r0 = hh * 64
                ob, smb = grp["ob"][hh], grp["smb"][hh]
                pi = (k % 2) * 2 + hh
                pt = PT4[pi]
                ptt = ("PT", pi)
                MM(ob, psb[ob][:, 0:qn], Vt[:, kb, m * 128:(m + 1) * 128], pt[:, 0:qn], first, last, ["Vt", ptt])
                MM(smb, psb[smb][:, 0:qn], ones_bf[:], pt[:, 0:qn], first, last, [ptt, "ones"])
                if last:
                    ACTF(rcb[r0:r0 + 64, 0:qn], psb[smb][r0:r0 + 64, 0:qn], AF.Ln, [("ps", smb)], ["rstd2"])
                    ACTF(rcb[r0:r0 + 64, 0:qn], rcb[r0:r0 + 64, 0:qn], AF.Exp, ["rstd2"], ["rstd2"], scale=-1.0)
                    TTO("dve", h[r0:r0 + 64, m, qc:qc + qn], psb[ob][r0:r0 + 64, 0:qn], rcb[r0:r0 + 64, 0:qn], ALU.mult,
                        [("ps", ob), "rstd2"], [("h", m)])
            if last:
                ps_release(*grp["ob"])
                ps_release(*grp["smb"])

        stageS(0)
        for k in range(len(steps)):
            if k + 1 < len(steps):
                stageS(k + 1)
            stageE(k)
            stagePV(k)
            conv_some(2 if k % 2 == 0 else 1)
        conv_some(len(conv_ops))

        s1 = [ps_alloc(hold=True) for _ in TT]
        s2 = [ps_alloc(hold=True) for _ in TT]
        for c in range(4):
            dc = dbuf[c]
            sqc = cpadB[:, c, 0:NT]
            ACTF(sqc, dc[:, :], AF.Square, dtok[c], [("cpad", c)])
            for ti, (t0, tn) in enumerate(TT):
                MM(s1[ti], psb[s1[ti]][:, 0:tn], ones_bf[:], dc[:, t0:t0 + tn], c == 0, c == 3, dtok[c] + ["ones"])
                MM(s2[ti], psb[s2[ti]][:, 0:tn], ones_bf[:], sqc[:, t0:t0 + tn], c == 0, c == 3,
                   [("cpad", c), "ones"])
        for ti, (t0, tn) in enumerate(TT):
            ACTF(rstd[:, t0:t0 + tn], psb[s1[ti]][:, 0:tn], AF.Identity, [("ps", s1[ti])], [rst[ti]], scale=1.0 / 512)
            ACTF(tmp[0][:, t0:t0 + tn], psb[s1[ti]][:, 0:tn], AF.Square, [("ps", s1[ti])], tmp0t, scale=1.0 / 512)
            STT(tmp[0][:, t0:t0 + tn], psb[s2[ti]][:, 0:tn], 1.0 / 512, tmp[0][:, t0:t0 + tn], ALU.mult, ALU.subtract,
                [("ps", s2[ti])] + tmp0t, tmp0t)
            ACTF(rstd2[:, t0:t0 + tn], tmp[0][:, t0:t0 + tn], AF.Ln, tmp0t + ["eps"], ["rstd2"], bias=eps_t[:], scale=1.0)
            ACTF(rstd2[:, t0:t0 + tn], rstd2[:, t0:t0 + tn], AF.Exp, ["rstd2"], ["rstd2"], scale=-0.5)
        ps_release(*s1)
        ps_release(*s2)
        for c in range(4):
            dc = dbuf[c][:, :]
            TTO("dve", tmp[1][:, 0:NT], dc, rstd[:, :], ALU.subtract, dtok[c] + rst, tmp1t)
            TTO("dve", tmp[1][:, 0:NT], tmp[1][:, 0:NT], rstd2[:, :], ALU.mult, tmp1t + ["rstd2"], tmp1t)
            ACTF(h[:, 4 + c, :], tmp[1][:, 0:NT], AF.Silu, tmp1t + ["oddv"], [("h", 4 + c)],
                 bias=ov(137 + c), scale=ov(133 + c))

        WO = WStream(woout_d[i], LAY_OUT)
        gated_out(WO, l, hreads)
        fence()


    Umat = cst[:, 0:128]
    Lmat = cst[:, 128:256]
    NEGU = cst[:, 256:384]
    NEGL = cst[:, 384:512]
    identf = cst[:, 512:640]
    onesf = cst[:, 640:768]

    def even_mixer(l):
        i = l // 2

        def ev(o, n=1):
            return evv[:, i * NEV + o:i * NEV + o + n]

        bg_flush()
        norm_mod(l, 1)
        fence()
        in_mixer[0] = True
        hreads = lambda kc: [("h", kc)]
        hall = [("h", kc) for kc in range(8)]
        off = [0]

        def carve(nw):
            r = big[:, off[0]:off[0] + nw]
            off[0] += nw
            return r

        Ub = carve(2560).bitcast(BF16).rearrange("p (m t) -> p m t", m=4)
        Zb = carve(2560).bitcast(BF16).rearrange("p (m t) -> p m t", m=4)
        XS = carve(2560).bitcast(BF16).rearrange("p (m t) -> p m t", m=4)
        BC = carve(2560).bitcast(BF16).rearrange("p (m t) -> p m t", m=4)
        Hb = carve(2560).bitcast(BF16).rearrange("p (c f) -> p c f", c=10)
        SC = carve(960).rearrange("p (k c f) -> p k c f", k=6, c=10)
        Mb_b = carve(512).bitcast(BF16).rearrange("p (h i) -> p h i", h=8)
        eoff = [0]

        def ecarve(nw):
            r = ext[:, eoff[0]:eoff[0] + nw]
            eoff[0] += nw
            return r

        evb = ecarve(NEB)
        WsT = ecarve(256).bitcast(BF16).rearrange("p (g i) -> p g i", g=4)
        REf = ecarve(1024).rearrange("p (h i) -> p h i", h=8)
        REb = ecarve(1024).rearrange("p (h i) -> p h i", h=8)
        Mb = ecarve(512).bitcast(BF16).rearrange("p (h i) -> p h i", h=8)
        Xt = ecarve(256).bitcast(BF16)
        Btok = ecarve(128).bitcast(BF16)
        Xw = ecarve(256).bitcast(BF16)
        vtm = ecarve(256).bitcast(BF16)
        Wtok = ecarve(256).bitcast(BF16)
        Hfc = ecarve(512)
        Hbc = ecarve(512)
        Hfb = ecarve(256).bitcast(BF16)
        gv_bc = evb[:, 0:512]
        bs_bc = evb[:, 512:1024]
        dtb_bc = evb[:, 1024:1040]
        alog_bc = evb[:, 1040:1056]
        flagA = flags[:, 0:1]
        tmp0t = [("tmp", 0, 0), ("tmp", 0, 1)]
        tmp1t = [("tmp", 1, 0), ("tmp", 1, 1)]
        tmpt = [tmp0t, tmp1t]

        DMA("sp", evb, evb_d[i], [], ["evb"], "mE")
        DMA("pool", WsT, wsT_d[i].rearrange("p (g i) -> p g i", g=4), [], ["WsT"], "mF")
        ACTF(aneg[:], alog_bc, AF.Exp, ["evb"], ["aneg"])
        TS("dve", aneg[:], aneg[:], -1.0, None, ALU.mult, None, ["aneg"], ["aneg"])

        TTO("dve", ev(32, 4), ev(32, 4), ev(36, 4), ALU.add, ["evv"], ["dsk"])
        WE = WStream(wein_d[i], LAY_EIN)

        def hrhs(kc, t0, tn):
            return h[:, kc, t0:t0 + tn]

        for c in range(4):
            banks = proj(WE, c, 8, hrhs, hreads, 128, TT)
            for ti, (t0, tn) in enumerate(TT):
                ACTF(Ub[:, c, t0:t0 + tn], psb[banks[ti]][:, 0:tn], AF.Gelu_apprx_tanh, [("ps", banks[ti])], [("Ub", c)])
            ps_release(*banks)
        for c in range(4):
            banks = proj(WE, 4 + c, 8, hrhs, hreads, 128, TT)
            for ti, (t0, tn) in enumerate(TT):
                ACTF(Zb[:, c, t0:t0 + tn], psb[banks[ti]][:, 0:tn], AF.Silu, [("ps", banks[ti])], [("Zb", c)])
            ps_release(*banks)
        segs = [(0, 2), (2, 4), (4, 5)]
        for c in range(8):
            xp = tmp[c % 2][:, 0:1290].rearrange("p (s w) -> p s w", s=5)
            tt_ = tmpt[c % 2]
            MEMSET("dve", xp[:, :, 0:1], 0.0, [], tt_)
            MEMSET("dve", xp[:, :, 257:258], 0.0, [], tt_)
            banks = proj(WE, 8 + c, 8, hrhs, hreads, 128, TT)
            for ti, (t0, tn) in enumerate(TT):
                s0, s1 = segs[ti]
                ACTF(xp[:, s0:s1, 1:257], psb[banks[ti]][:, 0:tn].rearrange("p (s w) -> p s w", w=256), AF.Identity,
                     [("ps", banks[ti])], tt_)
            ps_release(*banks)
            TS("dve", xp[:, 1:4, 0:1], xp[:, 0:3, 256:257], flagA, None, ALU.mult, None, tt_ + ["flags"], tt_)
            TS("dve", xp[:, 0:3, 257:258], xp[:, 1:4, 1:2], flagA, None, ALU.mult, None, tt_ + ["flags"], tt_)
            accb = [rstd2, rstd][c % 2]
            acct = [["rstd2"], [("rstd", 0), ("rstd", 1), ("rstd", 2)]][c % 2]
            acc = accb[:, :].rearrange("p (s w) -> p s w", s=5)
            TS("dve", acc, xp[:, :, 0:256], ev(c * 3 + 0), ev(24 + c), ALU.mult, ALU.add, tt_ + ["evv"], acct)
            STT(acc, xp[:, :, 1:257], ev(c * 3 + 1), acc, ALU.mult, ALU.add, tt_ + ["evv"] + acct, acct)
            STT(acc, xp[:, :, 2:258], ev(c * 3 + 2), acc, ALU.mult, ALU.add, tt_ + ["evv"] + acct, acct)
            if c < 4:
                dst, dtok = XS[:, c, :], ("XS", c)
            else:
                dst, dtok = BC[:, c - 4, :], ("BC", c - 4)
            ACTF(dst, accb[:, :], AF.Silu, acct, [dtok])

        def bc8(ap8):
            return ap8.unsqueeze(2).broadcast_to([128, 8, 64])

        def v3(ap):
            return ap.rearrange("p (h q) -> p h q", h=8)

        def tok_major(c):
            cs = slice(c * 128, (c + 1) * 128)
            bk = ps_alloc(hold=True)
            pb = psb[bk][:, :].bitcast(BF16)
            for m in range(4):
                S.op("pe", lambda e, m=m: e.transpose(pb[:, m * 128:(m + 1) * 128], XS[:, m, cs], identb[:]),
                     reads=_r([("XS", m), "identb"]), writes=[("ps", bk)])
            for g in range(2):
                S.op("pe", lambda e, g=g: e.transpose(pb[:, 512 + g * 128:512 + (g + 1) * 128], BC[:, g, cs], identb[:]),
                     reads=_r([("BC", g), "identb"]), writes=[("ps", bk)])
            ACTF(Xt[:, :], pb[:, 0:512], AF.Identity, [("ps", bk)], ["Xt"])
            S.op("dve", lambda e: e.tensor_copy(out=Btok[:, :], in_=pb[:, 512:768]), reads=_r([("ps", bk)]), writes=["Btok"])
            ps_release(bk)

        def seq_of(c):
            return c // 2

        sct = "SC"

        def f160(ap3):
            return ap3.rearrange("p c f -> p (c f)")

        def v160(ap2):
            return ap2.rearrange("p (c f) -> p c f", f=16)

        P0 = cfg.get('ev_p0', 2)
        if P0 >= 1:
            bkd = ps_alloc(hold=True)
            for c in range(10):
                cs = slice(c * 128, (c + 1) * 128)
                for kc in range(8):
                    slot, rw = WE.blk(17, kc)
                    MM(bkd, psb[bkd][:, c * 16:(c + 1) * 16], h[:, kc, cs], rw, kc == 0, kc == 7, [("w", slot), ("h", kc)])
            TTO("dve", v160(s160[0][:, :]), v160(psb[bkd][:, 0:160]), dtb_bc.unsqueeze(1).broadcast_to([128, 10, 16]), ALU.add,
                [("ps", bkd), "evb"], [("s160", 0)])
            ps_release(bkd)
            ACTF(s160[0][:, :], s160[0][:, :], AF.Exp, [("s160", 0)], [("s160", 0)])
            ACTF(f160(SC[:, 0, :, :]), s160[0][:, :], AF.Ln, [("s160", 0), "one"], [sct], bias=one_t[:], scale=1.0)
            ACTF(s160[1][:, :], f160(SC[:, 0, :, :]), AF.Ln, [sct], [("s160", 1)])
            TTO("dve", SC[:, 1, :, :], SC[:, 0, :, :], aneg[:, :].unsqueeze(1).broadcast_to([128, 10, 16]), ALU.mult,
                [sct, "aneg"], [sct])
        if P0 >= 2:
            bkc = ps_alloc(hold=True)
            for c in range(10):
                MM(bkc, psb[bkc][:, c * 32:c * 32 + 8], Umat, SC[:, 1, c, 0:8], True, True, [sct, "cst"])
                MM(bkc, psb[bkc][:, c * 32 + 8:c * 32 + 16], Lmat, SC[:, 1, c, 8:16], True, True, [sct, "cst"])
                MM(bkc, psb[bkc][:, c * 32 + 16:c * 32 + 32], onesf, SC[:, 1, c, :], True, True, [sct, "cst"])
            if P0 == 3:
                ps_release(bkc)
            else:
                ACTF(tmp[0][:, 0:320], psb[bkc][:, 0:320], AF.Identity, [("ps", bkc)], tmp0t)
                ps_release(bkc)
                pc = tmp[0][:, 0:320].rearrange("p (c f) -> p c f", f=32)
                cumv = pc[:, :, 0:16]
                totv = pc[:, :, 16:32]
                TTO("dve", SC[:, 2, :, :], v160(s160[1][:, :]), cumv, ALU.subtract, [("s160", 1)] + tmp0t, [sct])
                ACTF(SC[:, 4, :, :], cumv, AF.Exp, tmp0t, [sct])
                ACTF(SC[:, 5, :, :], totv, AF.Exp, tmp0t, [sct])
                TTO("dve", v160(s160[0][:, :]), totv, cumv, ALU.subtract, tmp0t, [("s160", 0)])
                ACTF(s160[0][:, :], s160[0][:, :], AF.Exp, [("s160", 0)], [("s160", 0)])
                TTO("dve", SC[:, 3, :, :], v160(s160[0][:, :]), SC[:, 0, :, :], ALU.mult, [("s160", 0), sct], [sct])

        def chunk_state(c, d):
            TTO("dve", v3(Xw[:, :]), v3(Xt[:, :]), bc8(SC[:, 3, c, d * 8:(d + 1) * 8]), ALU.mult, ["Xt", sct], ["Xw"])
            bk = ps_alloc(hold=True)
            for g in range(2):
                MM(bk, psb[bk][:, g * 256:(g + 1) * 256], Btok[:, g * 128:(g + 1) * 128], Xw[:, g * 256:(g + 1) * 256],
                   True, True, ["Btok", "Xw"])
            return bk

        for c in ([9, 8, 7, 6, 5, 4, 3, 2, 1, 0] if cfg.get('ev_p1', True) else []):
            cs = slice(c * 128, (c + 1) * 128)
            par = c % 2
            tpt = tmpt[par]
            if c == 9:
                MEMSET("dve", Hbc[:, :], 0.0, [], ["Hbc"])
            if c == 7:
                DMA("sp", Hbc[:, :], h0T_d[i][:, 1, :], [], ["Hbc"], "mG")
            bk = ps_alloc(hold=True)
            for kc in range(8):
                slot, rw = WE.blk(16, kc)
                MM(bk, psb[bk][:, 0:512], h[:, kc, cs], rw, kc == 0, kc == 7, [("w", slot), ("h", kc)])
            ACTF(tmp[par][:, 0:512], psb[bk][:, 0:512], AF.Gelu_apprx_tanh, [("ps", bk)], tpt)
            ps_release(bk)
            S.op("act", lambda e, par=par: e.activation(out=sq[0][:, 0:512], in_=tmp[par][:, 0:512], func=AF.Square,
                                                        accum_out=sml[:, 2 * par:2 * par + 1]),
                 reads=_r(tpt), writes=[("sq", 0), ("sml", par)])
            ACTF(sml[:, 2 * par + 1:2 * par + 2], sml[:, 2 * par:2 * par + 1], AF.Ln, [("sml", par), "eps"], [("sml1", par)],
                 bias=eps_t[:], scale=1.0 / 512)
            ACTF(sml[:, 2 * par + 1:2 * par + 2], sml[:, 2 * par + 1:2 * par + 2], AF.Exp, [("sml1", par)], [("sml1", par)],
                 scale=-0.5)
            STT(vtm[:, :], tmp[par][:, 0:512], sml[:, 2 * par + 1:2 * par + 2], gv_bc, ALU.mult, ALU.mult,
                tpt + [("sml1", par), "evb"], ["vtm"])
            tok_major(c)
            bk = chunk_state(c, 1)
            ACTF(Hb[:, c, :], Hbc[:, :], AF.Identity, ["Hbc"], [("Hb", c)])
            TTO("dve", v3(Hbc[:, :]), v3(Hbc[:, :]), bc8(SC[:, 5, c, 8:16]), ALU.mult, ["Hbc", sct], ["Hbc"])
            TTO("dve", Hbc[:, :], Hbc[:, :], psb[bk][:, 0:512], ALU.add, ["Hbc", ("ps", bk)], ["Hbc"])
            ps_release(bk)
            bk = ps_alloc(hold=True)
            for g in range(4):
                MM(bk, psb[bk][:, g * 128:(g + 1) * 128], vtm[:, g * 128:(g + 1) * 128], WsT[:, g, :], True, True,
                   ["vtm", "WsT"])
            TTO("dve", tmp[par][:, 512:1024], psb[bk][:, 0:512], bs_bc, ALU.add, [("ps", bk), "evb"], tpt)
            ps_release(bk)
            TTO("dve", Ub[:, :, cs], tmp[par][:, 512:1024].rearrange("p (g i) -> p g i", g=4), Ub[:, :, cs], ALU.mult,
                tpt + [("Ub", m) for m in range(4)], [("Ub", m) for m in range(4)])
            if c % 2 == 0:
                out_evs.append(DMA("sp", ssdT_d[i, seq_of(c), 1], Hbc[:, :], ["Hbc"], [], "o_Hbc"))
                if c in (2, 4, 6):
                    TS("dve", Hbc[:, :], Hbc[:, :], flagA, None, ALU.mult, None, ["Hbc", "flags"], ["Hbc"])

        Mbs = [Mb, Mb_b]
        grp_s = {}

        def stageP(c):
            cs = slice(c * 128, (c + 1) * 128)
            Mc = Mbs[c % 2]
            bf_ = [ps_alloc(hold=True) for _ in range(2)]
            bb_ = [ps_alloc(hold=True) for _ in range(2)]
            for hh in range(8):
                o_ = (hh % 4) * 128
                MM(bf_[hh // 4], psb[bf_[hh // 4]][:, o_:o_ + 128], SC[:, 1, c, hh:hh + 1].broadcast_to([128, 128]), Umat,
                   True, False, [sct, "cst"])
                MM(bf_[hh // 4], psb[bf_[hh // 4]][:, o_:o_ + 128], identb[:], negub[:, 0:128], False, True, ["identb", "negub"])
                MM(bb_[hh // 4], psb[bb_[hh // 4]][:, o_:o_ + 128], SC[:, 1, c, 8 + hh:9 + hh].broadcast_to([128, 128]), Lmat,
                   True, False, [sct, "cst"])
                MM(bb_[hh // 4], psb[bb_[hh // 4]][:, o_:o_ + 128], identb[:], negub[:, 128:256], False, True, ["identb", "negub"])
            for hh in range(8):
                ACTF(REf[:, hh, :], psb[bf_[hh // 4]][:, (hh % 4) * 128:(hh % 4 + 1) * 128], AF.Exp,
                     [("ps", bf_[hh // 4]), sct], ["REf"], bias=SC[:, 2, c, hh:hh + 1], scale=1.0)
                ACTF(REb[:, hh, :], psb[bb_[hh // 4]][:, (hh % 4) * 128:(hh % 4 + 1) * 128], AF.Exp,
                     [("ps", bb_[hh // 4]), sct], ["REb"], bias=SC[:, 2, c, 8 + hh:9 + hh], scale=1.0)
            ps_release(*bf_)
            ps_release(*bb_)
            bg_ = ps_alloc(hold=True)
            for g in range(2):
                MM(bg_, psb[bg_][:, g * 128:(g + 1) * 128], BC[:, g, cs], BC[:, 2 + g, cs], True, True,
                   [("BC", g), ("BC", 2 + g)])
            TTO("dve", REf, REf, REb, ALU.add, ["REf", "REb"], ["REf"])
            for g in range(2):
                TTO("dve", Mc[:, 4 * g:4 * g + 4, :], REf[:, 4 * g:4 * g + 4, :],
                    psb[bg_][:, g * 128:(g + 1) * 128].unsqueeze(1).broadcast_to([128, 4, 128]), ALU.mult,
                    ["REf", ("ps", bg_)], [("Mb", c % 2)])
            ps_release(bg_)

        def stageQ(c):
            cs = slice(c * 128, (c + 1) * 128)
            Mc = Mbs[c % 2]
            if c == 8:
                MEMSET("dve", Hfc[:, :], 0.0, [], ["Hfc"])
            tok_major(c)
            ACTF(Hfb[:, :], Hfc[:, :], AF.Identity, ["Hfc"], ["Hfb"])
            bwf = ps_alloc(hold=True)
            bwb = ps_alloc(hold=True)
            for g in range(2):
                MM(bwf, psb[bwf][:, g * 256:(g + 1) * 256], BC[:, 2 + g, cs], Hfb[:, g * 256:(g + 1) * 256], True, True,
                   [("BC", 2 + g), "Hfb"])
                MM(bwb, psb[bwb][:, g * 256:(g + 1) * 256], BC[:, 2 + g, cs], Hb[:, c, g * 256:(g + 1) * 256], True, True,
                   [("BC", 2 + g), ("Hb", c)])
            TTO("dve", v3(tmp[0][:, 0:512]), v3(psb[bwf][:, 0:512]), bc8(SC[:, 4, c, 0:8]), ALU.mult,
                [("ps", bwf), sct], tmp0t)
            TTO("dve", v3(tmp[1][:, 0:512]), v3(psb[bwb][:, 0:512]), bc8(SC[:, 4, c, 8:16]), ALU.mult,
                [("ps", bwb), sct], tmp1t)
            ps_release(bwf, bwb)
            TTO("dve", Wtok[:, :], tmp[0][:, 0:512], tmp[1][:, 0:512], ALU.add, tmp0t + tmp1t, ["Wtok"])
            by = [ps_alloc(hold=True) for _ in range(2)]
            for m in range(4):
                for hh in range(2):
                    hd = 2 * m + hh
                    col = ((m % 2) * 2 + hh) * 128
                    bk = by[m // 2]
                    MM(bk, psb[bk][:, col:col + 128], Xt[:, m * 128:(m + 1) * 128], Mc[:, hd, :], True, False,
                       ["Xt", ("Mb", c % 2)])
                    MM(bk, psb[bk][:, col:col + 128], Wtok[:, m * 128:(m + 1) * 128], identb[:], False, True,
                       ["Wtok", "identb"])
            grp_s["bk"] = chunk_state(c, 0)
            return by

        def stageQ2(c, by):
            cs = slice(c * 128, (c + 1) * 128)
            yz = tmp[0][:, 0:512].rearrange("p (m i) -> p m i", m=4)
            for m in range(4):
                for hh in range(2):
                    r0 = hh * 64
                    col = ((m % 2) * 2 + hh) * 128
                    bk = by[m // 2]
                    STT(yz[r0:r0 + 64, m, :], XS[r0:r0 + 64, m, cs], ev(32 + m)[r0:r0 + 64, :],
                        psb[bk][r0:r0 + 64, col:col + 128], ALU.mult, ALU.add, [("XS", m), ("ps", bk), "dsk"], tmp0t)
            ps_release(*by)
            TTO("dve", yz, yz, Zb[:, :, cs], ALU.mult, tmp0t + [("Zb", m) for m in range(4)], tmp0t)
            sqv = sq[1][:, 0:512].rearrange("p (m i) -> p m i", m=4)
            ACTF(sq[1][:, 0:512], tmp[0][:, 0:512], AF.Square, tmp0t, [("sq", 1)])
            bn = ps_alloc(hold=True)
            for m in range(4):
                MM(bn, psb[bn][:, 0:128], ones_bf[:], sqv[:, m, :], m == 0, m == 3, [("sq", 1), "ones"])
            ACTF(tmp[1][:, 0:128], psb[bn][:, 0:128], AF.Ln, [("ps", bn), "eps"], tmp1t, bias=eps_t[:], scale=1.0 / 512)
            ps_release(bn)
            ACTF(tmp[1][:, 0:128], tmp[1][:, 0:128], AF.Exp, tmp1t, tmp1t, scale=-0.5)
            for m in range(4):
                STT(Zb[:, m, cs], yz[:, m, :], ev(40 + m), tmp[1][:, 0:128], ALU.mult, ALU.mult,
                    tmp0t + tmp1t + ["evv"], [("Zb", m)])
            bk = grp_s["bk"]
            TTO("dve", v3(Hfc[:, :]), v3(Hfc[:, :]), bc8(SC[:, 5, c, 0:8]), ALU.mult, ["Hfc", sct], ["Hfc"])
            TTO("dve", Hfc[:, :], Hfc[:, :], psb[bk][:, 0:512], ALU.add, ["Hfc", ("ps", bk)], ["Hfc"])
            ps_release(bk)
            if c % 2 == 1:
                out_evs.append(DMA("sp", ssdT_d[i, seq_of(c), 0], Hfc[:, :], ["Hfc"], [], "o_Hfc"))
                if c in (1, 3, 5):
                    TS("dve", Hfc[:, :], Hfc[:, :], flagA, None, ALU.mult, None, ["Hfc", "flags"], ["Hfc"])

        DMA("sp", Hfc[:, :], h0T_d[i][:, 0, :], [], ["Hfc"], "mH")
        PIPE = cfg.get('ev_pipe', False)
        if cfg.get('ev_p3', True) and PIPE:
            stageP(0)
        for c in (range(10) if cfg.get('ev_p3', True) else []):
            if PIPE:
                by_ = stageQ(c)
                if c + 1 < 10:
                    stageP(c + 1)
                stageQ2(c, by_)
            else:
                stageP(c)
                by_ = stageQ(c)
                stageQ2(c, by_)

        WO = WStream(weout_d[i], LAY_OUT)
        mreads = [("Ub", m) for m in range(4)] + [("Zb", m) for m in range(4)]
        for j in range(8):
            banks = proj(WO, j, 8, lambda kc, t0, tn: (Ub[:, kc, t0:t0 + tn] if kc < 4 else Zb[:, kc - 4, t0:t0 + tn]),
                         mreads, 128, TT)
            for ti, (t0, tn) in enumerate(TT):
                s_ = SLOT_OF_TT[ti]
                STT(x[:, j, t0:t0 + tn], psb[banks[ti]][:, 0:tn], G_ap(l, 1, j, s_), x[:, j, t0:t0 + tn],
                    ALU.mult, ALU.add, [("ps", banks[ti]), ("gate", l % 2, 1), ("x", j)], [("x", j)])
            ps_release(*banks)
            if j == 4:
                rms_begin()
                for jj in range(4):
                    rms_chunk(jj)
            elif j > 4:
                rms_chunk(j - 1)
        rms_chunk(7)
        fence()

    bg_start(mod_layer_gen(0), 1)
    for l in range(nlayers):
        if l == 0:
            bg_ensure(0, 0)
            bg_limit[0] = 3
        else:
            bg_limit[0] = 3
        ffn(l, 0)
        if l % 2 == 1 and cfg.get("odd", True):
            odd_mixer(l)
        if l % 2 == 0 and cfg.get("even", True):
            even_mixer(l)
        bg_flush()
        if l + 1 < nlayers:
            bg_start(mod_layer_gen(l + 1), 2)
        ffn(l, 1)

    rms_stats()
    for kc in range(8):
        tb = tmp[kc % 2]
        S.op("dve", lambda e, kc=kc, tb=tb: e.scalar_tensor_tensor(
            out=tb[:, 0:NT], in0=x[:, kc, :], scalar=gfin[:, kc:kc + 1], in1=rstd[:], op0=ALU.mult, op1=ALU.mult),
            reads=[("x", kc), ("rstd", 0), ("rstd", 1), ("rstd", 2), "gfin"], writes=[("tmp", kc % 2, 0), ("tmp", kc % 2, 1)])
        ev = S.op("sp", lambda e, kc=kc, tb=tb: e.dma_start(out=yT_d[kc], in_=tb[:, 0:NT]),
                  reads=[("tmp", kc % 2, 0), ("tmp", kc % 2, 1)], dma="o_tmp%d" % (kc % 2))
        out_evs.append(ev)
    S.wait_all("sp", out_evs)

    S.emit(nc, stack)
    stack.close()
    return nc


def _core_tokens(core, x_prompt, x_sample):
    if core < 2:
        a = x_sample[core]
        b = x_prompt[core]
    else:
        p0 = 2 + (core - 2) * 5
        a = x_prompt[p0:p0 + 4].reshape(1024, D)
        b = x_prompt[p0 + 4]
    return np.concatenate([a, b], axis=0)


def _prep_shared(inp):
    f = np.float32
    sh = {}
    w_mod = np.asarray(inp["w_mod"], f)
    sh["wmod"] = np.ascontiguousarray(
        w_mod.reshape(DEPTH, 8, 128, 36, 2, 128).transpose(0, 3, 2, 4, 1, 5)).reshape(DEPTH, 36, 128, 2048)
    b_mod = np.asarray(inp["b_mod"], f)
    sh["bmod"] = np.ascontiguousarray(b_mod.reshape(DEPTH, 72, 128).transpose(2, 0, 1)).reshape(128, DEPTH * 72)
    g_norm = np.asarray(inp["g_norm"], f)
    sh["gnorm"] = np.ascontiguousarray(g_norm.reshape(DEPTH, 3, 8, 128).transpose(3, 0, 1, 2)).reshape(128, DEPTH * 24)
    sh["gfin"] = np.ascontiguousarray(np.asarray(inp["g_final"], f).reshape(8, 128).T)
    w_gu = np.asarray(inp["w_ff_gu"], f)
    sh["wgu"] = np.ascontiguousarray(
        w_gu.reshape(DEPTH, 2, 8, 128, 2, 11, 2, 128).transpose(0, 1, 5, 3, 4, 6, 2, 7)).reshape(DEPTH, 2, 11, 128, 4096)
    w_dn = np.asarray(inp["w_ff_down"], f)
    sh["wdn"] = np.ascontiguousarray(
        w_dn.reshape(DEPTH, 2, NFC, 128, 8, 128).transpose(0, 1, 4, 3, 2, 5)).reshape(DEPTH, 2, 8, 128, 2816)
    w_in_odd = np.asarray(inp["w_in_odd"], f)
    w_uq = np.asarray(inp["w_uq"], f)
    w_ukv = np.asarray(inp["w_ukv"], f)
    w_out_odd = np.asarray(inp["w_out_odd"], f)
    sh["woin"] = np.stack([_pack(w_in_odd[i], _odd_in_blocks()) for i in range(2)])
    sh["wuq"] = np.stack([_pack(w_uq[i], _uq_blocks()) for i in range(2)])
    sh["wukv"] = np.stack([_pack(w_ukv[i], _ukv_blocks()) for i in range(2)])
    sh["woout"] = np.stack([_pack(w_out_odd[i], _out_blocks()) for i in range(2)])
    ov = np.zeros((128, 2, NOV), f)
    for i in range(2):
        ov[:, i, 0:3] = np.asarray(inp["g_cq"], f)[i].reshape(3, 128).T
        ov[:, i, 3:5] = np.asarray(inp["g_ckv"], f)[i].reshape(2, 128).T
        wdw = np.asarray(inp["w_dwconv"], f)[i]
        ov[:, i, 5:129] = wdw.reshape(31, 4, 128).transpose(2, 1, 0).reshape(128, 124)
        ov[:, i, 129:133] = np.asarray(inp["b_dwconv"], f)[i].reshape(4, 128).T
        ov[:, i, 133:137] = np.asarray(inp["g_conv_ln"], f)[i].reshape(4, 128).T
        ov[:, i, 137:141] = np.asarray(inp["b_conv_ln"], f)[i].reshape(4, 128).T
    sh["oddv"] = np.ascontiguousarray(ov.reshape(128, 2 * NOV))
    w_in_even = np.asarray(inp["w_in_even"], f)
    w_out_even = np.asarray(inp["w_out_even"], f)
    sh["wein"] = np.stack([_pack(w_in_even[i], _even_in_blocks()) for i in range(2)])
    sh["weout"] = np.stack([_pack(w_out_even[i], _out_blocks()) for i in range(2)])
    evv = np.zeros((128, 2, NEV), f)
    evb = np.zeros((2, 128, NEB), f)
    for i in range(2):
        wc = np.asarray(inp["w_conv_ssm"], f)[i]
        evv[:, i, 0:24] = wc.reshape(3, 8, 128).transpose(2, 1, 0).reshape(128, 24)
        evv[:, i, 24:32] = np.asarray(inp["b_conv_ssm"], f)[i].reshape(8, 128).T
        dsk = np.asarray(inp["d_skip"], f)[i]
        for m in range(4):
            evv[0:64, i, 32 + m] = dsk[0, 2 * m]
            evv[64:128, i, 32 + m] = dsk[0, 2 * m + 1]
            evv[0:64, i, 36 + m] = dsk[1, 2 * m]
            evv[64:128, i, 36 + m] = dsk[1, 2 * m + 1]
        evv[:, i, 40:44] = np.asarray(inp["g_ssm_out"], f)[i].reshape(4, 128).T
        evb[i, :, 0:512] = np.asarray(inp["g_gmlp_v"], f)[i][None, :]
        evb[i, :, 512:1024] = np.asarray(inp["b_spatial"], f)[i].reshape(1, 512)
        evb[i, :, 1024:1040] = np.asarray(inp["dt_bias"], f)[i].reshape(1, 16)
        evb[i, :, 1040:1056] = np.asarray(inp["a_log"], f)[i].reshape(1, 16)
    sh["evv"] = np.ascontiguousarray(evv.reshape(128, 2 * NEV))
    sh["evb"] = evb
    ws = np.asarray(inp["w_spatial"], f)
    sh["wsT"] = np.ascontiguousarray(ws.transpose(0, 3, 1, 2)).reshape(2, 128, 512)
    jj = np.arange(128)[:, None]
    ii = np.arange(128)[None, :]
    U = (jj <= ii).astype(f)
    L = (jj >= ii).astype(f)
    cst = np.stack([U, L, NEG * (1 - U), NEG * (1 - L), np.eye(128, dtype=f), np.ones((128, 128), f)], axis=1)
    sh["cst"] = np.ascontiguousarray(cst.reshape(128, 6 * 128))
    return sh


def _rope_tables():
    f = np.float32
    t = np.arange(1024)
    row = (t // 64).astype(f)
    col = (t % 64).astype(f)
    freqs = (np.float32(10000.0) ** (-np.arange(8, dtype=f) / np.float32(8))).astype(f)
    ang = np.stack([row[:, None] * freqs, col[:, None] * freqs], axis=1)
    cos = np.cos(ang).astype(f)
    sin = np.sin(ang).astype(f)
    C = np.zeros((32, 1024), f)
    Sg = np.zeros((32, 1024), f)
    for a in range(2):
        for r in range(2):
            for q in range(8):
                d = a * 16 + r * 8 + q
                C[d] = cos[:, a, q]
                Sg[d] = (-sin[:, a, q]) if r == 0 else sin[:, a, q]
    return np.concatenate([C, C], 0), np.concatenate([Sg, Sg], 0)


def _prep_core(core, inp):
    f = np.float32
    m = {}
    toks = _core_tokens(core, np.asarray(inp["x_prompt"], f), np.asarray(inp["x_sample"], f))
    m["xT"] = np.ascontiguousarray(toks.T).reshape(8, 128, NT)
    c_ctx = np.asarray(inp["c_ctx"], f)
    cA = np.asarray(inp["c"], f)[core] if core < 2 else c_ctx
    cv = np.stack([cA, c_ctx], axis=-1)
    m["cvec"] = np.ascontiguousarray(cv.reshape(8, 128, 2).transpose(1, 0, 2)).reshape(128, 16)
    is_s = core < 2
    fl = np.zeros((128, 4), f)
    fl[:, 0] = 1.0 if is_s else 0.0
    fl[:, 1] = 0.0 if is_s else NEG
    m["flags"] = fl
    if is_s:
        C, Sg = _rope_tables()
        m["rope"] = np.ascontiguousarray(np.stack([C, Sg]))
        ck = np.asarray(inp["cache_mla_ckv"], f)[core]
        m["cckv"] = np.ascontiguousarray(ck.reshape(2, 256, 2, 128).transpose(0, 3, 2, 1))
        kr = np.asarray(inp["cache_mla_krope"], f)[core]
        krT = kr.transpose(0, 2, 1)
        m["ckr"] = np.ascontiguousarray(np.concatenate([krT, krT], axis=1))
        st = np.asarray(inp["state_ssd"], f)[core]
        m["h0T"] = np.ascontiguousarray(st.transpose(0, 4, 1, 2, 3)).reshape(2, 128, 2, 512)
    else:
        m["h0T"] = np.zeros((2, 128, 2, 512), f)
        m["rope"] = np.ascontiguousarray(np.stack([np.ones((64, 1024), f), np.zeros((64, 1024), f)]))
        m["cckv"] = np.zeros((2, 128, 2, 256), f)
        m["ckr"] = np.zeros((2, 64, 256), f)
    return m


_NC_CACHE = {}


def kernel(**inputs):
    if "nc" not in _NC_CACHE:
        _NC_CACHE["nc"] = build_program()
    nc = _NC_CACHE["nc"]
    shared = _prep_shared(inputs)
    in_maps = []
    for core in range(NCORES):
        m = dict(shared)
        m.update(_prep_core(core, inputs))
        in_maps.append(m)
    res = run_bass_kernel_spmd(nc, in_maps, core_ids=list(range(NCORES)))
    return _gather(res.results, inputs)


def _gather(results, inputs):
    f = np.float32
    y_prompt = np.zeros((32, 256, D), f)
    y_sample = np.zeros((2, 1024, D), f)
    for core in range(NCORES):
        y = results[core]["yT"].reshape(D, NT).T
        if core < 2:
            y_sample[core] = y[:1024]
            y_prompt[core] = y[1024:]
        else:
            p0 = 2 + (core - 2) * 5
            y_prompt[p0:p0 + 4] = y[:1024].reshape(4, 256, D)
            y_prompt[p0 + 4] = y[1024:]
    new_ssd = np.zeros((32, 2, 2, 8, 64, 128), f)
    new_ckv = np.zeros((32, 2, 256, 256), f)
    new_kr = np.zeros((32, 2, 256, 32), f)
    for core in range(NCORES):
        ck = results[core]["ckvT"].reshape(2, 256, NT).transpose(0, 2, 1)
        kr = results[core]["krT"].reshape(2, 32, NT).transpose(0, 2, 1)
        if core < 2:
            seqs = [(core, 1024)]
        else:
            p0 = 2 + (core - 2) * 5
            seqs = [(p0 + j, j * 256) for j in range(4)] + [(p0 + 4, 1024)]
        for (b, t0) in seqs:
            new_ckv[b] = ck[:, t0:t0 + 256]
            new_kr[b] = kr[:, t0:t0 + 256]
        sd = results[core]["ssdT"].reshape(2, 5, 2, 128, 8, 64)
        for (b, t0) in seqs:
            new_ssd[b] = sd[:, t0 // 256].transpose(0, 1, 3, 4, 2)
    return (y_prompt, y_sample, new_ssd, new_ckv, new_kr)
```
